# Optimizing a Trainium2 kernel written in Bass

```python
import math
import jax
import jax.numpy as jnp
from jax import lax
import numpy as np

D_MODEL = 1024
BATCH = 4
SEQ = 4096
DEPTH = 2
DEC_BATCH = 32
DEC_SEQ = 4
PAST_LEN = 16384
PAGE_SIZE = 128

HEAD_DIM = 128
N_DIL_HEADS = 4
DIL_GROUPS = ((128, 1), (512, 4), (2048, 16))
N_GROUPS = len(DIL_GROUPS)
DIL_SPAN = 128
ATT_WIDTH = N_DIL_HEADS * HEAD_DIM
N_CROSS_HEADS = 4
CROSS_WIDTH = N_CROSS_HEADS * HEAD_DIM
N_MEM = 256
ROT_DIM = HEAD_DIM // 4
ROPE_THETA = 500000.0
S5_WIDTH = ATT_WIDTH
S5_GROUP = 16
S5_GROUPS = S5_WIDTH // S5_GROUP
S5_STATE = 64
D_FF = 2816
CONV_W = 3
BLOCK = 128
EPS = 1e-6
NEG = -1e30
N_A_LAYERS = (DEPTH + 1) // 2
N_B_LAYERS = DEPTH // 2
QKV_WIDTH = 3 * N_GROUPS * ATT_WIDTH
IN_A = QKV_WIDTH + CROSS_WIDTH
IN_B = S5_WIDTH + CROSS_WIDTH
MIX_OUT = ATT_WIDTH + CROSS_WIDTH
SCALE = HEAD_DIM ** -0.5

kernel_name = 'dilated_s5_hybrid_decode_step'


def rmsnorm(x, g):
    xf = x.astype(jnp.float32)
    y = xf * lax.rsqrt(jnp.mean(xf * xf, axis=-1, keepdims=True) + EPS)
    return (y * g.astype(jnp.float32)).astype(x.dtype)


def rope_partial(x, pos):
    half = ROT_DIM // 2
    inv = jnp.exp(-math.log(ROPE_THETA) * jnp.arange(half, dtype=jnp.float32) / half)
    ang = pos.astype(jnp.float32)[:, None] * inv[None, :]
    cos = jnp.cos(ang)[:, None, :]
    sin = jnp.sin(ang)[:, None, :]
    xf = x.astype(jnp.float32)
    x1, x2, rest = xf[..., :half], xf[..., half:ROT_DIM], xf[..., ROT_DIM:]
    return jnp.concatenate([x1 * cos - x2 * sin, x2 * cos + x1 * sin, rest], axis=-1).astype(x.dtype)


def banded_attention(q, k, v):
    n, l, h, d = q.shape
    nb = -(-l // BLOCK)
    pad = nb * BLOCK - l
    qb = jnp.pad(q.astype(jnp.float32), ((0, 0), (0, pad), (0, 0), (0, 0))).reshape(n, nb, BLOCK, h, d)

    def band(t):
        tp = jnp.pad(t.astype(jnp.float32), ((0, 0), (BLOCK, pad), (0, 0), (0, 0))).reshape(n, nb + 1, BLOCK, h, d)
        return jnp.concatenate([tp[:, :-1], tp[:, 1:]], axis=2)

    kb, vb = band(k), band(v)
    s = jnp.einsum('nbqhd,nbkhd->nbhqk', qb, kb) * SCALE
    qi = jnp.arange(BLOCK)[:, None]
    kj = jnp.arange(2 * BLOCK)[None, :]
    dist = BLOCK + qi - kj
    in_band = (dist >= 0) & (dist <= DIL_SPAN)
    past_ok = (jnp.arange(nb) > 0)[:, None, None] | (kj >= BLOCK)[None]
    mask = in_band[None] & past_ok
    s = jnp.where(mask[None, :, None], s, NEG)
    lse = jax.nn.logsumexp(s, axis=-1)
    p = jnp.exp(s - lse[..., None])
    o = jnp.einsum('nbhqk,nbkhd->nbqhd', p, vb).reshape(n, nb * BLOCK, h, d)[:, :l]
    lse = lse.transpose(0, 1, 3, 2).reshape(n, nb * BLOCK, h)[:, :l]
    return o, lse


def dilated_prompt(q, k, v, r):
    b, s, h, d = q.shape
    l = s // r

    def to_res(t):
        return t.reshape(b, l, r, h, d).transpose(0, 2, 1, 3, 4).reshape(b * r, l, h, d)

    o, lse = banded_attention(to_res(q), to_res(k), to_res(v))
    o = o.reshape(b, r, l, h, d).transpose(0, 2, 1, 3, 4).reshape(b, s, h, d)
    lse = lse.reshape(b, r, l, h).transpose(0, 2, 1, 3).reshape(b, s, h)
    return o, lse


def dilated_sample(q, k, v, kv_buf, r):
    t = q.shape[1]
    lb = kv_buf.shape[1]
    k_all = jnp.concatenate([kv_buf[:, :, 0].astype(jnp.float32), k.astype(jnp.float32)], axis=1)
    v_all = jnp.concatenate([kv_buf[:, :, 1].astype(jnp.float32), v.astype(jnp.float32)], axis=1)
    idx = lb + jnp.arange(t)[:, None] - r * jnp.arange(DIL_SPAN + 1)[None, :]
    valid = idx >= 0
    idx = jnp.maximum(idx, 0)
    kg = k_all[:, idx]
    vg = v_all[:, idx]
    s = jnp.einsum('bthd,btjhd->bthj', q.astype(jnp.float32), kg) * SCALE
    s = jnp.where(valid[None, :, None, :], s, NEG)
    lse = jax.nn.logsumexp(s, axis=-1)
    p = jnp.exp(s - lse[..., None])
    o = jnp.einsum('bthj,btjhd->bthd', p, vg)
    new_buf = jnp.concatenate([kv_buf, jnp.stack([k, v], axis=2).astype(kv_buf.dtype)], axis=1)[:, t:]
    return o, lse, new_buf


def combine_groups(outs, lses):
    wts = jax.nn.softmax(jnp.stack(lses, axis=0), axis=0)
    return jnp.einsum('gnsh,gnshd->nshd', wts, jnp.stack(outs, axis=0))


def s5_mixer(u, s0, lam_re, lam_im, log_dt, b_re, b_im, c_re, c_im, d_skip, w_glu, b_glu):
    n, s, _ = u.shape
    ug = u.astype(jnp.float32).reshape(n, s, S5_GROUPS, S5_GROUP)
    a_re = lam_re.astype(jnp.float32)
    a_im = lam_im.astype(jnp.float32)
    dt = jnp.exp(log_dt.astype(jnp.float32))[:, None]
    mag = jnp.exp(a_re * dt)
    lb_re = mag * jnp.cos(a_im * dt)
    lb_im = mag * jnp.sin(a_im * dt)
    den = a_re * a_re + a_im * a_im
    xr, yi = lb_re - 1.0, lb_im
    f_re = (xr * a_re + yi * a_im) / den
    f_im = (yi * a_re - xr * a_im) / den
    br = b_re.astype(jnp.float32)
    bi = b_im.astype(jnp.float32)
    bb_re = f_re[..., None] * br - f_im[..., None] * bi
    bb_im = f_re[..., None] * bi + f_im[..., None] * br
    bu_re = jnp.einsum('nsgc,gpc->nsgp', ug, bb_re)
    bu_im = jnp.einsum('nsgc,gpc->nsgp', ug, bb_im)
    ar0 = jnp.broadcast_to(lb_re, bu_re.shape)
    ai0 = jnp.broadcast_to(lb_im, bu_re.shape)

    def combine(e1, e2):
        a1r, a1i, b1r, b1i = e1
        a2r, a2i, b2r, b2i = e2
        return (a2r * a1r - a2i * a1i, a2r * a1i + a2i * a1r,
                a2r * b1r - a2i * b1i + b2r, a2r * b1i + a2i * b1r + b2i)

    cr, ci, sr, si = lax.associative_scan(combine, (ar0, ai0, bu_re, bu_im), axis=1)
    if s0 is not None:
        s0r = s0[:, 0].astype(jnp.float32)[:, None]
        s0i = s0[:, 1].astype(jnp.float32)[:, None]
        sr, si = sr + cr * s0r - ci * s0i, si + cr * s0i + ci * s0r
    y = (jnp.einsum('nsgp,gcp->nsgc', sr, c_re.astype(jnp.float32))
         - jnp.einsum('nsgp,gcp->nsgc', si, c_im.astype(jnp.float32))
         + d_skip.astype(jnp.float32) * ug).reshape(n, s, S5_WIDTH)
    y = jax.nn.gelu(y)
    out = y * jax.nn.sigmoid(y @ w_glu.astype(jnp.float32) + b_glu.astype(jnp.float32))
    state = jnp.stack([sr[:, -1], si[:, -1]], axis=1)
    return out.astype(u.dtype), state.astype(u.dtype)


def memory_kv(mem, g_mem, w_kv, g_k):
    n, m, _ = mem.shape
    kv = (rmsnorm(mem, g_mem) @ w_kv).reshape(n, m, 2, N_CROSS_HEADS, HEAD_DIM)
    return jnp.stack([rmsnorm(kv[:, :, 0], g_k), kv[:, :, 1]], axis=2)


def cross_attn(qc, kv, g_q):
    n, s, _ = qc.shape
    q = rmsnorm(qc.reshape(n, s, N_CROSS_HEADS, HEAD_DIM), g_q).astype(jnp.float32)
    sc = jnp.einsum('nshd,nmhd->nhsm', q, kv[:, :, 0].astype(jnp.float32)) * SCALE
    p = jax.nn.softmax(sc, axis=-1)
    o = jnp.einsum('nhsm,nmhd->nshd', p, kv[:, :, 1].astype(jnp.float32))
    return o.reshape(n, s, CROSS_WIDTH)


def conv_ffn(h, buf, w_up, conv_w, conv_b, w_down):
    s = h.shape[1]
    up = h @ w_up
    ext = jnp.concatenate([buf.astype(up.dtype), up], axis=1)
    c = conv_b
    for j in range(CONV_W):
        c = c + conv_w[j] * ext[:, j:j + s]
    a, b = jnp.split(c, 2, axis=-1)
    return (jax.nn.silu(a) * b) @ w_down, ext[:, s:]


def trunk(x, pos, p, mem=None, win_in=None, mem_kv_in=None, s5_in=None, conv_in=None):
    is_prompt = win_in is None
    n, s, _ = x.shape
    new_win = [[] for _ in range(N_GROUPS)]
    new_mem, new_s5, new_conv = [], [], []
    for i in range(DEPTH):
        h = rmsnorm(x, p['g_mix'][i])
        if is_prompt:
            kv_m = memory_kv(mem, p['g_mem'][i], p['w_mem_kv'][i], p['g_k_cross'][i])
            new_mem.append(kv_m)
        else:
            kv_m = mem_kv_in[i]
        if i % 2 == 0:
            ia = i // 2
            proj = h @ p['w_in_a'][ia]
            qkv = proj[..., :QKV_WIDTH].reshape(n, s, 3, N_GROUPS, N_DIL_HEADS, HEAD_DIM)
            qc = proj[..., QKV_WIDTH:]
            outs, lses = [], []
            for g, (w, r) in enumerate(DIL_GROUPS):
                q = rope_partial(rmsnorm(qkv[:, :, 0, g], p['g_q_dil'][ia, g]), pos)
                k = rope_partial(rmsnorm(qkv[:, :, 1, g], p['g_k_dil'][ia, g]), pos)
                v = qkv[:, :, 2, g]
                if is_prompt:
                    o, l = dilated_prompt(q, k, v, r)
                    keep = min(w, s)
                    new_win[g].append(jnp.stack([k, v], axis=2)[:, s - keep:])
                else:
                    o, l, nbuf = dilated_sample(q, k, v, win_in[g][ia], r)
                    new_win[g].append(nbuf)
                outs.append(o)
                lses.append(l)
            mix = combine_groups(outs, lses).reshape(n, s, ATT_WIDTH)
        else:
            ib = i // 2
            proj = h @ p['w_in_b'][ib]
            u = proj[..., :S5_WIDTH]
            qc = proj[..., S5_WIDTH:]
            s0 = None if is_prompt else s5_in[ib]
            mix, s_last = s5_mixer(u, s0, p['s5_lam_re'][ib], p['s5_lam_im'][ib], p['s5_log_dt'][ib],
                                   p['s5_b_re'][ib], p['s5_b_im'][ib], p['s5_c_re'][ib], p['s5_c_im'][ib],
                                   p['s5_d'][ib], p['w_glu'][ib], p['b_glu'][ib])
            new_s5.append(s_last)
        cross = cross_attn(qc, kv_m, p['g_q_cross'][i])
        merged = jnp.concatenate([mix.astype(x.dtype), cross.astype(x.dtype)], axis=-1)
        x = x + merged @ p['w_out'][i]
        h = rmsnorm(x, p['g_ffn'][i])
        buf = jnp.zeros((n, CONV_W - 1, 2 * D_FF), x.dtype) if is_prompt else conv_in[i]
        f, nconv = conv_ffn(h, buf, p['w_up'][i], p['conv_w'][i], p['conv_b'][i], p['w_down'][i])
        new_conv.append(nconv)
        x = x + f
    wins = [jnp.stack(nw, axis=0) for nw in new_win]
    mem_out = jnp.stack(new_mem, axis=0) if is_prompt else None
    return x, wins, mem_out, jnp.stack(new_s5, axis=0), jnp.stack(new_conv, axis=0)


def setup_inputs(seed: int = 0) -> dict:
    key = jax.random.key(seed)
    ks = iter(jax.random.split(key, 48))

    def nrm(shape, scale):
        return scale * jax.random.normal(next(ks), shape, jnp.float32)

    def gain(shape):
        return 1.0 + 0.05 * jax.random.normal(next(ks), shape, jnp.float32)

    lw = [min(w, PAST_LEN) for w, _ in DIL_GROUPS]
    kv_tail = (2, N_DIL_HEADS, HEAD_DIM)
    n_idx = jnp.arange(S5_STATE, dtype=jnp.float32)
    return {
        'x_prompt': nrm((BATCH, SEQ, D_MODEL), 1.0),
        'x_sample': nrm((DEC_BATCH, DEC_SEQ, D_MODEL), 1.0),
        'cache_win0_kv': nrm((N_A_LAYERS, DEC_BATCH, lw[0]) + kv_tail, 1.0),
        'cache_win1_kv': nrm((N_A_LAYERS, DEC_BATCH, lw[1]) + kv_tail, 1.0),
        'cache_win2_kv': nrm((N_A_LAYERS, DEC_BATCH, lw[2]) + kv_tail, 1.0),
        'cache_mem_kv': nrm((DEPTH, DEC_BATCH, N_MEM, 2, N_CROSS_HEADS, HEAD_DIM), 1.0),
        'state_s5': nrm((N_B_LAYERS, DEC_BATCH, 2, S5_GROUPS, S5_STATE), 0.1),
        'state_ffn_conv': nrm((DEPTH, DEC_BATCH, CONV_W - 1, 2 * D_FF), 1.0),
        'mem_prompt': nrm((BATCH, N_MEM, D_MODEL), 1.0),
        'g_mix': gain((DEPTH, D_MODEL)),
        'g_ffn': gain((DEPTH, D_MODEL)),
        'w_in_a': nrm((N_A_LAYERS, D_MODEL, IN_A), D_MODEL ** -0.5),
        'g_q_dil': gain((N_A_LAYERS, N_GROUPS, HEAD_DIM)),
        'g_k_dil': gain((N_A_LAYERS, N_GROUPS, HEAD_DIM)),
        'w_in_b': nrm((N_B_LAYERS, D_MODEL, IN_B), D_MODEL ** -0.5),
        's5_lam_re': -0.5 * jnp.exp(0.05 * jax.random.normal(next(ks), (N_B_LAYERS, S5_GROUPS, S5_STATE), jnp.float32)),
        's5_lam_im': math.pi * n_idx + 0.01 * jax.random.normal(next(ks), (N_B_LAYERS, S5_GROUPS, S5_STATE), jnp.float32),
        's5_log_dt': jax.random.uniform(next(ks), (N_B_LAYERS, S5_GROUPS), jnp.float32, math.log(0.001), math.log(0.1)),
        's5_b_re': nrm((N_B_LAYERS, S5_GROUPS, S5_STATE, S5_GROUP), (2 * S5_GROUP) ** -0.5),
        's5_b_im': nrm((N_B_LAYERS, S5_GROUPS, S5_STATE, S5_GROUP), (2 * S5_GROUP) ** -0.5),
        's5_c_re': nrm((N_B_LAYERS, S5_GROUPS, S5_GROUP, S5_STATE), (2 * S5_STATE) ** -0.5),
        's5_c_im': nrm((N_B_LAYERS, S5_GROUPS, S5_GROUP, S5_STATE), (2 * S5_STATE) ** -0.5),
        's5_d': nrm((N_B_LAYERS, S5_GROUPS, S5_GROUP), 1.0),
        'w_glu': nrm((N_B_LAYERS, S5_WIDTH, S5_WIDTH), S5_WIDTH ** -0.5),
        'b_glu': nrm((N_B_LAYERS, S5_WIDTH), 0.02),
        'g_mem': gain((DEPTH, D_MODEL)),
        'w_mem_kv': nrm((DEPTH, D_MODEL, 2 * CROSS_WIDTH), D_MODEL ** -0.5),
        'g_q_cross': gain((DEPTH, HEAD_DIM)),
        'g_k_cross': gain((DEPTH, HEAD_DIM)),
        'w_out': nrm((DEPTH, MIX_OUT, D_MODEL), MIX_OUT ** -0.5),
        'w_up': nrm((DEPTH, D_MODEL, 2 * D_FF), D_MODEL ** -0.5),
        'conv_w': nrm((DEPTH, CONV_W, 2 * D_FF), CONV_W ** -0.5),
        'conv_b': nrm((DEPTH, 2 * D_FF), 0.02),
        'w_down': nrm((DEPTH, D_FF, D_MODEL), D_FF ** -0.5),
    }


def reference(x_prompt, x_sample, cache_win0_kv, cache_win1_kv, cache_win2_kv, cache_mem_kv,
              state_s5, state_ffn_conv, mem_prompt, g_mix, g_ffn, w_in_a, g_q_dil, g_k_dil, w_in_b,
              s5_lam_re, s5_lam_im, s5_log_dt, s5_b_re, s5_b_im, s5_c_re, s5_c_im, s5_d, w_glu, b_glu,
              g_mem, w_mem_kv, g_q_cross, g_k_cross, w_out, w_up, conv_w, conv_b, w_down):
    params = dict(g_mix=g_mix, g_ffn=g_ffn, w_in_a=w_in_a, g_q_dil=g_q_dil, g_k_dil=g_k_dil,
                  w_in_b=w_in_b, s5_lam_re=s5_lam_re, s5_lam_im=s5_lam_im, s5_log_dt=s5_log_dt,
                  s5_b_re=s5_b_re, s5_b_im=s5_b_im, s5_c_re=s5_c_re, s5_c_im=s5_c_im, s5_d=s5_d,
                  w_glu=w_glu, b_glu=b_glu, g_mem=g_mem, w_mem_kv=w_mem_kv, g_q_cross=g_q_cross,
                  g_k_cross=g_k_cross, w_out=w_out, w_up=w_up, conv_w=conv_w, conv_b=conv_b,
                  w_down=w_down)
    pos_p = jnp.arange(x_prompt.shape[1], dtype=jnp.int32)
    pos_s = PAST_LEN + jnp.arange(x_sample.shape[1], dtype=jnp.int32)
    y_prompt, win_p, mem_p, s5_p, conv_p = trunk(x_prompt, pos_p, params, mem=mem_prompt)
    y_sample, win_s, _, s5_s, conv_s = trunk(
        x_sample, pos_s, params,
        win_in=(cache_win0_kv, cache_win1_kv, cache_win2_kv),
        mem_kv_in=cache_mem_kv, s5_in=state_s5, conv_in=state_ffn_conv)
    return (y_prompt, y_sample, win_p[0], win_p[1], win_p[2], mem_p, s5_p, conv_p,
            win_s[0], win_s[1], win_s[2], s5_s, conv_s)
```

```python
import contextlib
import math
import os

import numpy as np
import concourse.bass as bass
import concourse.mybir as mybir
from concourse.bass_utils import run_bass_kernel_spmd

F32 = mybir.dt.float32
BF16 = mybir.dt.bfloat16
AF = mybir.ActivationFunctionType
ALU = mybir.AluOpType
AX = mybir.AxisListType

D = 1024
T = 2048
NT = 16
TH = 4096
NTH = 32
DFF = 2816
NFF = 22
SCALE = 128 ** -0.5
EPS = 1e-6
GROUPS = ((128, 1), (512, 4), (2048, 16))
PAST = 16384
NEGB = -30000.0


class Sched:
    def __init__(self, nc, stack, n_dma_sems=32):
        self.nc = nc
        self.engs = {"pe": nc.tensor, "act": nc.scalar, "dve": nc.vector,
                     "pool": nc.gpsimd, "sp": nc.sync}
        self.sem, self.cnt = {}, {}
        for k in self.engs:
            self.sem[k] = stack.enter_context(nc.semaphore("s_" + k))
            self.cnt[k] = 0
        self.dsem = [stack.enter_context(nc.semaphore("d_%d" % i)) for i in range(n_dma_sems)]
        self.dcnt = [0] * n_dma_sems
        self.dnext = 0
        self.dnext_sw = 0
        self.ccsem = stack.enter_context(nc.semaphore("s_cc"))
        self.cccnt = 0
        self.waited = {k: {} for k in self.engs}
        self.last_w = {}
        self.readers = {}

    def _semobj(self, key):
        if key == "cc":
            return self.ccsem
        return self.sem[key] if isinstance(key, str) else self.dsem[key]

    def _wait(self, engname, key, val):
        w = self.waited[engname]
        if w.get(key, 0) >= val:
            return
        self.engs[engname].wait_ge(self._semobj(key), val)
        w[key] = val

    def _deps(self, reads, writes):
        deps = {}

        def add(d, raw):
            if d is None:
                return
            k, v = d
            o = deps.get(k)
            if o is None or o[0] < v:
                deps[k] = (v, raw or (o[1] if o else False))
            elif raw:
                deps[k] = (o[0], True)

        for b in reads:
            add(self.last_w.get(b), True)
        for b in writes:
            add(self.last_w.get(b), False)
            for k, v in self.readers.get(b, {}).items():
                add((k, v), False)
        return deps

    def _commit(self, reads, writes, key, val):
        for b in reads:
            self.readers.setdefault(b, {})[key] = val
        for b in writes:
            self.last_w[b] = (key, val)
            self.readers[b] = {}

    def op(self, engname, reads, writes, fn, chain=False):
        deps = self._deps(reads, writes)
        for k, (v, raw) in deps.items():
            if k == engname and (engname == "pe" or not raw):
                continue
            self._wait(engname, k, v)
        if chain:
            fn(_Chain(self, engname))
        else:
            last = fn(self.engs[engname])
            self.cnt[engname] += 1
            last.then_inc(self.sem[engname], 1)
        self._commit(reads, writes, engname, self.cnt[engname])

    def dma(self, issuer, reads, writes, out, in_, **kw):
        deps = self._deps(reads, writes)
        nh = len(self.dsem) - 8
        if issuer == "pool":
            i = nh + self.dnext_sw
            self.dnext_sw = (self.dnext_sw + 1) % 8
        else:
            i = self.dnext
            self.dnext = (self.dnext + 1) % nh
        if self.dcnt[i] > 0:
            o = deps.get(i)
            if o is None or o[0] < self.dcnt[i]:
                deps[i] = (self.dcnt[i], True)
        for k, (v, raw) in deps.items():
            self._wait(issuer, k, v)
        self.dcnt[i] += 16
        self.engs[issuer].dma_start(out=out, in_=in_, **kw).then_inc(self.dsem[i], 16)
        self._commit(reads, writes, i, self.dcnt[i])

    def allgather_pairs(self, reads, writes, src, dst):
        deps = self._deps(reads, writes)
        for k, (v, raw) in deps.items():
            self._wait("pool", k, v)
        self.nc.gpsimd.collective_compute(
            "AllGather", ALU.bypass, replica_groups=[[2 * i, 2 * i + 1] for i in range(int(os.environ.get("DBG_NCORES", "8")) // 2)],
            ins=[src], outs=[dst]).then_inc(self.ccsem)
        self.cccnt += 1
        self._commit(reads, writes, "cc", self.cccnt)

    def finish(self, engname="sp"):
        for i in range(len(self.dsem)):
            if self.dcnt[i] > 0:
                self._wait(engname, i, self.dcnt[i])
        for k in self.engs:
            if k != engname and self.cnt[k] > 0:
                self._wait(engname, k, self.cnt[k])


class _Chain:
    def __init__(self, sched, engname):
        self.s, self.n, self.first = sched, engname, True

    def __getattr__(self, name):
        real = getattr(self.s.engs[self.n], name)
        s, n = self.s, self.n

        def call(*a, **kw):
            if not self.first:
                s._wait(n, n, s.cnt[n])
            self.first = False
            ins = real(*a, **kw)
            s.cnt[n] += 1
            ins.then_inc(s.sem[n], 1)
            return ins
        return call


class Rot:
    def __init__(self, nc, stack, name, shape, dt, n):
        self.t = [stack.enter_context(nc.sbuf_tensor("%s_%d" % (name, i), list(shape), dt)) for i in range(n)]
        self.k = ["%s_%d" % (name, i) for i in range(n)]
        self.i = 0

    def next(self):
        j = self.i
        self.i = (self.i + 1) % len(self.t)
        return self.t[j], self.k[j]


def _ap(base, extra_off, dims):
    return bass.AP(base.tensor, base.offset + extra_off, [list(base.ap[0])] + [list(d) for d in dims])


def build(stage=99):
    nc = bass.Bass("TRN2", target_bir_lowering=False)

    def din(name, shape, dt=F32):
        return nc.dram_tensor(name, list(shape), dt, kind="ExternalInput").ap()

    def dout(name, shape, dt=F32):
        return nc.dram_tensor(name, list(shape), dt, kind="ExternalOutput").ap()

    def dscr(name, shape, dt):
        return nc.dram_tensor(name, list(shape), dt, kind="Internal").ap()

    xkv = din("xkv", [TH, D])
    flag_d = din("flag", [128, 1])
    ident_d = din("ident", [128, 128])
    maskb_d = din("maskb", [128, 2, 128])
    ropecs_d = din("ropecs", [TH + 128, 32])
    mem_d = din("mem", [256, D])
    g_mix = din("g_mix", [2, D])
    g_ffn = din("g_ffn", [2, D])
    g_mem = din("g_mem", [2, D])
    w_in_a = din("w_in_a", [D, 5120])
    w_in_b = din("w_in_b", [D, 1024])
    g_q_dil = din("g_q_dil", [3, 128])
    g_k_dil = din("g_k_dil", [3, 128])
    g_q_cross = din("g_q_cross", [2, 128])
    g_k_cross = din("g_k_cross", [2, 128])
    w_mem_kv = din("w_mem_kv", [2, D, 1024])
    w_out = din("w_out", [2, D, D])
    w_up = din("w_up", [2, D, 2 * DFF])
    conv_w = din("conv_w", [2, 3, 2 * DFF])
    conv_b = din("conv_b", [2, 2 * DFF])
    w_down = din("w_down", [2, DFF, D])
    lam_re_d = din("s5_lam_re", [16, 128])
    lam_im_d = din("s5_lam_im", [16, 128])
    log_dt_d = din("s5_log_dt", [16, 2])
    b_re_d = din("s5_b_re", [2048, 16])
    b_im_d = din("s5_b_im", [2048, 16])
    c_re_d = din("s5_c_re", [512, 64])
    c_im_d = din("s5_c_im", [512, 64])
    d_skip_d = din("s5_d", [4, 128])
    w_glu = din("w_glu", [512, 512])
    b_glu_d = din("b_glu", [4, 128])

    xs_d = din("xs", [16, D])
    cwin_d = [din("cwin%d" % g, [4, GROUPS[g][0], 2, 4, 128]) for g in range(3)]
    cmem_d = din("cmem", [2, 4, 256, 2, 4, 128])
    st5_d = din("st5", [128, 128])
    cst_d = din("cst", [2, 8, 2 * DFF])
    g0bias_d = din("g0bias", [128, 4])
    biasnew_d = din("biasnew", [16, 2, 16])

    ys_o = dout("y_s", [16, D])
    swin_o = [dout("swin%d" % g, [4, GROUPS[g][0], 2, 4, 128]) for g in range(3)]
    ss5_o = dout("ss5", [128, 128])
    sconv_o = dout("sconv", [2, 352, 128])
    y_o = dout("y_p", [T, D])
    win_o = [dout("win%d" % g, [GROUPS[g][0], 2, 4, 128]) for g in range(3)]
    memkv_o = dout("memkv", [2, 256, 2, 4, 128])
    s5_o = dout("s5o", [2, 16, 128])
    conv_o = dout("convo", [2, 88, 128])
    dbg = dout("dbg", [T, D]) if stage < 50 else None

    Qs = [dscr("Qs%d" % g, [T, 512], BF16) for g in range(3)]
    Ks = [dscr("Ks%d" % g, [TH, 512], BF16) for g in range(3)]
    Vs = [dscr("Vs%d" % g, [TH, 520], BF16) for g in range(3)]
    NUM = [dscr("NUM%d" % g, [T, 516], F32) for g in range(3)]
    QC = dscr("QC", [T, 512], BF16)
    MIX = dscr("MIX", [128, 4 * T], BF16)
    cc_w = [16, 16, 32]
    cc_src = [dscr("cc_src%d" % i, [128, cc_w[i]], F32) for i in range(3)]
    cc_dst = [dscr("cc_dst%d" % i, [256, cc_w[i]], F32) for i in range(3)]

    with contextlib.ExitStack() as st:
        S = Sched(nc, st)

        def sbt(stack, name, shape, dt=F32):
            return stack.enter_context(nc.sbuf_tensor(name, list(shape), dt))

        def sb(name, shape, dt=F32):
            return sbt(st, name, shape, dt)

        PS = st.enter_context(nc.psum_tensor("PS", [128, 8, 512], F32))

        def barrier():
            for e in S.engs:
                for i in range(len(S.dsem)):
                    if S.dcnt[i] > 0:
                        S._wait(e, i, S.dcnt[i])
                if S.cccnt > 0:
                    S._wait(e, "cc", S.cccnt)
                for k in S.engs:
                    if k != e and S.cnt[k] > 0:
                        S._wait(e, k, S.cnt[k])

        def end(dump=None):
            S.finish("sp")
            return nc

        identf = sb("identf", [128, 128])
        identb = sb("identb", [128, 128], BF16)
        maskf = sb("maskf", [128, 2, 128])
        maskb = sb("maskb_sb", [128, 2, 128], BF16)
        flag = sb("flag_sb", [128, 1])
        ones1 = sb("ones1", [128, 1])
        epsT = sb("epsT", [128, 1])
        ropecs = sb("ropecs_sb", [128, NTH + 1, 32])
        gq_bc = sb("gq_bc", [128, 3, 128])
        gk_bc = sb("gk_bc", [128, 3, 128])
        gqc_bc = sb("gqc_bc", [128, 2, 128])
        gkc_bc = sb("gkc_bc", [128, 2, 128])
        S.dma("sp", [], ["identf"], identf[:], ident_d[:, :])
        S.dma("sp", [], ["maskf"], maskf[:], maskb_d[:, :, :])
        S.dma("sp", [], ["flag"], flag[:], flag_d[:, :])
        S.dma("sp", [], ["ropecs"], ropecs[:], ropecs_d.rearrange("(n p) c -> p n c", p=128))

        def bc_load(dst, key, src2d):
            S.dma("sp", [], [key], dst[:], src2d.rearrange("g d -> (g d)").unsqueeze(0).partition_broadcast(128))
        bc_load(gq_bc, "gq_bc", g_q_dil)
        bc_load(gk_bc, "gk_bc", g_k_dil)
        bc_load(gqc_bc, "gqc_bc", g_q_cross)
        bc_load(gkc_bc, "gkc_bc", g_k_cross)
        S.op("dve", ["identf"], ["identb"], lambda e: e.tensor_copy(identb[:], identf[:]))
        S.op("dve", ["maskf"], ["maskb"], lambda e: e.tensor_copy(maskb[:], maskf[:]))
        S.op("dve", [], ["ones1"], lambda e: e.memset(ones1[:], 1.0))
        S.op("dve", [], ["epsT"], lambda e: e.memset(epsT[:], EPS))

        gbcR = Rot(nc, st, "gbc", [128, D], F32, 2)

        def load_gbc(src_row):
            t, k = gbcR.next()
            S.dma("sp", [], [k], t[:], src_row.partition_broadcast(128))
            return t, k

        junk = sb("junk", [128, D], BF16)
        ssR = Rot(nc, st, "ss", [128, 4], F32, 4)
        sdR = Rot(nc, st, "sd", [128, 4], F32, 4)
        rsR = Rot(nc, st, "rs", [128, 4], F32, 4)
        hbR = Rot(nc, st, "hb", [128, D], BF16, 2)
        xinR = Rot(nc, st, "xin", [128, D], F32, 2)
        SPECS = {
            "wch": ([128, 8, 512], BF16, 4), "qf": ([128, 4, 128], F32, 2), "sq": ([128, 4, 128], F32, 2),
            "qn": ([128, 4, 128], F32, 2), "rt": ([128, 4, 4, 16], F32, 2), "qb": ([128, 512], BF16, 2),
            "vf": ([128, 512], F32, 2), "v1": ([128, 4, 130], BF16, 3), "qu": ([128, 512], BF16, 2),
            "qT": ([128, 4, 128], BF16, 2), "pT": ([128, 4, 2, 128], BF16, 2), "ou": ([128, 4, 129], F32, 2),
            "mg": ([128, D], BF16, 2), "mgT": ([128, 8, 128], BF16, 2), "numt": ([128, 3, 516], F32, 2),
            "rd": ([128, 8], F32, 2),
        }
        W = {}
        uid = [0]

        def mk(stack, names, counts=None):
            uid[0] += 1
            for nm in names:
                shp, dt, n = SPECS[nm]
                if counts and nm in counts:
                    n = counts[nm]
                W[nm] = Rot(nc, stack, "%s%d" % (nm, uid[0]), shp, dt, n)

        def rstd_of(ss, ssk, ncol, inv_n):
            sd, sdk = sdR.next()
            rs, rsk = rsR.next()
            S.op("act", [ssk, "epsT"], [sdk], lambda e: e.activation(
                out=sd[:, 0:ncol], in_=ss[:, 0:ncol], func=AF.Sqrt, bias=epsT[:, 0:1], scale=inv_n))
            S.op("dve", [sdk], [rsk], lambda e: e.reciprocal(rs[:, 0:ncol], sd[:, 0:ncol]))
            return rs, rsk

        tp_i = [0]

        def next_tp(banks=(2, 3)):
            b = banks[tp_i[0] % 2]
            tp_i[0] += 1
            return PS[:, b, :].bitcast(BF16).rearrange("p (k n) -> p k n", k=8), "PS%d" % b

        def transposes(srcs, dst_ap, dstk, m=128):
            tp3, tpk = next_tp()

            def tr(e):
                for i, (a, _) in enumerate(srcs):
                    last = e.transpose(tp3[:, i, 0:m], a, identb[0:m, 0:m])
                return last
            S.op("pe", [k for _, k in srcs] + ["identb"], [tpk], tr)
            S.op("act", [tpk], [dstk], lambda e: e.activation(out=dst_ap, in_=tp3[:, 0:len(srcs), 0:m], func=AF.Copy))

        def rms_to_T(xt, xk, gbc, gk, dstT, dstk, col0, m=128):
            ss, ssk = ssR.next()
            S.op("act", [xk], ["junk", ssk], lambda e: e.activation(
                out=junk[0:m], in_=xt, func=AF.Square, accum_out=ss[0:m, 0:1]))
            rs, rsk = rstd_of(ss, ssk, 1, 1.0 / D)
            hb, hbk = hbR.next()
            S.op("dve", [xk, rsk, gk], [hbk], lambda e: e.scalar_tensor_tensor(
                out=hb[0:m], in0=xt, scalar=rs[0:m, 0:1], in1=gbc[0:m], op0=ALU.mult, op1=ALU.mult))
            transposes([(hb[0:m, k * 128:(k + 1) * 128], hbk) for k in range(8)],
                       dstT[:, :, col0:col0 + m], dstk, m)

        pj_i = [0]

        def next_pj(banks=(0, 1)):
            b = banks[pj_i[0] % len(banks)]
            pj_i[0] += 1
            return PS[:, b, :], "PS%d" % b

        def load_w(src_cols):
            w, wk = W["wch"].next()
            S.dma("pool", [], [wk], w[:], src_cols.rearrange("(k p) n -> p k n", p=128))
            return w, wk

        def proj_tile(w, wk, srcT, srck, col0, m=128):
            ps, psk = next_pj()

            def mm(e):
                for k in range(8):
                    last = e.matmul(ps[0:m, :], srcT[:, k, col0:col0 + m], w[:, k, :],
                                    start=(k == 0), stop=(k == 7))
                return last
            S.op("pe", [wk, srck], [psk], mm)
            return ps, psk

        def qk_post(ps, psk, gbc3, gbk, gidx, n_comb, rope=True, m=128):
            qf, qfk = W["qf"].next()
            S.op("act", [psk], [qfk], lambda e: e.activation(
                out=qf[0:m].rearrange("p h d -> p (h d)"), in_=ps[0:m, :], func=AF.Copy))
            sq, sqk = W["sq"].next()
            S.op("pool", [qfk], [sqk], lambda e: e.tensor_tensor(out=sq[0:m], in0=qf[0:m], in1=qf[0:m], op=ALU.mult))
            ss, ssk = ssR.next()
            S.op("dve", [sqk], [ssk], lambda e: e.tensor_reduce(out=ss[0:m, :], in_=sq[0:m], axis=AX.X, op=ALU.add))
            rs, rsk = rstd_of(ss, ssk, 4, 1.0 / 128)
            qn, qnk = W["qn"].next()
            S.op("dve", [qfk, rsk], [qnk], lambda e: e.tensor_tensor(
                out=qn[0:m], in0=qf[0:m], in1=rs[0:m, :].unsqueeze(2).to_broadcast([m, 4, 128]), op=ALU.mult))
            S.op("pool", [qnk, gbk], [qnk], lambda e: e.tensor_tensor(
                out=qn[0:m], in0=qn[0:m], in1=gbc3[0:m, gidx:gidx + 1, :].to_broadcast([m, 4, 128]), op=ALU.mult))
            if rope:
                rt, rtk = W["rt"].next()
                cosb = ropecs[0:m, n_comb, 0:16].unsqueeze(1).to_broadcast([m, 4, 16])
                sinb = ropecs[0:m, n_comb, 16:32].unsqueeze(1).to_broadcast([m, 4, 16])
                x1 = qn[0:m, :, 0:16]
                x2 = qn[0:m, :, 16:32]

                def r1(e):
                    e.tensor_tensor(out=rt[0:m, 0], in0=x1, in1=cosb, op=ALU.mult)
                    e.tensor_tensor(out=rt[0:m, 1], in0=x2, in1=sinb, op=ALU.mult)
                    e.tensor_tensor(out=rt[0:m, 2], in0=x2, in1=cosb, op=ALU.mult)
                    return e.tensor_tensor(out=rt[0:m, 3], in0=x1, in1=sinb, op=ALU.mult)
                S.op("dve", [qnk, "ropecs"], [rtk], r1)

                def r2(e):
                    e.tensor_tensor(out=x1, in0=rt[0:m, 0], in1=rt[0:m, 1], op=ALU.subtract)
                    return e.tensor_tensor(out=x2, in0=rt[0:m, 2], in1=rt[0:m, 3], op=ALU.add)
                S.op("dve", [rtk], [qnk], r2)
            return qn, qnk

        def to_bf16(qn, qnk, m=128):
            qb, qbk = W["qb"].next()
            S.op("act", [qnk], [qbk], lambda e: e.activation(
                out=qb[0:m, :], in_=qn[0:m].rearrange("p h d -> p (h d)"), func=AF.Copy))
            return qb, qbk

        def attn_unit(qT, qTk, kT, kTks, vu, vuks, use_mask):
            sc = PS[:, 4:6, :].rearrange("p b (x q) -> p (b x) q", q=128)

            def mm(e):
                for h in range(4):
                    for kb in range(2):
                        o = sc[:, h * 2 + kb, :]
                        last = e.matmul(o, kT[:, kb, h, :], qT[:, h, :], start=True, stop=not use_mask)
                        if use_mask:
                            last = e.matmul(o, identb[:], maskb[:, kb, :], start=False, stop=True)
                return last
            S.op("pe", [qTk] + kTks + ["identb", "maskb"], ["PS4", "PS5"], mm)
            pT, pTk = W["pT"].next()
            S.op("act", ["PS4", "PS5"], [pTk], lambda e: e.activation(
                out=pT[:].rearrange("p h k q -> p (h k) q"), in_=sc, func=AF.Exp, scale=SCALE))
            ov = PS[:, 6:8, 0:258].rearrange("p b (x c) -> p b x c", c=129)

            def pv(e):
                for h in range(4):
                    for kb in range(2):
                        last = e.matmul(ov[:, h // 2, h % 2, :], pT[:, h, kb, :], vu[:, kb, h, 0:129],
                                        start=(kb == 0), stop=(kb == 1))
                return last
            S.op("pe", [pTk] + vuks, ["PS6", "PS7"], pv)
            ou, ouk = W["ou"].next()
            S.op("dve", ["PS6", "PS7"], [ouk], lambda e: e.tensor_copy(
                ou[:].rearrange("p (b x) c -> p b (x c)", b=2), PS[:, 6:8, 0:258]))
            return ou, ouk

        g0bias = sb("g0bias_sb", [128, 4])
        biasnew = sb("biasnew_sb", [16, 2, 16])
        S.dma("sp", [], ["g0bias"], g0bias[:], g0bias_d[:, :])
        S.dma("sp", [], ["biasnew"], biasnew[:], biasnew_d[:, :, :])
        sqc = sb("sqc", [16, 2, 512])
        smg = sb("smg", [16, D], BF16)
        xsres = sb("xsres", [16, D])
        hTs = sb("hTs", [128, 8, 16], BF16)
        h2Ts = sb("h2Ts", [128, 8, 16], BF16)
        uTs = sb("uTs", [128, 4, 16], BF16)
        phs0 = contextlib.ExitStack()
        sqkv = sbt(phs0, "sqkv", [16, 3, 3, 512])
        pending = []
        for g, (Lg, r) in enumerate(GROUPS):
            for sq_ in range(4):
                r0 = 0
                while r0 < Lg - 4:
                    nr = min(256, Lg - 4 - r0)
                    pending.append((g, sq_, r0, nr))
                    r0 += nr

        def drip(k=1):
            for _ in range(k):
                if pending:
                    g_, s_, r0, nr = pending.pop(0)
                    S.dma("sp", [], ["swc%d_%d_%d" % (g_, s_, r0)], swin_o[g_][s_, r0:r0 + nr], cwin_d[g_][s_, r0 + 4:r0 + 4 + nr])

        with contextlib.ExitStack() as ph:
            hT = sbt(ph, "hT", [128, 8, TH], BF16)
            mk(ph, ["wch", "qf", "sq", "qn", "rt", "qb", "vf", "v1", "qu", "qT", "pT", "ou"])
            kuR = Rot(nc, ph, "ku", [128, 2, 512], BF16, 2)
            vuR = Rot(nc, ph, "vu", [128, 2, 4, 130], BF16, 2)
            kTR = Rot(nc, ph, "kT", [128, 2, 4, 128], BF16, 2)
            gb0, gb0k = load_gbc(g_mix[0:1, :])
            for n in range(NTH):
                xt, xk = xinR.next()
                S.dma("sp", [], [xk], xt[:], xkv[128 * n:128 * (n + 1), :])
                rms_to_T(xt[:], xk, gb0, gb0k, hT, "hT%d" % n, 128 * n)
            xt, xk = xinR.next()
            S.dma("sp", [], [xk], xt[0:16, :], xs_d[:, :])
            rms_to_T(xt[0:16, :], xk, gb0, gb0k, hTs, "hTs", 0, m=16)

            for g, (Lg, r) in enumerate(GROUPS):
                wq, wqk = load_w(w_in_a[:, g * 512:(g + 1) * 512])
                wk_, wkk = load_w(w_in_a[:, 1536 + g * 512:1536 + (g + 1) * 512])
                wv, wvk = load_w(w_in_a[:, 3072 + g * 512:3072 + (g + 1) * 512])
                n_lo = 16 - Lg // 128
                for n in range(n_lo, NTH):
                    own = n >= 16
                    srck = "hT%d" % n
                    in_win = own and (128 * (n - 16) >= T - Lg)
                    wrow = 128 * (n - 16) - (T - Lg)
                    ps, psk = proj_tile(wk_, wkk, hT, srck, 128 * n)
                    qn, qnk = qk_post(ps, psk, gk_bc, "gk_bc", g, n)
                    if in_win:
                        S.dma("sp", [qnk], ["wink%d_%d" % (g, n)], win_o[g][wrow:wrow + 128, 0, :, :], qn[:])
                    qb, qbk = to_bf16(qn, qnk)
                    S.dma("sp", [qbk], ["Ks%d_%d" % (g, n)], Ks[g][128 * n:128 * (n + 1), :], qb[:])
                    ps, psk = proj_tile(wv, wvk, hT, srck, 128 * n)
                    if in_win:
                        vf, vfk = W["vf"].next()
                        S.op("act", [psk], [vfk], lambda e, vf=vf, ps=ps: e.activation(out=vf[:], in_=ps, func=AF.Copy))
                        S.dma("sp", [vfk], ["winv%d_%d" % (g, n)], win_o[g][wrow:wrow + 128, 1, :, :],
                              vf[:].rearrange("p (h d) -> p h d", h=4))
                    v1, v1k = W["v1"].next()
                    fsrc = ones1 if own else flag

                    def vcp(e, v1=v1, ps=ps, fsrc=fsrc):
                        e.tensor_copy(v1[:, :, 0:128], ps.rearrange("p (h d) -> p h d", h=4))
                        return e.tensor_copy(v1[:, :, 128:130], fsrc[:, 0:1].unsqueeze(1).to_broadcast([128, 4, 2]))
                    S.op("dve", [psk, "flag", "ones1"], [v1k], vcp)
                    S.dma("sp", [v1k], ["Vs%d_%d" % (g, n)], Vs[g][128 * n:128 * (n + 1), :],
                          v1[:].rearrange("p h c -> p (h c)"))
                    if own:
                        ps, psk = proj_tile(wq, wqk, hT, srck, 128 * n)
                        qn, qnk = qk_post(ps, psk, gq_bc, "gq_bc", g, n)
                        qb, qbk = to_bf16(qn, qnk)
                        S.dma("sp", [qbk], ["Qs%d_%d" % (g, n - 16)], Qs[g][128 * (n - 16):128 * (n - 15), :], qb[:])

                for qi, (w_, wk2, gb3, gbk3) in enumerate(((wq, wqk, gq_bc, "gq_bc"), (wk_, wkk, gk_bc, "gk_bc"))):
                    ps, psk = proj_tile(w_, wk2, hTs, "hTs", 0, m=16)
                    qn, qnk = qk_post(ps, psk, gb3, gbk3, g, NTH, m=16)
                    S.op("pool", [qnk], ["sqkv%d_%d" % (qi, g)], lambda e, qn=qn, qi=qi, g=g: e.tensor_copy(
                        sqkv[0:16, qi, g, :], qn[0:16].rearrange("p h d -> p (h d)")))
                ps, psk = proj_tile(wv, wvk, hTs, "hTs", 0, m=16)
                S.op("act", [psk], ["sqkv2_%d" % g], lambda e, ps=ps, g=g: e.activation(
                    out=sqkv[0:16, 2, g, :], in_=ps[0:16, :], func=AF.Copy))
                for sq_ in range(4):
                    for kv_ in range(2):
                        S.dma("sp", ["sqkv%d_%d" % (kv_ + 1, g)], ["swn%d_%d_%d" % (g, sq_, kv_)],
                              swin_o[g][sq_, Lg - 4:Lg, kv_, :, :],
                              sqkv[4 * sq_:4 * sq_ + 4, kv_ + 1, g, :].rearrange("p (h d) -> p h d", h=4))
                nblk = 16 // r
                for rho in range(r):
                    for b in range(nblk):
                        q0 = rho + r * 128 * b
                        span = r * 127 + 1
                        qtiles = sorted(set((q0 + r * i) // 128 for i in range(128)))
                        c0 = 2048 + q0
                        p0 = c0 - 128 * r
                        ctiles = sorted(set((c0 + r * i) // 128 for i in range(128)))
                        ptiles = sorted(set((p0 + r * i) // 128 for i in range(128)))
                        drip(1)
                        qu, quk = W["qu"].next()
                        ku, kuk = kuR.next()
                        vu, vuk = vuR.next()
                        S.dma("sp", ["Qs%d_%d" % (g, t) for t in qtiles], [quk], qu[:], Qs[g][q0:q0 + span:r, :])
                        S.dma("sp", ["Ks%d_%d" % (g, t) for t in ptiles], [kuk + "a"], ku[:, 0, :], Ks[g][p0:p0 + span:r, :])
                        S.dma("sp", ["Ks%d_%d" % (g, t) for t in ctiles], [kuk + "b"], ku[:, 1, :], Ks[g][c0:c0 + span:r, :])
                        S.dma("sp", ["Vs%d_%d" % (g, t) for t in ptiles], [vuk + "a"],
                              vu[:, 0].rearrange("p h c -> p (h c)"), Vs[g][p0:p0 + span:r, :])
                        S.dma("sp", ["Vs%d_%d" % (g, t) for t in ctiles], [vuk + "b"],
                              vu[:, 1].rearrange("p h c -> p (h c)"), Vs[g][c0:c0 + span:r, :])
                        qT, qTk = W["qT"].next()
                        kT, kTk = kTR.next()
                        transposes([(qu[:, h * 128:(h + 1) * 128], quk) for h in range(4)], qT[:], qTk)
                        transposes([(ku[:, 0, h * 128:(h + 1) * 128], kuk + "a") for h in range(4)], kT[:, 0], kTk + "a")
                        transposes([(ku[:, 1, h * 128:(h + 1) * 128], kuk + "b") for h in range(4)], kT[:, 1], kTk + "b")
                        ou, ouk = attn_unit(qT, qTk, kT, [kTk + "a", kTk + "b"], vu, [vuk + "a", vuk + "b"], True)
                        S.dma("sp", [ouk], ["NUM%d_%d" % (g, t) for t in qtiles], NUM[g][q0:q0 + span:r, :],
                              ou[:].rearrange("p h c -> p (h c)"))

            wqc, wqck = load_w(w_in_a[:, 4608:5120])
            for n in range(16, NTH):
                ps, psk = proj_tile(wqc, wqck, hT, "hT%d" % n, 128 * n)
                qn, qnk = qk_post(ps, psk, gqc_bc, "gqc_bc", 0, n, rope=False)
                qb, qbk = to_bf16(qn, qnk)
                S.dma("sp", [qbk], ["QC_%d" % (n - 16)], QC[128 * (n - 16):128 * (n - 15), :], qb[:])
            ps, psk = proj_tile(wqc, wqck, hTs, "hTs", 0, m=16)
            qn, qnk = qk_post(ps, psk, gqc_bc, "gqc_bc", 0, NTH, rope=False, m=16)
            S.op("pool", [qnk], ["sqc0"], lambda e, qn=qn: e.tensor_copy(sqc[0:16, 0, :], qn[0:16].rearrange("p h d -> p (h d)")))
            drip(100)
            barrier()


        def sample_attention(layer, with_windows):
            with contextlib.ExitStack() as pa:
                kvR = Rot(nc, pa, "skv%d" % layer, [128, 2, 4, 128], F32, 3)
                prR = Rot(nc, pa, "spr%d" % layer, [128, 4, 128], F32, 2)
                scR = Rot(nc, pa, "ssc%d" % layer, [128, 4], F32, 4)
                ncc = (48 if with_windows else 0) + 32
                lt = sbt(pa, "slt%d" % layer, [128, ncc, 4, 16])
                lt2 = sbt(pa, "slt2%d" % layer, [16, 48 if with_windows else 1, 4, 16])
                selT = sbt(pa, "selT%d" % layer, [16, 16, 128])
                S.op("pool", [], ["lt0"], lambda e: e.memset(lt[:], 0.0))
                S.op("pool", [], ["lt20"], lambda e: e.memset(lt2[:], 0.0))
                S.op("dve", ["identf"], ["selT"], lambda e: e.tensor_copy(
                    selT[:], identf[0:16, 0:16].unsqueeze(2).to_broadcast([16, 16, 128])))
                NUMm, DENm, NUMc, DENc = PS[0:16, 2, :], PS[0:16, 3, 0:4], PS[0:16, 4, :], PS[0:16, 5, 0:4]
                started = {}
                totals = {"PS2": 96, "PS4": 32}
                qb_i = [0]
                cidx = [0, 0]

                def q_bcast(src_ap, srck, row):
                    bank = qb_i[0] % 2
                    qb_i[0] += 1
                    S.op("pe", [srck, "selT"], ["PS%d" % bank], lambda e: e.matmul(
                        PS[:, bank, :], selT[0:16, row, :], src_ap, start=True, stop=True))
                    return PS[:, bank, :].rearrange("p (h d) -> p h d", h=4), "PS%d" % bank

                def combo(qb, qbk, row, K_ap, V_ap, kvk, bias_ap, biask, acc, npart, ltile, which):
                    num_ps, den_ps, numk, denk = acc
                    pr, prk = prR.next()
                    sc_, sck = scR.next()
                    S.op("dve", [qbk] + kvk, [prk], lambda e: e.tensor_tensor(
                        out=pr[0:npart], in0=K_ap, in1=qb[0:npart], op=ALU.mult))
                    S.op("dve", [prk], [sck], lambda e: e.tensor_reduce(
                        out=sc_[0:npart, :], in_=pr[0:npart], axis=AX.X, op=ALU.add))
                    c = cidx[which]
                    cidx[which] += 1
                    dst = ltile[0:npart, c, :, row]
                    ltk = "lt%d_%d" % (which, c)
                    if bias_ap is None:
                        S.op("act", [sck, "lt0", "lt20"], [ltk], lambda e: e.activation(
                            out=dst, in_=sc_[0:npart, :], func=AF.Exp, scale=SCALE))
                    else:
                        S.op("act", [sck, biask, "lt0", "lt20"], [ltk], lambda e: e.activation(
                            out=dst, in_=sc_[0:npart, :], func=AF.Exp, scale=SCALE, bias=bias_ap))
                    first = False
                    started[numk] = started.get(numk, 0) + 1
                    fin = started[numk] == totals[numk]

                    def pe(e):
                        for h in range(4):
                            e.matmul(num_ps[:, h * 128:(h + 1) * 128], ltile[0:npart, c, h, :], V_ap[:, h, :],
                                     start=first, stop=fin)
                            last = e.matmul(den_ps[:, h:h + 1], ltile[0:npart, c, h, :], ones1[0:npart, 0:1],
                                            start=first, stop=fin)
                        return last
                    S.op("pe", [ltk] + kvk + ["ones1"], [numk, denk], pe)

                mix_acc = (NUMm, DENm, "PS2", "PS3")
                crs_acc = (NUMc, DENc, "PS4", "PS5")
                zt = sbt(pa, "szt%d" % layer, [16, 16])
                S.op("pool", [], ["szt"], lambda e: e.memset(zt[:], 0.0))
                for (num_ps, den_ps, numk, denk) in ([mix_acc] if with_windows else []) + [crs_acc]:
                    def z0(e, num_ps=num_ps, den_ps=den_ps):
                        e.matmul(num_ps, zt[0:16, :], sqc[0:16, layer, :], start=True, stop=False)
                        return e.matmul(den_ps, zt[0:16, :], sqc[0:16, layer, 0:4], start=True, stop=False)
                    S.op("pe", ["szt", "sqc%d" % layer], [numk, denk], z0)
                    started[numk] = 0
                for sq_ in range(4):
                    if with_windows:
                        for g, (Lg, r) in enumerate(GROUPS):
                            kv0 = None
                            for t in range(4):
                                row = 4 * sq_ + t
                                qb, qbk = q_bcast(sqkv[0:16, 0, g, :], "sqkv0_%d" % g, row)
                                if g == 0:
                                    if kv0 is None:
                                        kv0 = kvR.next()
                                        S.dma("sp", [], [kv0[1]], kv0[0][:], cwin_d[0][sq_, 0:128])
                                    kv, kvk = kv0
                                    bias_ap, biask = g0bias[:, t:t + 1], "g0bias"
                                else:
                                    kv, kvk = kvR.next()
                                    S.dma("sp", [], [kvk], kv[:], cwin_d[g][sq_, t:t + 127 * r + 1:r])
                                    bias_ap, biask = None, None
                                combo(qb, qbk, row, kv[:, 0], kv[:, 1], [kvk], bias_ap, biask, mix_acc, 128, lt, 0)
                                combo(qb, qbk, row, sqkv[0:16, 1, g, :].rearrange("p (h d) -> p h d", h=4),
                                      sqkv[0:16, 2, g, :].rearrange("p (h d) -> p h d", h=4),
                                      ["sqkv1_%d" % g, "sqkv2_%d" % g], biasnew[0:16, 0 if g == 0 else 1, row:row + 1],
                                      "biasnew", mix_acc, 16, lt2, 1)
                    kvm = [kvR.next() for _ in range(2)]
                    for mt in range(2):
                        S.dma("sp", [], [kvm[mt][1]], kvm[mt][0][:], cmem_d[layer, sq_, 128 * mt:128 * (mt + 1)])
                    for t in range(4):
                        row = 4 * sq_ + t
                        qb, qbk = q_bcast(sqc[0:16, layer, :], "sqc%d" % layer, row)
                        for mt in range(2):
                            combo(qb, qbk, row, kvm[mt][0][:, 0], kvm[mt][0][:, 1], [kvm[mt][1]], None, None,
                                  crs_acc, 128, lt, 0)
                nf = sbt(pa, "snf%d" % layer, [16, 2, 516])
                accs = ([(0, mix_acc, 0)] if with_windows else []) + [(1, crs_acc, 512)]
                for ai, (num_ps, den_ps, numk, denk), col0 in accs:
                    def cp(e, ai=ai, num_ps=num_ps, den_ps=den_ps):
                        e.activation(out=nf[0:16, ai, 0:512], in_=num_ps, func=AF.Copy)
                        return e.activation(out=nf[0:16, ai, 512:516], in_=den_ps, func=AF.Copy)
                    S.op("act", [numk, denk], ["snf%d" % ai], cp)

                    def nrm(e, ai=ai, col0=col0):
                        e.reciprocal(nf[0:16, ai, 512:516], nf[0:16, ai, 512:516])
                        return e.tensor_tensor(
                            out=smg[0:16, col0:col0 + 512].rearrange("p (h d) -> p h d", h=4),
                            in0=nf[0:16, ai, 0:512].rearrange("p (h d) -> p h d", h=4),
                            in1=nf[0:16, ai, 512:516].unsqueeze(2).to_broadcast([16, 4, 128]), op=ALU.mult)
                    S.op("dve", ["snf%d" % ai], ["smg_%d" % col0], nrm, chain=True)
                barrier()

        def sample_tail(layer, x_src_ap, x_srck, gf, gfk):
            mgT, mgTk = W["mgT"].next()
            if layer == 0:
                transposes([(smg[0:16, k * 128:(k + 1) * 128], "smg_%d" % (0 if k < 4 else 512)) for k in range(8)],
                           mgT[:, :, 0:16], mgTk, m=16)
                lhs = [(mgT[:, k, 0:16], mgTk) for k in range(8)]
            else:
                transposes([(smg[0:16, 512 + k * 128:512 + (k + 1) * 128], "smg_512") for k in range(4)],
                           mgT[:, 4:8, 0:16], mgTk, m=16)
                lhs = [(uTs[:, k, :], "uTs") for k in range(4)] + [(mgT[:, 4 + k, 0:16], mgTk) for k in range(4)]
            out_proj_residual(0, lhs, x_src_ap, x_srck, dst_ap=xsres[0:16, :], dstk="xsres", m=16)
            rms_to_T(xsres[0:16, :], "xsres", gf, gfk, h2Ts, "h2Ts", 0, m=16)

        sample_attention(0, True)
        phs0.close()
        if stage <= 1:
            return end()

        xres = sb("xres", [128, NT, D])
        KmT = sb("KmT", [128, 2, 4, 128], BF16)
        V1m = sb("V1m", [128, 2, 4, 130], BF16)
        S.op("dve", [], ["V1mones"], lambda e: e.memset(V1m[:], 1.0))

        def memory_kv(layer, ph):
            memT = sbt(ph, "memT%d" % layer, [128, 8, 256], BF16)
            gm, gmk = load_gbc(g_mem[layer:layer + 1, :])
            for mt in range(2):
                xt, xk = xinR.next()
                S.dma("sp", [], [xk], xt[:], mem_d[128 * mt:128 * (mt + 1), :])
                rms_to_T(xt[:], xk, gm, gmk, memT, "memT%d" % mt, 128 * mt)
            DM = int(os.environ.get("DBG_M", "9"))
            if DM <= 1:
                return
            wk_, wkk = load_w(w_mem_kv[layer, :, 0:512])
            wv, wvk = load_w(w_mem_kv[layer, :, 512:1024])
            for mt in range(2):
                ps, psk = proj_tile(wk_, wkk, memT, "memT%d" % mt, 128 * mt)
                qn, qnk = qk_post(ps, psk, gkc_bc, "gkc_bc", layer, 0, rope=False)
                S.dma("sp", [qnk], ["memk%d_%d" % (layer, mt)], memkv_o[layer, 128 * mt:128 * (mt + 1), 0, :, :], qn[:])
                qb, qbk = to_bf16(qn, qnk)
                transposes([(qb[:, h * 128:(h + 1) * 128], qbk) for h in range(4)], KmT[:, mt], "KmT%d" % mt)
                if DM <= 2:
                    continue
                ps, psk = proj_tile(wv, wvk, memT, "memT%d" % mt, 128 * mt)
                vf, vfk = W["vf"].next()
                S.op("act", [psk], [vfk], lambda e, vf=vf, ps=ps: e.activation(out=vf[:], in_=ps, func=AF.Copy))
                S.dma("sp", [vfk], ["memv%d_%d" % (layer, mt)], memkv_o[layer, 128 * mt:128 * (mt + 1), 1, :, :],
                      vf[:].rearrange("p (h d) -> p h d", h=4))

                S.op("act", [psk, "V1mones"], ["V1m%d" % mt], lambda e, mt=mt, ps=ps: e.activation(
                    out=V1m[:, mt, :, 0:128], in_=ps.rearrange("p (h d) -> p h d", h=4), func=AF.Copy))

        def load_wout(layer, stack):
            W["wout"] = sbt(stack, "wout%d" % layer, [128, 8, D], BF16)
            S.dma("pool", [], ["wout"], W["wout"][:], w_out[layer].rearrange("(k p) n -> p k n", p=128))

        def normalize_into(mg, mgk, col0, src, srck, srcap):
            rd, rdk = W["rd"].next()
            S.op("dve", [srck], [rdk], lambda e: e.reciprocal(rd[:, 0:4], srcap[:, :, 128]))
            S.op("dve", [srck, rdk], [mgk + "_%d" % col0], lambda e: e.tensor_tensor(
                out=mg[:, col0:col0 + 512].rearrange("p (h d) -> p h d", h=4), in0=srcap[:, :, 0:128],
                in1=rd[:, 0:4].unsqueeze(2).to_broadcast([128, 4, 128]), op=ALU.mult))

        def cross_attention(n, mg, mgk):
            qu, quk = W["qu"].next()
            S.dma("sp", ["QC_%d" % n], [quk], qu[:], QC[128 * n:128 * (n + 1), :])
            qT, qTk = W["qT"].next()
            transposes([(qu[:, h * 128:(h + 1) * 128], quk) for h in range(4)], qT[:], qTk)
            ou, ouk = attn_unit(qT, qTk, KmT, ["KmT0", "KmT1"], V1m, ["V1m0", "V1m1"], False)
            normalize_into(mg, mgk, 512, ou, ouk, ou[:])

        def out_proj_residual(n, lhs, x_src_ap, x_srck, dst_ap=None, dstk=None, m=128):
            if dst_ap is None:
                dst_ap, dstk = xres[:, n, :], "xres%d" % n

            def mm(e):
                for hf in range(2):
                    for k in range(8):
                        last = e.matmul(PS[0:m, hf, :], lhs[k][0], W["wout"][:, k, hf * 512:(hf + 1) * 512],
                                        start=(k == 0), stop=(k == 7))
                return last
            S.op("pe", sorted(set(k for _, k in lhs)) + ["wout"], ["PS0", "PS1"], mm)
            S.op("dve", ["PS0", "PS1", x_srck], [dstk], lambda e: e.tensor_tensor(
                out=dst_ap, in0=PS[0:m, 0:2, :].rearrange("p a b -> p (a b)"), in1=x_src_ap, op=ALU.add))

        def ffn(layer, ph, h2T, final_out):
            gT = sbt(ph, "gT%d" % layer, [128, NFF, 512], BF16)
            wupR = Rot(nc, ph, "wup%d" % layer, [128, 8, 2, 128], BF16, 3)
            wdnR = Rot(nc, ph, "wdn%d" % layer, [128, D], BF16, 2)
            cR = Rot(nc, ph, "cv%d" % layer, [128, 2, 512], F32, 2)
            saR = Rot(nc, ph, "sa%d" % layer, [128, 512], F32, 2)
            cwl = sbt(ph, "cwl%d" % layer, [44, 4, 128])
            cwT = sbt(ph, "cwT%d" % layer, [128, 4, 44])
            tails = sbt(ph, "tails%d" % layer, [128, 2, 44])
            tlo = sbt(ph, "tlo%d" % layer, [88, 128])
            hx = sbt(ph, "hx%d" % layer, [128, 16], BF16)
            hxf = sbt(ph, "hxf%d" % layer, [128, 32])
            hxr = sbt(ph, "hxr%d" % layer, [128, 32])
            for j in range(3):
                S.dma("sp", [], ["cwl%d" % j], cwl[:, j, :], conv_w[layer, j, :].rearrange("(t p) -> t p", p=128))
            S.dma("sp", [], ["cwl3"], cwl[:, 3, :], conv_b[layer, :].rearrange("(t p) -> t p", p=128))
            cps = PS[:, 7, 0:176].rearrange("p (j t) -> p j t", j=4)

            def ctr(e):
                for j in range(4):
                    last = e.transpose(cps[:, j, :], cwl[:, j, :], identf[0:44, 0:44])
                return last
            S.op("pe", ["cwl0", "cwl1", "cwl2", "cwl3", "identf"], ["PS7"], ctr)
            S.op("dve", ["PS7"], ["cwT"], lambda e: e.tensor_copy(cwT[:], cps))
            ci = layer
            S.op("dve", ["h2T%d" % 15], ["hxf"], lambda e: e.tensor_copy(
                hxf[:, 0:16].rearrange("p (k t) -> p k t", t=2), h2T[:, :, 2 + T - 2:2 + T]))
            S.dma("pool", ["hxf"], ["ccs%d" % ci], cc_src[ci][:, 0:16], hxf[:, 0:16])
            S.allgather_pairs(["ccs%d" % ci], ["ccd%d" % ci], cc_src[ci][:, :], cc_dst[ci][:, :])
            S.dma("pool", ["ccd%d" % ci], ["hxr"], hxr[:, 0:16], cc_dst[ci][0:128, 0:16])
            S.op("dve", ["hxr", "flag"], ["h2Th"], lambda e: e.tensor_scalar(
                out=h2T[:, :, 0:2], in0=hxr[:, 0:16].rearrange("p (k t) -> p k t", t=2),
                scalar1=flag[:, 0:1], scalar2=None, op0=ALU.mult))
            up_banks = [(0, 1), (2, 3)]
            upi = 0
            for blk in range(4):
                t0 = 2 + 512 * blk
                hkeys = ["h2T%d" % (4 * blk + j) for j in range(4)] + (["h2Th"] if blk == 0 else ["h2T%d" % (4 * blk - 1)])
                for i in range(NFF):
                    wup, wupk = wupR.next()
                    S.dma("pool", [], [wupk], wup[:, :, 0, :],
                          w_up[layer, :, 128 * i:128 * (i + 1)].rearrange("(k p) n -> p k n", p=128))
                    S.dma("pool", [], [wupk + "b"], wup[:, :, 1, :],
                          w_up[layer, :, DFF + 128 * i:DFF + 128 * (i + 1)].rearrange("(k p) n -> p k n", p=128))
                    ba, bb = up_banks[upi % 2]
                    upi += 1
                    hb_ = PS[:, 4, 0:8].rearrange("p (i a t) -> p i a t", i=2, a=2)[:, i % 2]

                    def mm(e, wup=wup, ba=ba, bb=bb, hb_=hb_):
                        for ab, bank in ((0, ba), (1, bb)):
                            for k in range(8):
                                e.matmul(PS[:, bank, :], wup[:, k, ab, :], h2T[:, k, t0:t0 + 512],
                                         start=(k == 0), stop=(k == 7))
                            for k in range(8):
                                last = e.matmul(hb_[:, ab, :], wup[:, k, ab, :], h2T[:, k, t0 - 2:t0],
                                                start=(k == 0), stop=(k == 7))
                        return last
                    S.op("pe", [wupk, wupk + "b"] + hkeys, ["PS%d" % ba, "PS%d" % bb, "PS4h%d" % (i % 2), "PS4"], mm)
                    cv, cvk = cR.next()

                    def conv(e, cv=cv, ba=ba, bb=bb, hb_=hb_, i=i):
                        for ab, bank in ((0, ba), (1, bb)):
                            ti = ab * NFF + i
                            up = PS[:, bank, :]
                            c = cv[:, ab, :]
                            e.tensor_scalar(out=c, in0=up, scalar1=cwT[:, 2, ti:ti + 1], scalar2=cwT[:, 3, ti:ti + 1],
                                            op0=ALU.mult, op1=ALU.add)
                            e.scalar_tensor_tensor(out=c[:, 1:512], in0=up[:, 0:511], scalar=cwT[:, 1, ti:ti + 1],
                                                   in1=c[:, 1:512], op0=ALU.mult, op1=ALU.add)
                            e.scalar_tensor_tensor(out=c[:, 2:512], in0=up[:, 0:510], scalar=cwT[:, 0, ti:ti + 1],
                                                   in1=c[:, 2:512], op0=ALU.mult, op1=ALU.add)
                            e.scalar_tensor_tensor(out=c[:, 0:2], in0=hb_[:, ab, :], scalar=cwT[:, 0, ti:ti + 1],
                                                   in1=c[:, 0:2], op0=ALU.mult, op1=ALU.add)
                            last = e.scalar_tensor_tensor(out=c[:, 0:1], in0=hb_[:, ab, 1:2], scalar=cwT[:, 1, ti:ti + 1],
                                                          in1=c[:, 0:1], op0=ALU.mult, op1=ALU.add)
                            if blk == 3:
                                last = e.tensor_copy(tails[:, :, ti], up[:, 510:512])
                        return last
                    S.op("dve", ["PS%d" % ba, "PS%d" % bb, "PS4h%d" % (i % 2), "cwT"],
                         [cvk] + (["tails"] if blk == 3 else []), conv, chain=True)
                    sa, sak = saR.next()
                    S.op("act", [cvk], [sak], lambda e, sa=sa, cv=cv: e.activation(out=sa[:], in_=cv[:, 0, :], func=AF.Silu))
                    S.op("pool", [sak, cvk], ["gT%d" % i], lambda e, sa=sa, cv=cv, i=i: e.tensor_tensor(
                        out=gT[:, i, :], in0=sa[:], in1=cv[:, 1, :], op=ALU.mult))
                for i in range(NFF):
                    wdn, wdnk = wdnR.next()
                    S.dma("pool", [], [wdnk], wdn[:], w_down[layer, 128 * i:128 * (i + 1), :])

                    def dn(e, wdn=wdn, i=i):
                        for tt in range(4):
                            for hf in range(2):
                                last = e.matmul(PS[:, 2 * tt + hf, :], gT[:, i, 128 * tt:128 * (tt + 1)],
                                                wdn[:, hf * 512:(hf + 1) * 512], start=(i == 0), stop=(i == NFF - 1))
                        return last
                    S.op("pe", [wdnk, "gT%d" % i], ["PS%d" % b for b in range(8)] + ["PS4h0", "PS4h1"], dn)
                for tt in range(4):
                    n = 4 * blk + tt
                    S.op("dve", ["PS%d" % (2 * tt), "PS%d" % (2 * tt + 1), "xres%d" % n] + (["PS4h0", "PS4h1"] if tt == 2 else []), ["xres%d" % n],
                         lambda e, tt=tt, n=n: e.tensor_tensor(
                             out=xres[:, n, :], in0=PS[:, 2 * tt:2 * tt + 2, :].rearrange("p a b -> p (a b)"),
                             in1=xres[:, n, :], op=ALU.add))
                    if final_out:
                        S.dma("sp", ["xres%d" % n], ["y%d" % n], y_o[128 * n:128 * (n + 1), :], xres[:, n, :])

            gTs = sbt(ph, "gTs%d" % layer, [128, NFF, 16], BF16)
            cstT = sbt(ph, "cstT%d" % layer, [128, 44, 8])
            tls = sbt(ph, "tls%d" % layer, [128, 8, 44])
            tlso = sbt(ph, "tlso%d" % layer, [128, 3, 128])
            extR = Rot(nc, ph, "ext%d" % layer, [128, 2, 4, 6], F32, 2)
            cvsR = Rot(nc, ph, "cvs%d" % layer, [128, 2, 4, 4], F32, 2)
            sasR = Rot(nc, ph, "sas%d" % layer, [128, 16], F32, 2)
            for q in range(6):
                xt, xk = xinR.next()
                ncol = min(1024, 2 * DFF - 1024 * q)
                S.dma("sp", [], [xk], xt[0:8, 0:ncol], cst_d[layer, :, 1024 * q:1024 * q + ncol])
                ntile = ncol // 128
                cps = PS[:, 6, 0:64].rearrange("p (t c) -> p t c", c=8)

                def ctr2(e, xt=xt, ntile=ntile, cps=cps):
                    for t in range(ntile):
                        last = e.transpose(cps[:, t, :], xt[0:8, 128 * t:128 * (t + 1)], identf[0:8, 0:8])
                    return last
                S.op("pe", [xk, "identf"], ["PS6"], ctr2)
                S.op("dve", ["PS6"], ["cstT"], lambda e, q=q, ntile=ntile, cps=cps: e.tensor_copy(
                    cstT[:, 8 * q:8 * q + ntile, :], cps[:, 0:ntile, :]))
            for i in range(NFF):
                wup, wupk = wupR.next()
                S.dma("pool", [], [wupk], wup[:, :, 0, :],
                      w_up[layer, :, 128 * i:128 * (i + 1)].rearrange("(k p) n -> p k n", p=128))
                S.dma("pool", [], [wupk + "b"], wup[:, :, 1, :],
                      w_up[layer, :, DFF + 128 * i:DFF + 128 * (i + 1)].rearrange("(k p) n -> p k n", p=128))
                bank = i % 2
                ups = PS[:, bank, 0:32].rearrange("p (a t) -> p a t", a=2)

                def mms(e, wup=wup, ups=ups):
                    for ab in range(2):
                        for k in range(8):
                            last = e.matmul(ups[:, ab, :], wup[:, k, ab, :], h2Ts[:, k, :], start=(k == 0), stop=(k == 7))
                    return last
                S.op("pe", [wupk, wupk + "b", "h2Ts"], ["PS%d" % bank], mms)
                ext, extk = extR.next()
                cvs, cvsk = cvsR.next()

                def convs(e, ext=ext, cvs=cvs, ups=ups, i=i):
                    for ab in range(2):
                        ti = ab * NFF + i
                        e.tensor_copy(ext[:, ab, :, 0:2], cstT[:, ti, :].rearrange("p (s t) -> p s t", t=2))
                        e.tensor_copy(ext[:, ab, :, 2:6], ups[:, ab, :].rearrange("p (s t) -> p s t", t=4))
                        e.tensor_copy(tls[:, :, ti].rearrange("p (s t) -> p s t", t=2), ext[:, ab, :, 4:6])
                        c = cvs[:, ab]
                        e.tensor_scalar(out=c, in0=ext[:, ab, :, 2:6], scalar1=cwT[:, 2, ti:ti + 1],
                                        scalar2=cwT[:, 3, ti:ti + 1], op0=ALU.mult, op1=ALU.add)
                        e.scalar_tensor_tensor(out=c, in0=ext[:, ab, :, 1:5], scalar=cwT[:, 1, ti:ti + 1], in1=c,
                                               op0=ALU.mult, op1=ALU.add)
                        e.scalar_tensor_tensor(out=c, in0=ext[:, ab, :, 0:4], scalar=cwT[:, 0, ti:ti + 1], in1=c,
                                               op0=ALU.mult, op1=ALU.add)
                S.op("dve", ["PS%d" % bank, "cstT", "cwT"], [extk, cvsk, "tls"], convs, chain=True)
                sas, sask = sasR.next()
                S.op("act", [cvsk], [sask], lambda e, sas=sas, cvs=cvs: e.activation(
                    out=sas[:].rearrange("p (s t) -> p s t", t=4), in_=cvs[:, 0], func=AF.Silu))
                S.op("pool", [sask, cvsk], ["gTs%d" % i], lambda e, sas=sas, cvs=cvs, i=i: e.tensor_tensor(
                    out=gTs[:, i, :].rearrange("p (s t) -> p s t", t=4), in0=sas[:].rearrange("p (s t) -> p s t", t=4),
                    in1=cvs[:, 1], op=ALU.mult))
            for i in range(NFF):
                wdn, wdnk = wdnR.next()
                S.dma("pool", [], [wdnk], wdn[:], w_down[layer, 128 * i:128 * (i + 1), :])

                def dns(e, wdn=wdn, i=i):
                    for hf in range(2):
                        last = e.matmul(PS[0:16, 2 + hf, :], gTs[:, i, :], wdn[:, hf * 512:(hf + 1) * 512],
                                        start=(i == 0), stop=(i == NFF - 1))
                    return last
                S.op("pe", [wdnk, "gTs%d" % i], ["PS2", "PS3"], dns)
            S.op("dve", ["PS2", "PS3", "xsres"], ["xsres"], lambda e: e.tensor_tensor(
                out=xsres[0:16, :], in0=PS[0:16, 2:4, :].rearrange("p a b -> p (a b)"), in1=xsres[0:16, :], op=ALU.add))
            if final_out:
                S.dma("sp", ["xsres"], ["ys"], ys_o[:, :], xsres[0:16, :])
            tl2 = tls[:].rearrange("p a t -> p (a t)")
            for q in range(3):
                ncol = min(128, 352 - 128 * q)
                S.op("pe", ["tls", "identf"], ["PS7"], lambda e, q=q, ncol=ncol: e.transpose(
                    PS[0:ncol, 7, 0:128], tl2[:, 128 * q:128 * q + ncol], identf[:]))
                S.op("dve", ["PS7"], ["tlso%d" % q], lambda e, q=q, ncol=ncol: e.tensor_copy(
                    tlso[0:ncol, q, :], PS[0:ncol, 7, 0:128]))
                S.dma("sp", ["tlso%d" % q], ["sconv%d_%d" % (layer, q)], sconv_o[layer, 128 * q:128 * q + ncol, :],
                      tlso[0:ncol, q, :])
            tps = PS[:, 7, 0:128]

            def ttr(e):
                return e.transpose(tps[0:88, :], tails[:].rearrange("p t c -> p (t c)"), identf[:])
            S.op("pe", ["tails", "identf"], ["PS7"], ttr)
            S.op("dve", ["PS7"], ["tlo"], lambda e: e.tensor_copy(tlo[:], tps[0:88, :]))
            S.dma("sp", ["tlo"], ["convo%d" % layer], conv_o[layer, :, :], tlo[:])

        with contextlib.ExitStack() as ph:
            h2T = sbt(ph, "h2T0", [128, 8, 2 + T], BF16)
            with contextlib.ExitStack() as ph1:
                mk(ph1, ["wch", "qf", "sq", "qn", "qb", "vf"], {"wch": 2})
                memory_kv(0, ph1)
                barrier()
            phb = contextlib.ExitStack()
            phb.__enter__()
            DB = int(os.environ.get("DBG_B", "9"))
            if DB <= 1:
                return end()
            mk(phb, ["qu", "qT", "pT", "ou", "mg", "mgT", "numt", "rd"])
            load_wout(0, phb)
            gf, gfk = load_gbc(g_ffn[0:1, :])
            for n in range(NT):
                mg, mgk = W["mg"].next()
                nt_, ntk = W["numt"].next()
                for g in range(3):
                    S.dma("sp", ["NUM%d_%d" % (g, n)], [ntk + "_%d" % g], nt_[:, g, :], NUM[g][128 * n:128 * (n + 1), :])
                S.op("pool", [ntk + "_0", ntk + "_1"], [ntk + "_0"], lambda e, nt_=nt_: e.tensor_tensor(
                    out=nt_[:, 0, :], in0=nt_[:, 0, :], in1=nt_[:, 1, :], op=ALU.add))
                S.op("pool", [ntk + "_0", ntk + "_2"], [ntk + "_0"], lambda e, nt_=nt_: e.tensor_tensor(
                    out=nt_[:, 0, :], in0=nt_[:, 0, :], in1=nt_[:, 2, :], op=ALU.add))
                normalize_into(mg, mgk, 0, nt_, ntk + "_0", nt_[:, 0, :].rearrange("p (h c) -> p h c", c=129))
                cross_attention(n, mg, mgk)
                mgT, mgTk = W["mgT"].next()
                transposes([(mg[:, k * 128:(k + 1) * 128], mgk + "_%d" % (0 if k < 4 else 512)) for k in range(8)],
                           mgT[:], mgTk)
                xt, xk = xinR.next()
                S.dma("sp", [], [xk], xt[:], xkv[T + 128 * n:T + 128 * (n + 1), :])
                if DB <= 4:
                    continue
                out_proj_residual(n, [(mgT[:, k, :], mgTk) for k in range(8)], xt[:], xk)
                if DB <= 5:
                    continue
                rms_to_T(xres[:, n, :], "xres%d" % n, gf, gfk, h2T, "h2T%d" % n, 2 + 128 * n)
            if stage <= 2:
                for n in range(NT if DB >= 5 else 0):
                    S.dma("sp", ["xres%d" % n], ["dbg%d" % n], dbg[128 * n:128 * (n + 1), :], xres[:, n, :])
                return end()
            xt, xk = xinR.next()
            S.dma("sp", [], [xk], xt[0:16, :], xs_d[:, :])
            sample_tail(0, xt[0:16, :], xk, gf, gfk)
            barrier()
            phb.close()
            with contextlib.ExitStack() as ph2:
                ffn(0, ph2, h2T, final_out=False)
                barrier()
        if stage <= 3:
            for n in range(NT):
                S.dma("sp", ["xres%d" % n], ["dbg%d" % n], dbg[128 * n:128 * (n + 1), :], xres[:, n, :])
            return end()


        TWO_PI = 6.2831845

        def s5_setup(ph):
            P = {}
            prm = sbt(ph, "prm", [128, 48])
            sm = sbt(ph, "s5sm", [128, 16, 16])
            cosT = sbt(ph, "cosT", [128, 16, 128])
            sinT = sbt(ph, "sinT", [128, 16, 128])
            Bmat = sbt(ph, "Bmat", [128, 2, 16, 128], BF16)
            Cmat = sbt(ph, "Cmat", [128, 2, 16, 128], BF16)
            dbT = sbt(ph, "dbT", [128, 8])
            Dmat = sbt(ph, "Dmat", [128, 4, 128], BF16)
            wglu = sbt(ph, "wglu", [128, 4, 512], BF16)
            tmp = contextlib.ExitStack()
            L3 = sbt(tmp, "L3", [48, 128])
            S.dma("sp", [], ["L3a"], L3[0:16, :], lam_re_d[:, :])
            S.dma("sp", [], ["L3b"], L3[16:32, :], lam_im_d[:, :])
            Lt = sbt(tmp, "Lt", [48, 2])
            S.dma("sp", [], ["Lt"], Lt[32:48, :], log_dt_d[:, :])
            S.op("act", ["Lt"], ["L3c"], lambda e: e.activation(
                out=L3[32:48, :].rearrange("p (e q) -> p e q", e=2),
                in_=Lt[32:48, :].unsqueeze(2).to_broadcast([16, 2, 64]), func=AF.Copy))
            S.op("pe", ["L3a", "L3b", "L3c", "identf"], ["PS6"],
                 lambda e: e.transpose(PS[:, 6, 0:48], L3[0:48, :], identf[0:48, 0:48]))
            S.op("dve", ["PS6"], ["prm"], lambda e: e.tensor_copy(prm[:], PS[:, 6, 0:48]))
            are, aim, ldt = prm[:, 0:16], prm[:, 16:32], prm[:, 32:48]
            dt, ard, th, mag, yv, lbr, lbi, xr, den, fre, fim, ta, tb = [sm[:, i, :] for i in range(13)]
            S.op("act", ["prm"], ["dt"], lambda e: e.activation(out=dt, in_=ldt, func=AF.Exp))

            def c1(e):
                e.tensor_tensor(out=ard, in0=are, in1=dt, op=ALU.mult)
                e.tensor_tensor(out=th, in0=aim, in1=dt, op=ALU.mult)
                return e.tensor_scalar(out=yv, in0=th, scalar1=1.0 / (2 * math.pi), scalar2=None, op0=ALU.mult)
            S.op("dve", ["prm", "dt"], ["c1"], c1, chain=True)
            S.op("act", ["c1"], ["mag"], lambda e: e.activation(out=mag, in_=ard, func=AF.Exp))
            kki = sbt(tmp, "kki", [128, 128], mybir.dt.int32)
            kk = sbt(tmp, "kk", [128, 128])
            ang = sbt(tmp, "ang", [128, 16, 128])
            ki = sbt(tmp, "ki", [128, 16, 128], mybir.dt.int32)
            kf = sbt(tmp, "kf", [128, 16, 128])
            S.op("pool", [], ["kki"], lambda e: e.iota(kki[:], pattern=[[1, 128]], base=1, channel_multiplier=0))
            S.op("dve", ["kki"], ["kk"], lambda e: e.tensor_copy(kk[:], kki[:]))

            def c2(e):
                e.tensor_tensor(out=ang[:], in0=yv.unsqueeze(2).to_broadcast([128, 16, 128]),
                                in1=kk[:].unsqueeze(1).to_broadcast([128, 16, 128]), op=ALU.mult)
                e.tensor_copy(ki[:], ang[:])
                e.tensor_copy(kf[:], ki[:])
                return e.tensor_tensor(out=ang[:], in0=ang[:], in1=kf[:], op=ALU.subtract)
            S.op("dve", ["c1", "kk"], ["ang"], c2, chain=True)
            S.op("act", ["ang"], ["sinT"], lambda e: e.activation(out=sinT[:], in_=ang[:], func=AF.Sin, scale=TWO_PI))

            def c3(e):
                e.tensor_scalar(out=ang[:], in0=ang[:], scalar1=0.25, scalar2=None, op0=ALU.add)
                e.tensor_scalar(out=kf[:], in0=ang[:], scalar1=0.5, scalar2=None, op0=ALU.is_gt)
                return e.tensor_tensor(out=ang[:], in0=ang[:], in1=kf[:], op=ALU.subtract)
            S.op("dve", ["ang", "sinT"], ["ang2"], c3, chain=True)
            S.op("act", ["ang2"], ["cosT"], lambda e: e.activation(out=cosT[:], in_=ang[:], func=AF.Sin, scale=TWO_PI))
            barrier()
            tmp.close()
            tmp = contextlib.ExitStack()
            braw = sbt(tmp, "braw", [128, 2, 16, 16])
            bb = sbt(tmp, "bb", [128, 2, 16, 16])
            tbb = sbt(tmp, "tbb", [128, 2, 16, 16])
            S.dma("sp", [], ["braw0"], braw[:, 0], b_re_d.rearrange("(j q) c -> q j c", q=128))
            S.dma("sp", [], ["braw1"], braw[:, 1], b_im_d.rearrange("(j q) c -> q j c", q=128))

            def c4(e):
                e.tensor_tensor(out=lbr, in0=mag, in1=cosT[:, :, 0], op=ALU.mult)
                e.tensor_tensor(out=lbi, in0=mag, in1=sinT[:, :, 0], op=ALU.mult)
                e.tensor_scalar(out=xr, in0=lbr, scalar1=-1.0, scalar2=None, op0=ALU.add)
                e.tensor_tensor(out=den, in0=are, in1=are, op=ALU.mult)
                e.tensor_tensor(out=ta, in0=aim, in1=aim, op=ALU.mult)
                e.tensor_tensor(out=den, in0=den, in1=ta, op=ALU.add)
                e.reciprocal(den, den)
                e.tensor_tensor(out=fre, in0=xr, in1=are, op=ALU.mult)
                e.tensor_tensor(out=ta, in0=lbi, in1=aim, op=ALU.mult)
                e.tensor_tensor(out=fre, in0=fre, in1=ta, op=ALU.add)
                e.tensor_tensor(out=fre, in0=fre, in1=den, op=ALU.mult)
                e.tensor_tensor(out=fim, in0=lbi, in1=are, op=ALU.mult)
                e.tensor_tensor(out=ta, in0=xr, in1=aim, op=ALU.mult)
                e.tensor_tensor(out=fim, in0=fim, in1=ta, op=ALU.subtract)
                e.tensor_tensor(out=fim, in0=fim, in1=den, op=ALU.mult)
                frb = fre.unsqueeze(2).to_broadcast([128, 16, 16])
                fib = fim.unsqueeze(2).to_broadcast([128, 16, 16])
                e.tensor_tensor(out=bb[:, 0], in0=braw[:, 0], in1=frb, op=ALU.mult)
                e.tensor_tensor(out=tbb[:, 0], in0=braw[:, 1], in1=fib, op=ALU.mult)
                e.tensor_tensor(out=bb[:, 0], in0=bb[:, 0], in1=tbb[:, 0], op=ALU.subtract)
                e.tensor_tensor(out=bb[:, 1], in0=braw[:, 1], in1=frb, op=ALU.mult)
                e.tensor_tensor(out=tbb[:, 1], in0=braw[:, 0], in1=fib, op=ALU.mult)
                return e.tensor_tensor(out=bb[:, 1], in0=bb[:, 1], in1=tbb[:, 1], op=ALU.add)
            S.op("dve", ["mag", "cosT", "sinT", "prm", "braw0", "braw1"], ["bb"], c4, chain=True)
            E = sbt(tmp, "Eexp", [128, 2, 16, 128])
            S.op("pool", [], ["E0"], lambda e: e.memset(E[:], 0.0))

            def c5(e):
                for arr in range(2):
                    for ee in range(2):
                        dst = _ap(E[64 * ee:64 * ee + 64], arr * 2048 + ee * 16, [[512, 4], [160, 4], [1, 16]])
                        src = _ap(bb[64 * ee:64 * ee + 64], arr * 256, [[64, 4], [16, 4], [1, 16]])
                        last = e.tensor_copy(dst, src)
                return last
            S.op("dve", ["bb", "E0"], ["E"], c5)
            for arr in range(2):
                for jq in range(4):
                    bank = 6 + (arr * 4 + jq) % 2
                    pb = PS[:, bank, :].rearrange("p (x q) -> p x q", q=128)

                    def trE(e, arr=arr, jq=jq, pb=pb):
                        for x in range(4):
                            last = e.transpose(pb[:, x, :], E[:, arr, 4 * jq + x, :], identf[:])
                        return last
                    S.op("pe", ["E", "identf"], ["PS%d" % bank], trE)
                    S.op("act", ["PS%d" % bank], ["Bmat"], lambda e, arr=arr, jq=jq, pb=pb: e.activation(
                        out=Bmat[:, arr, 4 * jq:4 * jq + 4, :], in_=pb, func=AF.Copy))
            S.op("pool", [], ["C0"], lambda e: e.memset(Cmat[:], 0.0))
            XR = Rot(nc, tmp, "Xc", [128, 128], F32, 2)
            for arr, cd in enumerate((c_re_d, c_im_d)):
                for a in range(4):
                    X, Xk = XR.next()
                    S.dma("sp", [], [Xk + "a"], X[:, 0:64], cd[128 * a:128 * (a + 1), :])
                    S.dma("sp", [], [Xk + "b"], X[:, 64:128], cd[128 * a:128 * (a + 1), :])
                    bank = 6 + (arr * 4 + a) % 2
                    S.op("pe", [Xk + "a", Xk + "b", "identf"], ["PS%d" % bank],
                         lambda e, X=X, bank=bank: e.transpose(PS[:, bank, 0:128], X[:], identf[:]))
                    for ee in range(2):
                        dst = _ap(Cmat[64 * ee:64 * ee + 64], arr * 2048 + 4 * a * 128 + ee * 16, [[160, 4], [1, 16]])
                        src = _ap(PS[64 * ee:64 * ee + 64, bank, :], ee * 16, [[32, 4], [1, 16]])
                        S.op("act", ["PS%d" % bank, "C0"], ["Cmat"], lambda e, dst=dst, src=src, arr=arr: e.activation(
                            out=dst, in_=src, func=AF.Identity, scale=(1.0 if arr == 0 else -1.0)))
            DB8 = sbt(tmp, "DB8", [8, 128])
            S.dma("sp", [], ["DB8a"], DB8[0:4, :], d_skip_d[:, :])
            S.dma("sp", [], ["DB8b"], DB8[4:8, :], b_glu_d[:, :])
            S.op("pe", ["DB8a", "DB8b", "identf"], ["PS6"],
                 lambda e: e.transpose(PS[:, 6, 0:8], DB8[0:8, :], identf[0:8, 0:8]))
            S.op("dve", ["PS6"], ["dbT"], lambda e: e.tensor_copy(dbT[:], PS[:, 6, 0:8]))

            def c6(e):
                for c in range(4):
                    last = e.tensor_scalar(out=Dmat[:, c, :], in0=identf[:], scalar1=dbT[:, c:c + 1], scalar2=None,
                                           op0=ALU.mult)
                return last
            S.op("dve", ["dbT", "identf"], ["Dmat"], c6)
            S.dma("pool", [], ["wglu"], wglu[:], w_glu.rearrange("(k p) n -> p k n", p=128))
            barrier()
            tmp.close()
            P.update(mag=mag, cosT=cosT, sinT=sinT, Bmat=Bmat, Cmat=Cmat, Dmat=Dmat, dbT=dbT, wglu=wglu, sm=sm)
            return P

        def s5_pass(P, uT, carry, final, wk):
            mag, cosT, sinT = P["mag"], P["cosT"], P["sinT"]
            tt, zre, zim, Sre, Sim, yg, ygb, sg, cl = wk
            wre, wim = tt[:, 0], tt[:, 2]
            for n in range(NT):
                for hh in range(2):
                    j0 = 8 * hh
                    Bre = PS[:, 0:2, :].rearrange("p b (x q) -> p (b x) q", q=128)
                    Bim = PS[:, 2:4, :].rearrange("p b (x q) -> p (b x) q", q=128)

                    def bu(e):
                        for arr, dstp in ((0, Bre), (1, Bim)):
                            for jj in range(8):
                                j = j0 + jj
                                last = e.matmul(dstp[:, jj, :], P["Bmat"][:, arr, j, :], uT[:, j // 4, 128 * n:128 * (n + 1)],
                                                start=True, stop=True)
                        return last
                    S.op("pe", ["uT%d" % n, "Bmat"], ["PS0", "PS1", "PS2", "PS3"], bu)
                    cs_ = cosT[:, j0:j0 + 8, :]
                    sn_ = sinT[:, j0:j0 + 8, :]

                    def rot_in(e):
                        e.tensor_tensor(out=tt[:, 0], in0=Bre, in1=cs_, op=ALU.mult)
                        e.tensor_tensor(out=tt[:, 1], in0=Bim, in1=sn_, op=ALU.mult)
                        e.tensor_tensor(out=tt[:, 2], in0=Bim, in1=cs_, op=ALU.mult)
                        return e.tensor_tensor(out=tt[:, 3], in0=Bre, in1=sn_, op=ALU.mult)
                    S.op("dve", ["PS0", "PS1", "PS2", "PS3", "cosT", "sinT"], ["tt", "tt2"], rot_in)

                    def rot_in2(e):
                        e.tensor_tensor(out=wre, in0=tt[:, 0], in1=tt[:, 1], op=ALU.add)
                        return e.tensor_tensor(out=wim, in0=tt[:, 2], in1=tt[:, 3], op=ALU.subtract)
                    S.op("pool", ["tt", "tt2"], ["tt", "tt2"], rot_in2)

                    def scans(e):
                        for jj in range(8):
                            j = j0 + jj
                            rho = mag[:, j:j + 1].to_broadcast([128, 128])
                            e.tensor_tensor_scan(out=zre[:, jj, :], data0=rho, data1=wre[:, jj, :],
                                                 initial=carry[:, 0, j:j + 1], op0=ALU.mult, op1=ALU.add)
                            last = e.tensor_tensor_scan(out=zim[:, jj, :], data0=rho, data1=wim[:, jj, :],
                                                        initial=carry[:, 1, j:j + 1], op0=ALU.mult, op1=ALU.add)
                        return last
                    S.op("dve", ["tt", "tt2", "carry", "mag"], ["z"], scans)

                    cL = cosT[:, j0:j0 + 8, 127]
                    sL = sinT[:, j0:j0 + 8, 127]
                    zr = zre[:, :, 127]
                    zi = zim[:, :, 127]

                    def carry_a(e):
                        e.tensor_tensor(out=cl[:, 0, :], in0=cL, in1=zr, op=ALU.mult)
                        e.tensor_tensor(out=cl[:, 1, :], in0=sL, in1=zi, op=ALU.mult)
                        e.tensor_tensor(out=cl[:, 2, :], in0=cL, in1=zi, op=ALU.mult)
                        return e.tensor_tensor(out=cl[:, 3, :], in0=sL, in1=zr, op=ALU.mult)
                    S.op("dve", ["z", "cosT", "sinT"], ["cl"], carry_a)

                    def carry_b(e):
                        e.tensor_tensor(out=carry[:, 0, j0:j0 + 8], in0=cl[:, 0, :], in1=cl[:, 1, :], op=ALU.subtract)
                        return e.tensor_tensor(out=carry[:, 1, j0:j0 + 8], in0=cl[:, 2, :], in1=cl[:, 3, :], op=ALU.add)
                    S.op("dve", ["cl"], ["carry"], carry_b)
                    if final:
                        def rot_out(e):
                            e.tensor_tensor(out=tt[:, 0], in0=zre[:], in1=cs_, op=ALU.mult)
                            return e.tensor_tensor(out=tt[:, 1], in0=zim[:], in1=sn_, op=ALU.mult)
                        S.op("dve", ["z", "cosT", "sinT"], ["tt"], rot_out)

                        def rot_out_b(e):
                            e.tensor_tensor(out=tt[:, 2], in0=zim[:], in1=cs_, op=ALU.mult)
                            return e.tensor_tensor(out=tt[:, 3], in0=zre[:], in1=sn_, op=ALU.mult)
                        S.op("pool", ["z", "cosT", "sinT"], ["tt2"], rot_out_b)

                        def rot_out_c(e):
                            e.tensor_tensor(out=Sim[:, j0:j0 + 8, :], in0=tt[:, 2], in1=tt[:, 3], op=ALU.add)
                            return e.tensor_tensor(out=Sre[:, j0:j0 + 8, :], in0=tt[:, 0], in1=tt[:, 1], op=ALU.subtract)
                        S.op("pool", ["tt", "tt2"], ["S%d" % hh], rot_out_c)
                if not final:
                    continue
                Y = PS[:, 4, :].rearrange("p (c q) -> p c q", q=128)

                def ymm(e):
                    for c in range(4):
                        for x in range(4):
                            j = 4 * c + x
                            e.matmul(Y[:, c, :], P["Cmat"][:, 0, j, :], Sre[:, j, :], start=(x == 0), stop=False)
                            e.matmul(Y[:, c, :], P["Cmat"][:, 1, j, :], Sim[:, j, :], start=False, stop=False)
                        last = e.matmul(Y[:, c, :], P["Dmat"][:, c, :], uT[:, c, 128 * n:128 * (n + 1)], start=False, stop=True)
                    return last
                S.op("pe", ["S0", "S1", "Cmat", "Dmat", "uT%d" % n], ["PS4"], ymm)
                S.op("act", ["PS4"], ["yg"], lambda e: e.activation(out=yg[:], in_=Y, func=AF.Gelu_apprx_tanh))
                S.op("pool", ["yg"], ["ygb"], lambda e: e.tensor_copy(ygb[:], yg[:]))
                Z = PS[:, 5, :].rearrange("p (c q) -> p c q", q=128)

                def zmm(e):
                    for c2 in range(4):
                        for c in range(4):
                            last = e.matmul(Z[:, c2, :], P["wglu"][:, c, 128 * c2:128 * (c2 + 1)], ygb[:, c, :],
                                            start=(c == 0), stop=(c == 3))
                    return last
                S.op("pe", ["ygb", "wglu"], ["PS5"], zmm)

                def sig(e):
                    for c2 in range(4):
                        last = e.activation(out=sg[:, c2, :], in_=Z[:, c2, :], func=AF.Sigmoid,
                                            bias=P["dbT"][:, 4 + c2:5 + c2])
                    return last
                S.op("act", ["PS5", "dbT"], ["sg"], sig)
                S.op("pool", ["yg", "sg"], ["uT%d" % n], lambda e, n=n: e.tensor_tensor(
                    out=uT[:, :, 128 * n:128 * (n + 1)], in0=yg[:], in1=sg[:], op=ALU.mult))


        def s5_sample(P, ph):
            sm = P["sm"]
            lbr, lbi = sm[:, 5, :], sm[:, 6, :]
            stin = sbt(ph, "stin", [128, 128])
            st0 = sbt(ph, "st0", [128, 4, 2, 16])
            bus = sbt(ph, "bus", [128, 2, 16, 16])
            ssb = sbt(ph, "ssb", [128, 2, 16, 16], BF16)
            cur = sbt(ph, "s5cur", [128, 2, 4, 16])
            tq = sbt(ph, "s5tq", [128, 4, 4, 16])
            sto = sbt(ph, "s5sto", [128, 128])
            ygs = sbt(ph, "ygs", [128, 4, 16])
            ygsb = sbt(ph, "ygsb", [128, 4, 16], BF16)
            sgs = sbt(ph, "sgs", [128, 4, 16])
            S.dma("sp", [], ["stin"], stin[:], st5_d[:, :])
            S.op("pe", ["stin", "identf"], ["PS6"], lambda e: e.transpose(PS[:, 6, 0:128], stin[:], identf[:]))
            S.op("dve", ["PS6"], ["st0"], lambda e: e.tensor_copy(st0[:].rearrange("p s a j -> p (s a j)"), PS[:, 6, 0:128]))
            bup = PS[:, 0, :].rearrange("p (a j t) -> p a j t", a=2, j=16)

            def bu(e):
                for arr in range(2):
                    for j in range(16):
                        last = e.matmul(bup[:, arr, j, :], P["Bmat"][:, arr, j, :], uTs[:, j // 4, :], start=True, stop=True)
                return last
            S.op("pe", ["uTs", "Bmat"], ["PS0"], bu)
            S.op("dve", ["PS0"], ["bus"], lambda e: e.tensor_copy(bus[:], bup))
            S.op("dve", ["st0"], ["s5cur"], lambda e: e.tensor_copy(cur[:], st0[:].rearrange("p s a j -> p a s j")))
            lrb = lbr.unsqueeze(1).to_broadcast([128, 4, 16])
            lib = lbi.unsqueeze(1).to_broadcast([128, 4, 16])

            def rec(e):
                for t in range(4):
                    bre = bus[:, 0].rearrange("p j (s t) -> p s j t", t=4)[:, :, :, t]
                    bim = bus[:, 1].rearrange("p j (s t) -> p s j t", t=4)[:, :, :, t]
                    e.tensor_tensor(out=tq[:, 0], in0=cur[:, 0], in1=lrb, op=ALU.mult)
                    e.tensor_tensor(out=tq[:, 1], in0=cur[:, 1], in1=lib, op=ALU.mult)
                    e.tensor_tensor(out=tq[:, 2], in0=cur[:, 1], in1=lrb, op=ALU.mult)
                    e.tensor_tensor(out=tq[:, 3], in0=cur[:, 0], in1=lib, op=ALU.mult)
                    e.tensor_tensor(out=tq[:, 0], in0=tq[:, 0], in1=tq[:, 1], op=ALU.subtract)
                    e.tensor_tensor(out=tq[:, 2], in0=tq[:, 2], in1=tq[:, 3], op=ALU.add)
                    e.tensor_tensor(out=cur[:, 0], in0=tq[:, 0], in1=bre, op=ALU.add)
                    e.tensor_tensor(out=cur[:, 1], in0=tq[:, 2], in1=bim, op=ALU.add)
                    e.tensor_copy(ssb[:, 0].rearrange("p j (s t) -> p s j t", t=4)[:, :, :, t], cur[:, 0])
                    e.tensor_copy(ssb[:, 1].rearrange("p j (s t) -> p s j t", t=4)[:, :, :, t], cur[:, 1])
            S.op("dve", ["bus", "s5cur", "s5sm"], ["s5cur", "ssb"], rec, chain=True)
            S.op("dve", ["s5cur"], ["s5sto"], lambda e: e.tensor_copy(
                sto[:].rearrange("p (s a j) -> p s a j", s=4, a=2), cur[:].rearrange("p a s j -> p s a j")))
            S.op("pe", ["s5sto", "identf"], ["PS6"], lambda e: e.transpose(PS[:, 6, 0:128], sto[:], identf[:]))
            S.op("dve", ["PS6"], ["stin"], lambda e: e.tensor_copy(stin[:], PS[:, 6, 0:128]))
            S.dma("sp", ["stin"], ["ss5o"], ss5_o[:, :], stin[:])
            Y = PS[:, 4, 0:64].rearrange("p (c t) -> p c t", c=4)

            def ymm(e):
                for c in range(4):
                    for x in range(4):
                        j = 4 * c + x
                        e.matmul(Y[:, c, :], P["Cmat"][:, 0, j, :], ssb[:, 0, j, :], start=(x == 0), stop=False)
                        e.matmul(Y[:, c, :], P["Cmat"][:, 1, j, :], ssb[:, 1, j, :], start=False, stop=False)
                    last = e.matmul(Y[:, c, :], P["Dmat"][:, c, :], uTs[:, c, :], start=False, stop=True)
                return last
            S.op("pe", ["ssb", "Cmat", "Dmat", "uTs"], ["PS4"], ymm)
            S.op("act", ["PS4"], ["ygs"], lambda e: e.activation(out=ygs[:], in_=Y, func=AF.Gelu_apprx_tanh))
            S.op("pool", ["ygs"], ["ygsb"], lambda e: e.tensor_copy(ygsb[:], ygs[:]))
            Z = PS[:, 5, 0:64].rearrange("p (c t) -> p c t", c=4)

            def zmm(e):
                for c2 in range(4):
                    for c in range(4):
                        last = e.matmul(Z[:, c2, :], P["wglu"][:, c, 128 * c2:128 * (c2 + 1)], ygsb[:, c, :],
                                        start=(c == 0), stop=(c == 3))
                return last
            S.op("pe", ["ygsb", "wglu"], ["PS5"], zmm)

            def sig(e):
                for c2 in range(4):
                    last = e.activation(out=sgs[:, c2, :], in_=Z[:, c2, :], func=AF.Sigmoid, bias=P["dbT"][:, 4 + c2:5 + c2])
                return last
            S.op("act", ["PS5", "dbT"], ["sgs"], sig)
            S.op("pool", ["ygs", "sgs"], ["uTs"], lambda e: e.tensor_tensor(out=uTs[:], in0=ygs[:], in1=sgs[:], op=ALU.mult))

        with contextlib.ExitStack() as ph:
            uT = sbt(ph, "uT", [128, 4, T], BF16)
            with contextlib.ExitStack() as ph1:
                h2T = sbt(ph1, "hT1", [128, 8, 2 + T], BF16)
                mk(ph1, ["wch", "qf", "sq", "qn", "qb"], {"wch": 2})
                gb1, gb1k = load_gbc(g_mix[1:2, :])
                for n in range(NT):
                    rms_to_T(xres[:, n, :], "xres%d" % n, gb1, gb1k, h2T, "h1T%d" % n, 2 + 128 * n)
                wu, wuk = load_w(w_in_b[:, 0:512])
                wqc, wqck = load_w(w_in_b[:, 512:1024])
                ub = 0
                for c in range(4):
                    for blk in range(4):
                        bank = ub % 2
                        ub += 1

                        def umm(e, c=c, blk=blk, bank=bank):
                            for k in range(8):
                                last = e.matmul(PS[:, bank, :], wu[:, k, 128 * c:128 * (c + 1)],
                                                h2T[:, k, 2 + 512 * blk:2 + 512 * (blk + 1)], start=(k == 0), stop=(k == 7))
                            return last
                        S.op("pe", [wuk] + ["h1T%d" % (4 * blk + x) for x in range(4)], ["PS%d" % bank], umm)
                        S.op("act", ["PS%d" % bank], ["uT%d" % (4 * blk + x) for x in range(4)],
                             lambda e, c=c, blk=blk, bank=bank: e.activation(
                                 out=uT[:, c, 512 * blk:512 * (blk + 1)], in_=PS[:, bank, :], func=AF.Copy))
                for n in range(NT):
                    ps, psk = proj_tile(wqc, wqck, h2T, "h1T%d" % n, 2 + 128 * n)
                    qn, qnk = qk_post(ps, psk, gqc_bc, "gqc_bc", 1, 0, rope=False)
                    qb, qbk = to_bf16(qn, qnk)
                    S.dma("sp", [qbk], ["QC_%d" % n], QC[128 * n:128 * (n + 1), :], qb[:])
                rms_to_T(xsres[0:16, :], "xsres", gb1, gb1k, hTs, "hTs", 0, m=16)
                usp = PS[:, 0, 0:64].rearrange("p (c t) -> p c t", c=4)

                def umms(e):
                    for c in range(4):
                        for k in range(8):
                            last = e.matmul(usp[:, c, :], wu[:, k, 128 * c:128 * (c + 1)], hTs[:, k, :],
                                            start=(k == 0), stop=(k == 7))
                    return last
                S.op("pe", [wuk, "hTs"], ["PS0"], umms)
                S.op("act", ["PS0"], ["uTs"], lambda e: e.activation(out=uTs[:], in_=usp, func=AF.Copy))
                ps, psk = proj_tile(wqc, wqck, hTs, "hTs", 0, m=16)
                qn, qnk = qk_post(ps, psk, gqc_bc, "gqc_bc", 1, 0, rope=False, m=16)
                S.op("pool", [qnk], ["sqc1"], lambda e, qn=qn: e.tensor_copy(sqc[0:16, 1, :], qn[0:16].rearrange("p h d -> p (h d)")))
                barrier()
            with contextlib.ExitStack() as ph2:
                P5 = s5_setup(ph2)
                with contextlib.ExitStack() as phss:
                    s5_sample(P5, phss)
                    barrier()
                carry = sbt(ph2, "carry", [128, 2, 16])
                cin = sbt(ph2, "cin", [128, 32])
                wk = (sbt(ph2, "s5tt", [128, 4, 8, 128]),
                      sbt(ph2, "zre", [128, 8, 128]), sbt(ph2, "zim", [128, 8, 128]),
                      sbt(ph2, "Sre", [128, 16, 128], BF16), sbt(ph2, "Sim", [128, 16, 128], BF16),
                      sbt(ph2, "yg", [128, 4, 128]), sbt(ph2, "ygb", [128, 4, 128], BF16),
                      sbt(ph2, "sg", [128, 4, 128]), sbt(ph2, "cl", [128, 4, 8]))
                S.op("dve", [], ["carry"], lambda e: e.memset(carry[:], 0.0))
                if os.environ.get("DBG_S5PASS1", "1") == "1":
                    s5_pass(P5, uT, carry, False, wk)
                    S.dma("pool", ["carry"], ["ccs2"], cc_src[2][:, 0:32], carry[:].rearrange("p a j -> p (a j)"))
                    S.allgather_pairs(["ccs2"], ["ccd2"], cc_src[2][:, :], cc_dst[2][:, :])
                    S.dma("pool", ["ccd2"], ["cin"], cin[:], cc_dst[2][0:128, 0:32])
                    S.op("dve", ["cin", "flag"], ["carry"], lambda e: e.tensor_scalar(
                        out=carry[:].rearrange("p a j -> p (a j)"), in0=cin[:], scalar1=flag[:, 0:1], scalar2=None,
                        op0=ALU.mult))
                s5_pass(P5, uT, carry, True, wk)
                S.op("pe", ["carry", "identf"], ["PS6"], lambda e: e.transpose(
                    PS[0:32, 6, 0:128], carry[:].rearrange("p a j -> p (a j)"), identf[:]))
                so = sbt(ph2, "s5out", [32, 128])
                S.op("dve", ["PS6"], ["s5out"], lambda e: e.tensor_copy(so[:], PS[0:32, 6, 0:128]))
                S.dma("sp", ["s5out"], ["s5o"], s5_o.rearrange("a j q -> (a j) q"), so[:])
                barrier()
            if stage <= 4:
                return end()
            S.dma("sp", ["uT%d" % n for n in range(NT)], ["MIXall"], MIX[:, :], uT[:].rearrange("p c t -> p (c t)"))
            barrier()
            ph.close()
            mxR = Rot(nc, ph, "mx", [128, 4, 128], BF16, 2)
            h2T = sbt(ph, "h2T1", [128, 8, 2 + T], BF16)
            with contextlib.ExitStack() as ph3:
                mk(ph3, ["wch", "qf", "sq", "qn", "qb", "vf"], {"wch": 2})
                memory_kv(1, ph3)
                barrier()
            sample_attention(1, False)
            with contextlib.ExitStack() as phb:
                mk(phb, ["qu", "qT", "pT", "ou", "mg", "mgT", "rd"])
                load_wout(1, phb)
                gf1, gf1k = load_gbc(g_ffn[1:2, :])
                for n in range(NT):
                    mg, mgk = W["mg"].next()
                    cross_attention(n, mg, mgk)
                    mgT, mgTk = W["mgT"].next()
                    transposes([(mg[:, 512 + k * 128:512 + (k + 1) * 128], mgk + "_512") for k in range(4)],
                               mgT[:, 0:4, :], mgTk)
                    mx, mxk = mxR.next()
                    S.dma("sp", ["MIXall"], [mxk], mx[:], MIX.rearrange("p (c t) -> p c t", c=4)[:, :, 128 * n:128 * (n + 1)])
                    lhs = [(mx[:, k, :], mxk) for k in range(4)] + [(mgT[:, k, :], mgTk) for k in range(4)]
                    out_proj_residual(n, lhs, xres[:, n, :], "xres%d" % n)
                    rms_to_T(xres[:, n, :], "xres%d" % n, gf1, gf1k, h2T, "h2T%d" % n, 2 + 128 * n)
                sample_tail(1, xsres[0:16, :], "xsres", gf1, gf1k)
                barrier()
            with contextlib.ExitStack() as ph4:
                ffn(1, ph4, h2T, final_out=True)
                barrier()

        S.finish("sp")
    return nc


def _consts():
    ident = np.eye(128, dtype=np.float32)
    k = np.arange(128)[:, None]
    q = np.arange(128)[None, :]
    maskb = np.zeros((128, 2, 128), np.float32)
    maskb[:, 0, :] = np.where(k >= q, 0.0, NEGB)
    maskb[:, 1, :] = np.where(k <= q, 0.0, NEGB)
    return ident, maskb


def _rope_table(pos):
    half = 16
    inv = np.exp(-math.log(500000.0) * np.arange(half, dtype=np.float32) / half).astype(np.float32)
    ang = pos.astype(np.float32)[:, None] * inv[None, :]
    return np.concatenate([np.cos(ang), np.sin(ang)], axis=1).astype(np.float32)


def prep(inputs, stage=99):
    ident, maskb = _consts()
    f = lambda k: np.ascontiguousarray(np.asarray(inputs[k], np.float32))
    xp = f("x_prompt")
    shared = {
        "ident": ident, "maskb": maskb,
        "g_mix": f("g_mix"), "g_ffn": f("g_ffn"), "g_mem": f("g_mem"),
        "w_in_a": f("w_in_a")[0], "w_in_b": f("w_in_b")[0],
        "g_q_dil": f("g_q_dil")[0], "g_k_dil": f("g_k_dil")[0],
        "g_q_cross": f("g_q_cross"), "g_k_cross": f("g_k_cross"),
        "w_mem_kv": f("w_mem_kv"), "w_out": f("w_out"), "w_up": f("w_up"),
        "conv_w": f("conv_w"), "conv_b": f("conv_b"), "w_down": f("w_down"),
        "s5_lam_re": f("s5_lam_re").reshape(16, 128), "s5_lam_im": f("s5_lam_im").reshape(16, 128),
        "s5_log_dt": f("s5_log_dt").reshape(16, 2),
        "s5_b_re": f("s5_b_re").reshape(2048, 16), "s5_b_im": f("s5_b_im").reshape(2048, 16),
        "s5_c_re": f("s5_c_re").reshape(512, 64), "s5_c_im": f("s5_c_im").reshape(512, 64),
        "s5_d": f("s5_d").reshape(4, 128), "w_glu": f("w_glu")[0], "b_glu": f("b_glu").reshape(4, 128),
    }
    mem = f("mem_prompt")
    xs = f("x_sample")
    cw = [f("cache_win0_kv")[0], f("cache_win1_kv")[0], f("cache_win2_kv")[0]]
    cmem = f("cache_mem_kv")
    st5 = f("state_s5")[0]
    cst = f("state_ffn_conv")
    g0bias = np.zeros((128, 4), np.float32)
    for t in range(4):
        g0bias[:t, t] = NEGB * SCALE
    biasnew = np.full((16, 2, 16), NEGB * SCALE, np.float32)
    for kr in range(16):
        for qr in range(16):
            if kr // 4 == qr // 4:
                if kr % 4 <= qr % 4:
                    biasnew[kr, 0, qr] = 0.0
                if kr == qr:
                    biasnew[kr, 1, qr] = 0.0
    in_maps = []
    for c in range(8):
        b, hf = c // 2, c % 2
        xkv = np.zeros((TH, D), np.float32)
        if hf == 0:
            xkv[T:] = xp[b, 0:T]
        else:
            xkv[:] = xp[b]
        pos = np.concatenate([np.arange(TH) + (hf * T - T), PAST + (np.arange(128) % 4)])
        m = dict(shared)
        sl = slice(4 * c, 4 * c + 4)
        m.update({
            "xkv": xkv,
            "flag": np.full((128, 1), float(hf), np.float32),
            "ropecs": _rope_table(pos),
            "mem": mem[b],
            "xs": np.ascontiguousarray(xs[sl].reshape(16, D)),
            "cwin0": np.ascontiguousarray(cw[0][sl]), "cwin1": np.ascontiguousarray(cw[1][sl]),
            "cwin2": np.ascontiguousarray(cw[2][sl]),
            "cmem": np.ascontiguousarray(cmem[:, sl]),
            "st5": np.ascontiguousarray(st5[sl].reshape(128, 128)),
            "cst": np.ascontiguousarray(cst[:, sl].reshape(2, 8, 2 * DFF)),
            "g0bias": g0bias, "biasnew": biasnew,
        })
        in_maps.append(m)
    return in_maps


_NC_CACHE = {}


def kernel(**inputs):
    if "nc" not in _NC_CACHE:
        _NC_CACHE["nc"] = build(99)
    nc = _NC_CACHE["nc"]
    in_maps = prep(inputs)
    res = run_bass_kernel_spmd(nc, in_maps, core_ids=list(range(8)))
    R = res.results
    f32 = np.float32
    y_prompt = np.stack([np.concatenate([R[2 * b]["y_p"], R[2 * b + 1]["y_p"]], 0) for b in range(4)]).astype(f32)
    y_sample = np.concatenate([R[c]["y_s"].reshape(4, 4, D) for c in range(8)], 0).astype(f32)
    p_win = [np.stack([R[2 * b + 1]["win%d" % g] for b in range(4)])[None].astype(f32) for g in range(3)]
    p_mem = np.stack([R[2 * b]["memkv"] for b in range(4)], 1).astype(f32)
    p_s5 = np.stack([R[2 * b + 1]["s5o"].reshape(2, 32, 64) for b in range(4)])[None].astype(f32)
    p_conv = np.stack([R[2 * b + 1]["convo"].reshape(2, 2, 2 * DFF) for b in range(4)], 1).astype(f32)
    s_win = [np.concatenate([R[c]["swin%d" % g] for c in range(8)], 0)[None].astype(f32) for g in range(3)]
    s_s5 = np.concatenate([R[c]["ss5"].reshape(4, 2, 32, 64) for c in range(8)], 0)[None].astype(f32)
    s_conv = np.concatenate([R[c]["sconv"].reshape(2, 4, 2, 2 * DFF) for c in range(8)], 1).astype(f32)
    return (y_prompt, y_sample, p_win[0], p_win[1], p_win[2], p_mem, p_s5, p_conv,
            s_win[0], s_win[1], s_win[2], s_s5, s_conv)
```

```python
import contextlib
import math
import os

import numpy as np
import concourse.bass as bass
import concourse.mybir as mybir
from concourse.bass_utils import run_bass_kernel_spmd

F32 = mybir.dt.float32
BF16 = mybir.dt.bfloat16
AF = mybir.ActivationFunctionType
ALU = mybir.AluOpType
AX = mybir.AxisListType

D = 1024
T = 2048
NT = 16
TH = 4096
NTH = 32
DFF = 2816
NFF = 22
SCALE = 128 ** -0.5
EPS = 1e-6
GROUPS = ((128, 1), (512, 4), (2048, 16))
PAST = 16384
NEGB = -30000.0


class Sched:
    def __init__(self, nc, stack, n_dma_sems=32):
        self.nc = nc
        self.engs = {"pe": nc.tensor, "act": nc.scalar, "dve": nc.vector,
                     "pool": nc.gpsimd, "sp": nc.sync}
        self.sem, self.cnt = {}, {}
        for k in self.engs:
            self.sem[k] = stack.enter_context(nc.semaphore("s_" + k))
            self.cnt[k] = 0
        self.dsem = [stack.enter_context(nc.semaphore("d_%d" % i)) for i in range(n_dma_sems)]
        self.dcnt = [0] * n_dma_sems
        self.dnext = 0
        self.dnext_sw = 0
        self.ccsem = stack.enter_context(nc.semaphore("s_cc"))
        self.cccnt = 0
        self.waited = {k: {} for k in self.engs}
        self.last_w = {}
        self.readers = {}

    def _semobj(self, key):
        if key == "cc":
            return self.ccsem
        return self.sem[key] if isinstance(key, str) else self.dsem[key]

    def _wait(self, engname, key, val):
        w = self.waited[engname]
        if w.get(key, 0) >= val:
            return
        self.engs[engname].wait_ge(self._semobj(key), val)
        w[key] = val

    def _deps(self, reads, writes):
        deps = {}

        def add(d, raw):
            if d is None:
                return
            k, v = d
            o = deps.get(k)
            if o is None or o[0] < v:
                deps[k] = (v, raw or (o[1] if o else False))
            elif raw:
                deps[k] = (o[0], True)

        for b in reads:
            add(self.last_w.get(b), True)
        for b in writes:
            add(self.last_w.get(b), False)
            for k, v in self.readers.get(b, {}).items():
                add((k, v), False)
        return deps

    def _commit(self, reads, writes, key, val):
        for b in reads:
            self.readers.setdefault(b, {})[key] = val
        for b in writes:
            self.last_w[b] = (key, val)
            self.readers[b] = {}

    def op(self, engname, reads, writes, fn, chain=False, lag=1):
        deps = self._deps(reads, writes)
        for k, (v, raw) in deps.items():
            if k == engname and (engname == "pe" or (not raw and engname != "pool")):
                continue
            self._wait(engname, k, v)
        if chain:
            fn(_Chain(self, engname, lag))
        else:
            last = fn(self.engs[engname])
            self.cnt[engname] += 1
            last.then_inc(self.sem[engname], 1)
        self._commit(reads, writes, engname, self.cnt[engname])

    def dma(self, issuer, reads, writes, out, in_, **kw):
        deps = self._deps(reads, writes)
        nh = len(self.dsem) - 8
        if issuer == "pool":
            i = nh + self.dnext_sw
            self.dnext_sw = (self.dnext_sw + 1) % 8
        else:
            i = self.dnext
            self.dnext = (self.dnext + 1) % nh
        if self.dcnt[i] > 0:
            o = deps.get(i)
            if o is None or o[0] < self.dcnt[i]:
                deps[i] = (self.dcnt[i], True)
        for k, (v, raw) in deps.items():
            self._wait(issuer, k, v)
        self.dcnt[i] += 16
        self.engs[issuer].dma_start(out=out, in_=in_, **kw).then_inc(self.dsem[i], 16)
        self._commit(reads, writes, i, self.dcnt[i])

    def allgather_pairs(self, reads, writes, src, dst):
        deps = self._deps(reads, writes)
        for k, (v, raw) in deps.items():
            self._wait("pool", k, v)
        self.nc.gpsimd.collective_compute(
            "AllGather", ALU.bypass, replica_groups=[[2 * i, 2 * i + 1] for i in range(int(os.environ.get("DBG_NCORES", "8")) // 2)],
            ins=[src], outs=[dst]).then_inc(self.ccsem)
        self.cccnt += 1
        self._commit(reads, writes, "cc", self.cccnt)

    def finish(self, engname="sp"):
        for i in range(len(self.dsem)):
            if self.dcnt[i] > 0:
                self._wait(engname, i, self.dcnt[i])
        for k in self.engs:
            if k != engname and self.cnt[k] > 0:
                self._wait(engname, k, self.cnt[k])


class _Chain:
    def __init__(self, sched, engname, lag=1):
        self.s, self.n, self.lag, self.k, self.base = sched, engname, lag, 0, None

    def __getattr__(self, name):
        real = getattr(self.s.engs[self.n], name)
        s, n = self.s, self.n

        def call(*a, **kw):
            if self.base is None:
                self.base = s.cnt[n]
            if self.k >= self.lag:
                s._wait(n, n, self.base + self.k - self.lag + 1)
            self.k += 1
            ins = real(*a, **kw)
            s.cnt[n] += 1
            ins.then_inc(s.sem[n], 1)
            return ins
        return call


class Rot:
    def __init__(self, nc, stack, name, shape, dt, n):
        self.t = [stack.enter_context(nc.sbuf_tensor("%s_%d" % (name, i), list(shape), dt)) for i in range(n)]
        self.k = ["%s_%d" % (name, i) for i in range(n)]
        self.i = 0

    def next(self):
        j = self.i
        self.i = (self.i + 1) % len(self.t)
        return self.t[j], self.k[j]


def _ap(base, extra_off, dims):
    return bass.AP(base.tensor, base.offset + extra_off, [list(base.ap[0])] + [list(d) for d in dims])


def build(stage=99):
    nc = bass.Bass("TRN2", target_bir_lowering=False)

    def din(name, shape, dt=F32):
        return nc.dram_tensor(name, list(shape), dt, kind="ExternalInput").ap()

    def dout(name, shape, dt=F32):
        return nc.dram_tensor(name, list(shape), dt, kind="ExternalOutput").ap()

    def dscr(name, shape, dt):
        return nc.dram_tensor(name, list(shape), dt, kind="Internal").ap()

    xkv = din("xkv", [TH, D])
    flag_d = din("flag", [128, 1])
    ident_d = din("ident", [128, 128])
    maskb_d = din("maskb", [128, 2, 128])
    ropecs_d = din("ropecs", [TH + 128, 32])
    mem_d = din("mem", [256, D])
    g_mix = din("g_mix", [2, D])
    g_ffn = din("g_ffn", [2, D])
    g_mem = din("g_mem", [2, D])
    w_in_a = din("w_in_a", [D, 5120])
    w_in_b = din("w_in_b", [D, 1024])
    g_q_dil = din("g_q_dil", [3, 128])
    g_k_dil = din("g_k_dil", [3, 128])
    g_q_cross = din("g_q_cross", [2, 128])
    g_k_cross = din("g_k_cross", [2, 128])
    w_mem_kv = din("w_mem_kv", [2, D, 1024])
    w_out = din("w_out", [2, D, D])
    w_up = din("w_up", [2, D, 2 * DFF])
    conv_w = din("conv_w", [2, 3, 2 * DFF])
    conv_b = din("conv_b", [2, 2 * DFF])
    w_down = din("w_down", [2, DFF, D])
    lam_re_d = din("s5_lam_re", [16, 128])
    lam_im_d = din("s5_lam_im", [16, 128])
    log_dt_d = din("s5_log_dt", [16, 2])
    b_re_d = din("s5_b_re", [2048, 16])
    b_im_d = din("s5_b_im", [2048, 16])
    c_re_d = din("s5_c_re", [512, 64])
    c_im_d = din("s5_c_im", [512, 64])
    d_skip_d = din("s5_d", [4, 128])
    w_glu = din("w_glu", [512, 512])
    b_glu_d = din("b_glu", [4, 128])

    xs_d = din("xs", [16, D])
    cwin_d = [din("cwin%d" % g, [4, GROUPS[g][0], 2, 4, 128]) for g in range(3)]
    cmem_d = din("cmem", [2, 4, 256, 2, 4, 128])
    st5_d = din("st5", [128, 128])
    cst_d = din("cst", [2, 8, 2 * DFF])
    g0bias_d = din("g0bias", [128, 4])
    biasnew_d = din("biasnew", [16, 2, 16])

    ys_o = dout("y_s", [16, D])
    swin_o = [dout("swin%d" % g, [4, GROUPS[g][0], 2, 4, 128]) for g in range(3)]
    ss5_o = dout("ss5", [128, 128])
    sconv_o = dout("sconv", [2, 352, 128])
    y_o = dout("y_p", [T, D])
    win_o = [dout("win%d" % g, [GROUPS[g][0], 2, 4, 128]) for g in range(3)]
    memkv_o = dout("memkv", [2, 256, 2, 4, 128])
    s5_o = dout("s5o", [2, 16, 128])
    conv_o = dout("convo", [2, 88, 128])
    dbg = dout("dbg", [T, D]) if stage < 50 else None

    Qs = [dscr("Qs%d" % g, [T, 512], BF16) for g in range(3)]
    Ks = [dscr("Ks%d" % g, [TH, 512], BF16) for g in range(3)]
    Vs = [dscr("Vs%d" % g, [TH, 520], BF16) for g in range(3)]
    NUM = [dscr("NUM%d" % g, [T, 516], F32) for g in range(3)]
    QC = dscr("QC", [T, 512], BF16)
    MIX = dscr("MIX", [128, 4 * T], BF16)
    cc_w = [16, 16, 32]
    cc_src = [dscr("cc_src%d" % i, [128, cc_w[i]], F32) for i in range(3)]
    cc_dst = [dscr("cc_dst%d" % i, [256, cc_w[i]], F32) for i in range(3)]

    with contextlib.ExitStack() as st:
        S = Sched(nc, st)

        def sbt(stack, name, shape, dt=F32):
            return stack.enter_context(nc.sbuf_tensor(name, list(shape), dt))

        def sb(name, shape, dt=F32):
            return sbt(st, name, shape, dt)

        PS = st.enter_context(nc.psum_tensor("PS", [128, 8, 512], F32))

        def barrier():
            for e in S.engs:
                for i in range(len(S.dsem)):
                    if S.dcnt[i] > 0:
                        S._wait(e, i, S.dcnt[i])
                if S.cccnt > 0:
                    S._wait(e, "cc", S.cccnt)
                for k in S.engs:
                    if k != e and S.cnt[k] > 0:
                        S._wait(e, k, S.cnt[k])

        def end(dump=None):
            S.finish("sp")
            return nc

        identf = sb("identf", [128, 128])
        identb = sb("identb", [128, 128], BF16)
        maskf = sb("maskf", [128, 2, 128])
        maskb = sb("maskb_sb", [128, 2, 128], BF16)
        flag = sb("flag_sb", [128, 1])
        ones1 = sb("ones1", [128, 1])
        epsT = sb("epsT", [128, 1])
        ropecs = sb("ropecs_sb", [128, NTH + 1, 32])
        gq_bc = sb("gq_bc", [128, 3, 128])
        gk_bc = sb("gk_bc", [128, 3, 128])
        gqc_bc = sb("gqc_bc", [128, 2, 128])
        gkc_bc = sb("gkc_bc", [128, 2, 128])
        S.dma("sp", [], ["identf"], identf[:], ident_d[:, :])
        S.dma("sp", [], ["maskf"], maskf[:], maskb_d[:, :, :])
        S.dma("sp", [], ["flag"], flag[:], flag_d[:, :])
        S.dma("sp", [], ["ropecs"], ropecs[:], ropecs_d.rearrange("(n p) c -> p n c", p=128))

        def bc_load(dst, key, src2d):
            S.dma("sp", [], [key], dst[:], src2d.rearrange("g d -> (g d)").unsqueeze(0).partition_broadcast(128))
        bc_load(gq_bc, "gq_bc", g_q_dil)
        bc_load(gk_bc, "gk_bc", g_k_dil)
        bc_load(gqc_bc, "gqc_bc", g_q_cross)
        bc_load(gkc_bc, "gkc_bc", g_k_cross)
        S.op("dve", ["identf"], ["identb"], lambda e: e.tensor_copy(identb[:], identf[:]))
        S.op("dve", ["maskf"], ["maskb"], lambda e: e.tensor_copy(maskb[:], maskf[:]))
        S.op("dve", [], ["ones1"], lambda e: e.memset(ones1[:], 1.0))
        S.op("dve", [], ["epsT"], lambda e: e.memset(epsT[:], EPS))

        gbcR = Rot(nc, st, "gbc", [128, D], F32, 2)

        def load_gbc(src_row):
            t, k = gbcR.next()
            S.dma("sp", [], [k], t[:], src_row.partition_broadcast(128))
            return t, k

        ssR = Rot(nc, st, "ss", [128, 4], F32, 4)
        sdR = Rot(nc, st, "sd", [128, 4], F32, 4)
        rsR = Rot(nc, st, "rs", [128, 4], F32, 4)
        hbR = Rot(nc, st, "hb", [128, D], BF16, 2)
        xinR = Rot(nc, st, "xin", [128, D], F32, 2)
        SPECS = {
            "wch": ([128, 8, 512], BF16, 4), "qf": ([128, 4, 128], F32, 2), "sq": ([128, 4, 128], F32, 2),
            "qn": ([128, 4, 128], F32, 2), "rt": ([128, 4, 4, 16], F32, 2), "qb": ([128, 512], BF16, 2),
            "vf": ([128, 512], F32, 2), "v1": ([128, 4, 130], BF16, 3), "qu": ([128, 512], BF16, 2),
            "qT": ([128, 4, 128], BF16, 2), "pT": ([128, 4, 2, 128], BF16, 2), "ou": ([128, 4, 129], F32, 2),
            "mg": ([128, D], BF16, 2), "mgT": ([128, 8, 128], BF16, 2), "numt": ([128, 3, 516], F32, 2),
            "rd": ([128, 8], F32, 2),
        }
        W = {}
        uid = [0]

        def mk(stack, names, counts=None):
            uid[0] += 1
            for nm in names:
                shp, dt, n = SPECS[nm]
                if counts and nm in counts:
                    n = counts[nm]
                W[nm] = Rot(nc, stack, "%s%d" % (nm, uid[0]), shp, dt, n)

        def rstd_of(ss, ssk, ncol, inv_n):
            sd, sdk = sdR.next()
            rs, rsk = rsR.next()
            S.op("act", [ssk, "epsT"], [sdk], lambda e: e.activation(
                out=sd[:, 0:ncol], in_=ss[:, 0:ncol], func=AF.Sqrt, bias=epsT[:, 0:1], scale=inv_n))
            S.op("dve", [sdk], [rsk], lambda e: e.reciprocal(rs[:, 0:ncol], sd[:, 0:ncol]))
            return rs, rsk

        tp_i = [0]

        def next_tp(banks=(2, 3)):
            b = banks[tp_i[0] % 2]
            tp_i[0] += 1
            return PS[:, b, :].bitcast(BF16).rearrange("p (k n) -> p k n", k=8), "PS%d" % b

        def transposes(srcs, dst_ap, dstk, m=128):
            tp3, tpk = next_tp()

            def tr(e):
                for i, (a, _) in enumerate(srcs):
                    last = e.transpose(tp3[:, i, 0:m], a, identb[0:m, 0:m])
                return last
            S.op("pe", [k for _, k in srcs] + ["identb"], [tpk], tr)
            S.op("act", [tpk], [dstk], lambda e: e.activation(out=dst_ap, in_=tp3[:, 0:len(srcs), 0:m], func=AF.Copy))

        def rms_to_T(xt, xk, gbc, gk, dstT, dstk, col0, m=128):
            ss, ssk = ssR.next()
            hb, hbk = hbR.next()
            S.op("act", [xk], [hbk, ssk], lambda e: e.activation(
                out=hb[0:m], in_=xt, func=AF.Square, accum_out=ss[0:m, 0:1]))
            rs, rsk = rstd_of(ss, ssk, 1, 1.0 / D)
            S.op("dve", [xk, rsk, gk], [hbk], lambda e: e.scalar_tensor_tensor(
                out=hb[0:m], in0=xt, scalar=rs[0:m, 0:1], in1=gbc[0:m], op0=ALU.mult, op1=ALU.mult))
            transposes([(hb[0:m, k * 128:(k + 1) * 128], hbk) for k in range(8)],
                       dstT[:, :, col0:col0 + m], dstk, m)

        pj_i = [0]

        def next_pj(banks=(0, 1)):
            b = banks[pj_i[0] % len(banks)]
            pj_i[0] += 1
            return PS[:, b, :], "PS%d" % b

        def load_w(src_cols):
            w, wk = W["wch"].next()
            S.dma("pool", [], [wk], w[:], src_cols.rearrange("(k p) n -> p k n", p=128))
            return w, wk

        def proj_tile(w, wk, srcT, srck, col0, m=128):
            ps, psk = next_pj()

            def mm(e):
                for k in range(8):
                    last = e.matmul(ps[0:m, :], srcT[:, k, col0:col0 + m], w[:, k, :],
                                    start=(k == 0), stop=(k == 7))
                return last
            S.op("pe", [wk, srck], [psk], mm)
            return ps, psk

        def qk_post(ps, psk, gbc3, gbk, gidx, n_comb, rope=True, m=128):
            qf, qfk = W["qf"].next()
            S.op("act", [psk], [qfk], lambda e: e.activation(
                out=qf[0:m].rearrange("p h d -> p (h d)"), in_=ps[0:m, :], func=AF.Copy))
            sq, sqk = W["sq"].next()
            S.op("pool", [qfk], [sqk], lambda e: e.tensor_tensor(out=sq[0:m], in0=qf[0:m], in1=qf[0:m], op=ALU.mult))
            ss, ssk = ssR.next()
            S.op("dve", [sqk], [ssk], lambda e: e.tensor_reduce(out=ss[0:m, :], in_=sq[0:m], axis=AX.X, op=ALU.add))
            rs, rsk = rstd_of(ss, ssk, 4, 1.0 / 128)
            qn, qnk = W["qn"].next()
            S.op("dve", [qfk, rsk], [qnk], lambda e: e.tensor_tensor(
                out=qn[0:m], in0=qf[0:m], in1=rs[0:m, :].unsqueeze(2).to_broadcast([m, 4, 128]), op=ALU.mult))
            S.op("pool", [qnk, gbk], [qnk], lambda e: e.tensor_tensor(
                out=qn[0:m], in0=qn[0:m], in1=gbc3[0:m, gidx:gidx + 1, :].to_broadcast([m, 4, 128]), op=ALU.mult))
            if rope:
                rt, rtk = W["rt"].next()
                cosb = ropecs[0:m, n_comb, 0:16].unsqueeze(1).to_broadcast([m, 4, 16])
                sinb = ropecs[0:m, n_comb, 16:32].unsqueeze(1).to_broadcast([m, 4, 16])
                x1 = qn[0:m, :, 0:16]
                x2 = qn[0:m, :, 16:32]

                def r1(e):
                    e.tensor_tensor(out=rt[0:m, 0], in0=x1, in1=cosb, op=ALU.mult)
                    e.tensor_tensor(out=rt[0:m, 1], in0=x2, in1=sinb, op=ALU.mult)
                    e.tensor_tensor(out=rt[0:m, 2], in0=x2, in1=cosb, op=ALU.mult)
                    return e.tensor_tensor(out=rt[0:m, 3], in0=x1, in1=sinb, op=ALU.mult)
                S.op("dve", [qnk, "ropecs"], [rtk], r1)

                def r2(e):
                    e.tensor_tensor(out=x1, in0=rt[0:m, 0], in1=rt[0:m, 1], op=ALU.subtract)
                    return e.tensor_tensor(out=x2, in0=rt[0:m, 2], in1=rt[0:m, 3], op=ALU.add)
                S.op("dve", [rtk], [qnk], r2)
            return qn, qnk

        def to_bf16(qn, qnk, m=128):
            qb, qbk = W["qb"].next()
            S.op("act", [qnk], [qbk], lambda e: e.activation(
                out=qb[0:m, :], in_=qn[0:m].rearrange("p h d -> p (h d)"), func=AF.Copy))
            return qb, qbk

        sc_banks = [4, 0]
        sc_i = [0]

        def attn_unit(qT, qTk, kT, kTks, vu, vuks, use_mask):
            sb0 = sc_banks[sc_i[0] % len(sc_banks)]
            sc_i[0] += 1
            sck_ = ["PS%d" % sb0, "PS%d" % (sb0 + 1)]
            sc = PS[:, sb0:sb0 + 2, :].rearrange("p b (x q) -> p (b x) q", q=128)

            def mm(e):
                for h in range(4):
                    for kb in range(2):
                        o = sc[:, h * 2 + kb, :]
                        last = e.matmul(o, kT[:, kb, h, :], qT[:, h, :], start=True, stop=not use_mask)
                        if use_mask:
                            last = e.matmul(o, identb[:], maskb[:, kb, :], start=False, stop=True)
                return last
            S.op("pe", [qTk] + kTks + ["identb", "maskb"], sck_, mm)
            pT, pTk = W["pT"].next()
            S.op("act", sck_, [pTk], lambda e: e.activation(
                out=pT[:].rearrange("p h k q -> p (h k) q"), in_=sc, func=AF.Exp, scale=SCALE))
            ov = PS[:, 6:8, 0:258].rearrange("p b (x c) -> p b x c", c=129)

            def pv(e):
                for h in range(4):
                    for kb in range(2):
                        last = e.matmul(ov[:, h // 2, h % 2, :], pT[:, h, kb, :], vu[:, kb, h, 0:129],
                                        start=(kb == 0), stop=(kb == 1))
                return last
            S.op("pe", [pTk] + vuks, ["PS6", "PS7"], pv)
            ou, ouk = W["ou"].next()
            S.op("dve", ["PS6", "PS7"], [ouk], lambda e: e.tensor_copy(
                ou[:].rearrange("p (b x) c -> p b (x c)", b=2), PS[:, 6:8, 0:258]))
            return ou, ouk

        g0bias = sb("g0bias_sb", [128, 4])
        biasnew = sb("biasnew_sb", [16, 2, 16])
        S.dma("sp", [], ["g0bias"], g0bias[:], g0bias_d[:, :])
        S.dma("sp", [], ["biasnew"], biasnew[:], biasnew_d[:, :, :])
        sqc = sb("sqc", [16, 2, 512])
        smg = sb("smg", [16, D], BF16)
        xsres = sb("xsres", [16, D])
        hTs = sb("hTs", [128, 8, 16], BF16)
        h2Ts = sb("h2Ts", [128, 8, 16], BF16)
        uTs = sb("uTs", [128, 4, 16], BF16)
        phs0 = contextlib.ExitStack()
        sqkv = sbt(phs0, "sqkv", [16, 3, 3, 512])
        pending = []
        for g, (Lg, r) in enumerate(GROUPS):
            for sq_ in range(4):
                r0 = 0
                while r0 < Lg - 4:
                    nr = min(256, Lg - 4 - r0)
                    pending.append((g, sq_, r0, nr))
                    r0 += nr

        def drip(k=1):
            for _ in range(k):
                if pending:
                    g_, s_, r0, nr = pending.pop(0)
                    S.dma("sp", [], ["swc%d_%d_%d" % (g_, s_, r0)], swin_o[g_][s_, r0:r0 + nr], cwin_d[g_][s_, r0 + 4:r0 + 4 + nr])

        with contextlib.ExitStack() as ph:
            hT = sbt(ph, "hT", [128, 8, TH], BF16)
            mk(ph, ["wch", "qf", "sq", "qn", "rt", "qb", "vf", "v1", "qu", "qT", "pT", "ou"])
            kuR = Rot(nc, ph, "ku", [128, 2, 512], BF16, 2)
            vuR = Rot(nc, ph, "vu", [128, 2, 4, 130], BF16, 2)
            kTR = Rot(nc, ph, "kT", [128, 2, 4, 128], BF16, 2)
            gb0, gb0k = load_gbc(g_mix[0:1, :])
            for n in range(NTH):
                xt, xk = xinR.next()
                S.dma("sp", [], [xk], xt[:], xkv[128 * n:128 * (n + 1), :])
                rms_to_T(xt[:], xk, gb0, gb0k, hT, "hT%d" % n, 128 * n)
            xt, xk = xinR.next()
            S.dma("sp", [], [xk], xt[0:16, :], xs_d[:, :])
            rms_to_T(xt[0:16, :], xk, gb0, gb0k, hTs, "hTs", 0, m=16)

            for g, (Lg, r) in enumerate(GROUPS):
                wq, wqk = load_w(w_in_a[:, g * 512:(g + 1) * 512])
                wk_, wkk = load_w(w_in_a[:, 1536 + g * 512:1536 + (g + 1) * 512])
                wv, wvk = load_w(w_in_a[:, 3072 + g * 512:3072 + (g + 1) * 512])
                n_lo = 16 - Lg // 128
                for n in range(n_lo, NTH):
                    own = n >= 16
                    srck = "hT%d" % n
                    in_win = own and (128 * (n - 16) >= T - Lg)
                    wrow = 128 * (n - 16) - (T - Lg)
                    ps, psk = proj_tile(wk_, wkk, hT, srck, 128 * n)
                    qn, qnk = qk_post(ps, psk, gk_bc, "gk_bc", g, n)
                    if in_win:
                        S.dma("sp", [qnk], ["wink%d_%d" % (g, n)], win_o[g][wrow:wrow + 128, 0, :, :], qn[:])
                    qb, qbk = to_bf16(qn, qnk)
                    S.dma("sp", [qbk], ["Ks%d_%d" % (g, n)], Ks[g][128 * n:128 * (n + 1), :], qb[:])
                    ps, psk = proj_tile(wv, wvk, hT, srck, 128 * n)
                    if in_win:
                        vf, vfk = W["vf"].next()
                        S.op("act", [psk], [vfk], lambda e, vf=vf, ps=ps: e.activation(out=vf[:], in_=ps, func=AF.Copy))
                        S.dma("sp", [vfk], ["winv%d_%d" % (g, n)], win_o[g][wrow:wrow + 128, 1, :, :],
                              vf[:].rearrange("p (h d) -> p h d", h=4))
                    v1, v1k = W["v1"].next()
                    fsrc = ones1 if own else flag

                    def vcp(e, v1=v1, ps=ps, fsrc=fsrc):
                        e.tensor_copy(v1[:, :, 0:128], ps.rearrange("p (h d) -> p h d", h=4))
                        return e.tensor_copy(v1[:, :, 128:130], fsrc[:, 0:1].unsqueeze(1).to_broadcast([128, 4, 2]))
                    S.op("dve", [psk, "flag", "ones1"], [v1k], vcp)
                    S.dma("sp", [v1k], ["Vs%d_%d" % (g, n)], Vs[g][128 * n:128 * (n + 1), :],
                          v1[:].rearrange("p h c -> p (h c)"))
                    if own:
                        ps, psk = proj_tile(wq, wqk, hT, srck, 128 * n)
                        qn, qnk = qk_post(ps, psk, gq_bc, "gq_bc", g, n)
                        qb, qbk = to_bf16(qn, qnk)
                        S.dma("sp", [qbk], ["Qs%d_%d" % (g, n - 16)], Qs[g][128 * (n - 16):128 * (n - 15), :], qb[:])

                for qi, (w_, wk2, gb3, gbk3) in enumerate(((wq, wqk, gq_bc, "gq_bc"), (wk_, wkk, gk_bc, "gk_bc"))):
                    ps, psk = proj_tile(w_, wk2, hTs, "hTs", 0, m=16)
                    qn, qnk = qk_post(ps, psk, gb3, gbk3, g, NTH, m=16)
                    S.op("pool", [qnk], ["sqkv%d_%d" % (qi, g)], lambda e, qn=qn, qi=qi, g=g: e.tensor_copy(
                        sqkv[0:16, qi, g, :], qn[0:16].rearrange("p h d -> p (h d)")))
                ps, psk = proj_tile(wv, wvk, hTs, "hTs", 0, m=16)
                S.op("act", [psk], ["sqkv2_%d" % g], lambda e, ps=ps, g=g: e.activation(
                    out=sqkv[0:16, 2, g, :], in_=ps[0:16, :], func=AF.Copy))
                for sq_ in range(4):
                    for kv_ in range(2):
                        S.dma("sp", ["sqkv%d_%d" % (kv_ + 1, g)], ["swn%d_%d_%d" % (g, sq_, kv_)],
                              swin_o[g][sq_, Lg - 4:Lg, kv_, :, :],
                              sqkv[4 * sq_:4 * sq_ + 4, kv_ + 1, g, :].rearrange("p (h d) -> p h d", h=4))
                nblk = 16 // r
                for rho in range(r):
                    for b in range(nblk):
                        q0 = rho + r * 128 * b
                        span = r * 127 + 1
                        qtiles = sorted(set((q0 + r * i) // 128 for i in range(128)))
                        c0 = 2048 + q0
                        p0 = c0 - 128 * r
                        ctiles = sorted(set((c0 + r * i) // 128 for i in range(128)))
                        ptiles = sorted(set((p0 + r * i) // 128 for i in range(128)))
                        drip(1)
                        qu, quk = W["qu"].next()
                        ku, kuk = kuR.next()
                        vu, vuk = vuR.next()
                        S.dma("sp", ["Qs%d_%d" % (g, t) for t in qtiles], [quk], qu[:], Qs[g][q0:q0 + span:r, :])
                        S.dma("sp", ["Ks%d_%d" % (g, t) for t in ptiles], [kuk + "a"], ku[:, 0, :], Ks[g][p0:p0 + span:r, :])
                        S.dma("sp", ["Ks%d_%d" % (g, t) for t in ctiles], [kuk + "b"], ku[:, 1, :], Ks[g][c0:c0 + span:r, :])
                        S.dma("sp", ["Vs%d_%d" % (g, t) for t in ptiles], [vuk + "a"],
                              vu[:, 0].rearrange("p h c -> p (h c)"), Vs[g][p0:p0 + span:r, :])
                        S.dma("sp", ["Vs%d_%d" % (g, t) for t in ctiles], [vuk + "b"],
                              vu[:, 1].rearrange("p h c -> p (h c)"), Vs[g][c0:c0 + span:r, :])
                        qT, qTk = W["qT"].next()
                        kT, kTk = kTR.next()
                        transposes([(qu[:, h * 128:(h + 1) * 128], quk) for h in range(4)], qT[:], qTk)
                        transposes([(ku[:, 0, h * 128:(h + 1) * 128], kuk + "a") for h in range(4)], kT[:, 0], kTk + "a")
                        transposes([(ku[:, 1, h * 128:(h + 1) * 128], kuk + "b") for h in range(4)], kT[:, 1], kTk + "b")
                        ou, ouk = attn_unit(qT, qTk, kT, [kTk + "a", kTk + "b"], vu, [vuk + "a", vuk + "b"], True)
                        S.dma("sp", [ouk], ["NUM%d_%d" % (g, t) for t in qtiles], NUM[g][q0:q0 + span:r, :],
                              ou[:].rearrange("p h c -> p (h c)"))

            wqc, wqck = load_w(w_in_a[:, 4608:5120])
            for n in range(16, NTH):
                ps, psk = proj_tile(wqc, wqck, hT, "hT%d" % n, 128 * n)
                qn, qnk = qk_post(ps, psk, gqc_bc, "gqc_bc", 0, n, rope=False)
                qb, qbk = to_bf16(qn, qnk)
                S.dma("sp", [qbk], ["QC_%d" % (n - 16)], QC[128 * (n - 16):128 * (n - 15), :], qb[:])
            ps, psk = proj_tile(wqc, wqck, hTs, "hTs", 0, m=16)
            qn, qnk = qk_post(ps, psk, gqc_bc, "gqc_bc", 0, NTH, rope=False, m=16)
            S.op("pool", [qnk], ["sqc0"], lambda e, qn=qn: e.tensor_copy(sqc[0:16, 0, :], qn[0:16].rearrange("p h d -> p (h d)")))
            drip(100)
            barrier()


        def sample_attention(layer, with_windows):
            with contextlib.ExitStack() as pa:
                kvR = Rot(nc, pa, "skv%d" % layer, [128, 2, 4, 128], F32, 3)
                prR = Rot(nc, pa, "spr%d" % layer, [128, 4, 128], F32, 2)
                scR = Rot(nc, pa, "ssc%d" % layer, [128, 4], F32, 4)
                ncc = (48 if with_windows else 0) + 32
                lt = sbt(pa, "slt%d" % layer, [128, ncc, 4, 16])
                lt2 = sbt(pa, "slt2%d" % layer, [16, 48 if with_windows else 1, 4, 16])
                selT = sbt(pa, "selT%d" % layer, [16, 16, 128])
                S.op("pool", [], ["lt0"], lambda e: e.memset(lt[:], 0.0))
                S.op("pool", [], ["lt20"], lambda e: e.memset(lt2[:], 0.0))
                S.op("dve", ["identf"], ["selT"], lambda e: e.tensor_copy(
                    selT[:], identf[0:16, 0:16].unsqueeze(2).to_broadcast([16, 16, 128])))
                accm = sbt(pa, "saccm%d" % layer, [16, 516])
                accc = sbt(pa, "saccc%d" % layer, [16, 516])
                S.op("pool", [], ["saccm"], lambda e: e.memset(accm[:], 0.0))
                S.op("pool", [], ["saccc"], lambda e: e.memset(accc[:], 0.0))
                qb_i = [0]
                cb_i = [0]
                cidx = [0, 0]

                def q_bcast(src_ap, srck, row):
                    bank = qb_i[0] % 2
                    qb_i[0] += 1
                    S.op("pe", [srck, "selT"], ["PS%d" % bank], lambda e: e.matmul(
                        PS[:, bank, :], selT[0:16, row, :], src_ap, start=True, stop=True))
                    return PS[:, bank, :].rearrange("p (h d) -> p h d", h=4), "PS%d" % bank

                def combo(qb, qbk, row, K_ap, V_ap, kvk, bias_ap, biask, acc, npart, ltile, which):
                    pr, prk = prR.next()
                    sc_, sck = scR.next()
                    S.op("dve", [qbk] + kvk, [prk], lambda e: e.tensor_tensor(
                        out=pr[0:npart], in0=K_ap, in1=qb[0:npart], op=ALU.mult))
                    S.op("dve", [prk], [sck], lambda e: e.tensor_reduce(
                        out=sc_[0:npart, :], in_=pr[0:npart], axis=AX.X, op=ALU.add))
                    c = cidx[which]
                    cidx[which] += 1
                    dst = ltile[0:npart, c, :, row]
                    ltk = "lt%d_%d" % (which, c)
                    if bias_ap is None:
                        S.op("act", [sck, "lt0", "lt20"], [ltk], lambda e: e.activation(
                            out=dst, in_=sc_[0:npart, :], func=AF.Exp, scale=SCALE))
                    else:
                        S.op("act", [sck, biask, "lt0", "lt20"], [ltk], lambda e: e.activation(
                            out=dst, in_=sc_[0:npart, :], func=AF.Exp, scale=SCALE, bias=bias_ap))
                    bn = 2 + 2 * (cb_i[0] % 2)
                    cb_i[0] += 1
                    num_ps, den_ps = PS[0:16, bn, :], PS[0:16, bn + 1, 0:4]

                    def pe(e):
                        for h in range(4):
                            e.matmul(num_ps[:, h * 128:(h + 1) * 128], ltile[0:npart, c, h, :], V_ap[:, h, :],
                                     start=True, stop=True)
                            last = e.matmul(den_ps[:, h:h + 1], ltile[0:npart, c, h, :], ones1[0:npart, 0:1],
                                            start=True, stop=True)
                        return last
                    S.op("pe", [ltk] + kvk + ["ones1"], ["PS%d" % bn, "PS%d" % (bn + 1)], pe)

                    def addacc(e):
                        e.tensor_tensor(out=acc[0][0:16, 0:512], in0=num_ps, in1=acc[0][0:16, 0:512], op=ALU.add)
                        return e.tensor_tensor(out=acc[0][0:16, 512:516], in0=den_ps, in1=acc[0][0:16, 512:516], op=ALU.add)
                    S.op("dve", ["PS%d" % bn, "PS%d" % (bn + 1), acc[1]], [acc[1]], addacc)

                mix_acc = (accm, "saccm")
                crs_acc = (accc, "saccc")
                for sq_ in range(4):
                    if with_windows:
                        for g, (Lg, r) in enumerate(GROUPS):
                            kv0 = None
                            for t in range(4):
                                row = 4 * sq_ + t
                                qb, qbk = q_bcast(sqkv[0:16, 0, g, :], "sqkv0_%d" % g, row)
                                if g == 0:
                                    if kv0 is None:
                                        kv0 = kvR.next()
                                        S.dma("sp", [], [kv0[1]], kv0[0][:], cwin_d[0][sq_, 0:128])
                                    kv, kvk = kv0
                                    bias_ap, biask = g0bias[:, t:t + 1], "g0bias"
                                else:
                                    kv, kvk = kvR.next()
                                    S.dma("sp", [], [kvk], kv[:], cwin_d[g][sq_, t:t + 127 * r + 1:r])
                                    bias_ap, biask = None, None
                                combo(qb, qbk, row, kv[:, 0], kv[:, 1], [kvk], bias_ap, biask, mix_acc, 128, lt, 0)
                                combo(qb, qbk, row, sqkv[0:16, 1, g, :].rearrange("p (h d) -> p h d", h=4),
                                      sqkv[0:16, 2, g, :].rearrange("p (h d) -> p h d", h=4),
                                      ["sqkv1_%d" % g, "sqkv2_%d" % g], biasnew[0:16, 0 if g == 0 else 1, row:row + 1],
                                      "biasnew", mix_acc, 16, lt2, 1)
                    kvm = [kvR.next() for _ in range(2)]
                    for mt in range(2):
                        S.dma("sp", [], [kvm[mt][1]], kvm[mt][0][:], cmem_d[layer, sq_, 128 * mt:128 * (mt + 1)])
                    for t in range(4):
                        row = 4 * sq_ + t
                        qb, qbk = q_bcast(sqc[0:16, layer, :], "sqc%d" % layer, row)
                        for mt in range(2):
                            combo(qb, qbk, row, kvm[mt][0][:, 0], kvm[mt][0][:, 1], [kvm[mt][1]], None, None,
                                  crs_acc, 128, lt, 0)
                for (acc_t, acck), col0 in ([(mix_acc, 0)] if with_windows else []) + [(crs_acc, 512)]:
                    def nrm(e, acc_t=acc_t, col0=col0):
                        e.reciprocal(acc_t[0:16, 512:516], acc_t[0:16, 512:516])
                        e.tensor_tensor(
                            out=smg[0:16, col0:col0 + 512].rearrange("p (h d) -> p h d", h=4),
                            in0=acc_t[0:16, 0:512].rearrange("p (h d) -> p h d", h=4),
                            in1=acc_t[0:16, 512:516].unsqueeze(2).to_broadcast([16, 4, 128]), op=ALU.mult)
                    S.op("dve", [acck], ["smg_%d" % col0], nrm, chain=True)
                barrier()

        def sample_tail(layer, x_src_ap, x_srck, gf, gfk):
            mgT, mgTk = W["mgT"].next()
            if layer == 0:
                transposes([(smg[0:16, k * 128:(k + 1) * 128], "smg_%d" % (0 if k < 4 else 512)) for k in range(8)],
                           mgT[:, :, 0:16], mgTk, m=16)
                lhs = [(mgT[:, k, 0:16], mgTk) for k in range(8)]
            else:
                transposes([(smg[0:16, 512 + k * 128:512 + (k + 1) * 128], "smg_512") for k in range(4)],
                           mgT[:, 4:8, 0:16], mgTk, m=16)
                lhs = [(uTs[:, k, :], "uTs") for k in range(4)] + [(mgT[:, 4 + k, 0:16], mgTk) for k in range(4)]
            out_proj_residual(0, lhs, x_src_ap, x_srck, dst_ap=xsres[0:16, :], dstk="xsres", m=16)
            rms_to_T(xsres[0:16, :], "xsres", gf, gfk, h2Ts, "h2Ts", 0, m=16)

        sample_attention(0, True)
        phs0.close()
        if stage <= 1:
            return end()

        xres = sb("xres", [128, NT, D])
        KmT = sb("KmT", [128, 2, 4, 128], BF16)
        V1m = sb("V1m", [128, 2, 4, 130], BF16)
        S.op("dve", [], ["V1mones"], lambda e: e.memset(V1m[:], 1.0))

        def memory_kv(layer, ph):
            memT = sbt(ph, "memT%d" % layer, [128, 8, 256], BF16)
            gm, gmk = load_gbc(g_mem[layer:layer + 1, :])
            for mt in range(2):
                xt, xk = xinR.next()
                S.dma("sp", [], [xk], xt[:], mem_d[128 * mt:128 * (mt + 1), :])
                rms_to_T(xt[:], xk, gm, gmk, memT, "memT%d" % mt, 128 * mt)
            DM = int(os.environ.get("DBG_M", "9"))
            if DM <= 1:
                return
            wk_, wkk = load_w(w_mem_kv[layer, :, 0:512])
            wv, wvk = load_w(w_mem_kv[layer, :, 512:1024])
            for mt in range(2):
                ps, psk = proj_tile(wk_, wkk, memT, "memT%d" % mt, 128 * mt)
                qn, qnk = qk_post(ps, psk, gkc_bc, "gkc_bc", layer, 0, rope=False)
                S.dma("sp", [qnk], ["memk%d_%d" % (layer, mt)], memkv_o[layer, 128 * mt:128 * (mt + 1), 0, :, :], qn[:])
                qb, qbk = to_bf16(qn, qnk)
                transposes([(qb[:, h * 128:(h + 1) * 128], qbk) for h in range(4)], KmT[:, mt], "KmT%d" % mt)
                if DM <= 2:
                    continue
                ps, psk = proj_tile(wv, wvk, memT, "memT%d" % mt, 128 * mt)
                vf, vfk = W["vf"].next()
                S.op("act", [psk], [vfk], lambda e, vf=vf, ps=ps: e.activation(out=vf[:], in_=ps, func=AF.Copy))
                S.dma("sp", [vfk], ["memv%d_%d" % (layer, mt)], memkv_o[layer, 128 * mt:128 * (mt + 1), 1, :, :],
                      vf[:].rearrange("p (h d) -> p h d", h=4))

                S.op("act", [psk, "V1mones"], ["V1m%d" % mt], lambda e, mt=mt, ps=ps: e.activation(
                    out=V1m[:, mt, :, 0:128], in_=ps.rearrange("p (h d) -> p h d", h=4), func=AF.Copy))

        def load_wout(layer, stack):
            W["wout"] = sbt(stack, "wout%d" % layer, [128, 8, D], BF16)
            S.dma("pool", [], ["wout"], W["wout"][:], w_out[layer].rearrange("(k p) n -> p k n", p=128))

        def normalize_into(mg, mgk, col0, src, srck, srcap):
            rd, rdk = W["rd"].next()
            S.op("dve", [srck], [rdk], lambda e: e.reciprocal(rd[:, 0:4], srcap[:, :, 128]))
            S.op("dve", [srck, rdk], [mgk + "_%d" % col0], lambda e: e.tensor_tensor(
                out=mg[:, col0:col0 + 512].rearrange("p (h d) -> p h d", h=4), in0=srcap[:, :, 0:128],
                in1=rd[:, 0:4].unsqueeze(2).to_broadcast([128, 4, 128]), op=ALU.mult))

        def cross_attention(n, mg, mgk):
            qu, quk = W["qu"].next()
            S.dma("sp", ["QC_%d" % n], [quk], qu[:], QC[128 * n:128 * (n + 1), :])
            qT, qTk = W["qT"].next()
            transposes([(qu[:, h * 128:(h + 1) * 128], quk) for h in range(4)], qT[:], qTk)
            ou, ouk = attn_unit(qT, qTk, KmT, ["KmT0", "KmT1"], V1m, ["V1m0", "V1m1"], False)
            normalize_into(mg, mgk, 512, ou, ouk, ou[:])

        def out_proj_residual(n, lhs, x_src_ap, x_srck, dst_ap=None, dstk=None, m=128):
            if dst_ap is None:
                dst_ap, dstk = xres[:, n, :], "xres%d" % n

            def mm(e):
                for hf in range(2):
                    for k in range(8):
                        last = e.matmul(PS[0:m, hf, :], lhs[k][0], W["wout"][:, k, hf * 512:(hf + 1) * 512],
                                        start=(k == 0), stop=(k == 7))
                return last
            S.op("pe", sorted(set(k for _, k in lhs)) + ["wout"], ["PS0", "PS1"], mm)
            S.op("dve", ["PS0", "PS1", x_srck], [dstk], lambda e: e.tensor_tensor(
                out=dst_ap, in0=PS[0:m, 0:2, :].rearrange("p a b -> p (a b)"), in1=x_src_ap, op=ALU.add))

        def ffn(layer, ph, h2T, final_out):
            gT = sbt(ph, "gT%d" % layer, [128, NFF, 512], BF16)
            wupR = Rot(nc, ph, "wup%d" % layer, [128, 8, 2, 128], BF16, 3)
            wdnR = Rot(nc, ph, "wdn%d" % layer, [128, D], BF16, 3)
            cR = Rot(nc, ph, "cv%d" % layer, [128, 2, 512], F32, 2)
            saR = Rot(nc, ph, "sa%d" % layer, [128, 512], F32, 2)
            cwl = sbt(ph, "cwl%d" % layer, [44, 4, 128])
            cwT = sbt(ph, "cwT%d" % layer, [128, 4, 44])
            tails = sbt(ph, "tails%d" % layer, [128, 2, 44])
            tlo = sbt(ph, "tlo%d" % layer, [88, 128])
            hx = sbt(ph, "hx%d" % layer, [128, 16], BF16)
            hxf = sbt(ph, "hxf%d" % layer, [128, 32])
            hxr = sbt(ph, "hxr%d" % layer, [128, 32])
            for j in range(3):
                S.dma("sp", [], ["cwl%d" % j], cwl[:, j, :], conv_w[layer, j, :].rearrange("(t p) -> t p", p=128))
            S.dma("sp", [], ["cwl3"], cwl[:, 3, :], conv_b[layer, :].rearrange("(t p) -> t p", p=128))
            cps = PS[:, 7, 0:176].rearrange("p (j t) -> p j t", j=4)

            def ctr(e):
                for j in range(4):
                    last = e.transpose(cps[:, j, :], cwl[:, j, :], identf[0:44, 0:44])
                return last
            S.op("pe", ["cwl0", "cwl1", "cwl2", "cwl3", "identf"], ["PS7"], ctr)
            S.op("dve", ["PS7"], ["cwT"], lambda e: e.tensor_copy(cwT[:], cps))
            ci = layer
            S.op("dve", ["h2T%d" % 15], ["hxf"], lambda e: e.tensor_copy(
                hxf[:, 0:16].rearrange("p (k t) -> p k t", t=2), h2T[:, :, 2 + T - 2:2 + T]))
            S.dma("pool", ["hxf"], ["ccs%d" % ci], cc_src[ci][:, 0:16], hxf[:, 0:16])
            S.allgather_pairs(["ccs%d" % ci], ["ccd%d" % ci], cc_src[ci][:, :], cc_dst[ci][:, :])
            S.dma("pool", ["ccd%d" % ci], ["hxr"], hxr[:, 0:16], cc_dst[ci][0:128, 0:16])
            S.op("dve", ["hxr", "flag"], ["h2Th"], lambda e: e.tensor_scalar(
                out=h2T[:, :, 0:2], in0=hxr[:, 0:16].rearrange("p (k t) -> p k t", t=2),
                scalar1=flag[:, 0:1], scalar2=None, op0=ALU.mult))
            by_kind = {"up": [i for _b in range(5) for i in range(NFF)], "dn": [i for _b in range(5) for i in range(NFF)]}
            issued = {"up": 0, "dn": 0}
            consumed = {"up": 0, "dn": 0}
            handles = {}
            PFD = {"up": 2, "dn": 2}

            def issue_upto():
                for kind in ("up", "dn"):
                    while issued[kind] < len(by_kind[kind]) and issued[kind] <= consumed[kind] + PFD[kind]:
                        n_ = issued[kind]
                        i_ = by_kind[kind][n_]
                        if kind == "up":
                            t_, k_ = wupR.next()
                            S.dma("pool", [], [k_], t_[:, :, 0, :],
                                  w_up[layer, :, 128 * i_:128 * (i_ + 1)].rearrange("(k p) n -> p k n", p=128))
                            S.dma("pool", [], [k_ + "b"], t_[:, :, 1, :],
                                  w_up[layer, :, DFF + 128 * i_:DFF + 128 * (i_ + 1)].rearrange("(k p) n -> p k n", p=128))
                        else:
                            t_, k_ = wdnR.next()
                            S.dma("pool", [], [k_], t_[:], w_down[layer, 128 * i_:128 * (i_ + 1), :])
                        handles[(kind, n_)] = (t_, k_)
                        issued[kind] += 1

            def take(kind):
                issue_upto()
                n_ = consumed[kind]
                consumed[kind] += 1
                return handles.pop((kind, n_))

            up_banks = [(0, 1), (2, 3)]
            upi = 0
            for blk in range(4):
                t0 = 2 + 512 * blk
                hkeys = ["h2T%d" % (4 * blk + j) for j in range(4)] + (["h2Th"] if blk == 0 else ["h2T%d" % (4 * blk - 1)])
                for i in range(NFF):
                    wup, wupk = take("up")
                    ba, bb = up_banks[upi % 2]
                    upi += 1
                    hb_ = PS[:, 4 + i % 2, 0:4].rearrange("p (a t) -> p a t", a=2)

                    def mm(e, wup=wup, ba=ba, bb=bb, hb_=hb_):
                        for ab, bank in ((0, ba), (1, bb)):
                            for k in range(8):
                                e.matmul(PS[:, bank, :], wup[:, k, ab, :], h2T[:, k, t0:t0 + 512],
                                         start=(k == 0), stop=(k == 7))
                            for k in range(8):
                                last = e.matmul(hb_[:, ab, :], wup[:, k, ab, :], h2T[:, k, t0 - 2:t0],
                                                start=(k == 0), stop=(k == 7))
                        return last
                    S.op("pe", [wupk, wupk + "b"] + hkeys, ["PS%d" % ba, "PS%d" % bb, "PS%d" % (4 + i % 2)], mm)
                    cv, cvk = cR.next()

                    def conv(e, cv=cv, ba=ba, bb=bb, hb_=hb_, i=i):
                        hv = ((0, ba), (1, bb))
                        ti_ = lambda ab: ab * NFF + i
                        for ab, bank in hv:
                            e.tensor_scalar(out=cv[:, ab, :], in0=PS[:, bank, :], scalar1=cwT[:, 2, ti_(ab):ti_(ab) + 1],
                                            scalar2=cwT[:, 3, ti_(ab):ti_(ab) + 1], op0=ALU.mult, op1=ALU.add)
                        for ab, bank in hv:
                            e.scalar_tensor_tensor(out=cv[:, ab, 1:512], in0=PS[:, bank, 0:511],
                                                   scalar=cwT[:, 1, ti_(ab):ti_(ab) + 1], in1=cv[:, ab, 1:512],
                                                   op0=ALU.mult, op1=ALU.add)
                        for ab, bank in hv:
                            e.scalar_tensor_tensor(out=cv[:, ab, 2:512], in0=PS[:, bank, 0:510],
                                                   scalar=cwT[:, 0, ti_(ab):ti_(ab) + 1], in1=cv[:, ab, 2:512],
                                                   op0=ALU.mult, op1=ALU.add)
                        for ab, bank in hv:
                            e.scalar_tensor_tensor(out=cv[:, ab, 0:2], in0=hb_[:, ab, :], scalar=cwT[:, 0, ti_(ab):ti_(ab) + 1],
                                                   in1=cv[:, ab, 0:2], op0=ALU.mult, op1=ALU.add)
                        for ab, bank in hv:
                            e.scalar_tensor_tensor(out=cv[:, ab, 0:1], in0=hb_[:, ab, 1:2], scalar=cwT[:, 1, ti_(ab):ti_(ab) + 1],
                                                   in1=cv[:, ab, 0:1], op0=ALU.mult, op1=ALU.add)
                        if blk == 3:
                            for ab, bank in hv:
                                e.tensor_copy(tails[:, :, ti_(ab)], PS[:, bank, 510:512])
                    S.op("dve", ["PS%d" % ba, "PS%d" % bb, "PS%d" % (4 + i % 2), "cwT"],
                         [cvk] + (["tails"] if blk == 3 else []), conv, chain=True, lag=2)
                    sa, sak = saR.next()
                    S.op("act", [cvk], [sak], lambda e, sa=sa, cv=cv: e.activation(out=sa[:], in_=cv[:, 0, :], func=AF.Silu))
                    S.op("pool", [sak, cvk], ["gT%d" % i], lambda e, sa=sa, cv=cv, i=i: e.tensor_tensor(
                        out=gT[:, i, :], in0=sa[:], in1=cv[:, 1, :], op=ALU.mult))
                for i in range(NFF):
                    wdn, wdnk = take("dn")

                    def dn(e, wdn=wdn, i=i):
                        for tt in range(4):
                            for hf in range(2):
                                last = e.matmul(PS[:, 2 * tt + hf, :], gT[:, i, 128 * tt:128 * (tt + 1)],
                                                wdn[:, hf * 512:(hf + 1) * 512], start=(i == 0), stop=(i == NFF - 1))
                        return last
                    S.op("pe", [wdnk, "gT%d" % i], ["PS%d" % b for b in range(8)], dn)
                for tt in range(4):
                    n = 4 * blk + tt
                    S.op("dve", ["PS%d" % (2 * tt), "PS%d" % (2 * tt + 1), "xres%d" % n], ["xres%d" % n],
                         lambda e, tt=tt, n=n: e.tensor_tensor(
                             out=xres[:, n, :], in0=PS[:, 2 * tt:2 * tt + 2, :].rearrange("p a b -> p (a b)"),
                             in1=xres[:, n, :], op=ALU.add))
                    if final_out:
                        S.dma("sp", ["xres%d" % n], ["y%d" % n], y_o[128 * n:128 * (n + 1), :], xres[:, n, :])

            gTs = sbt(ph, "gTs%d" % layer, [128, NFF, 16], BF16)
            cstT = sbt(ph, "cstT%d" % layer, [128, 44, 8])
            tls = sbt(ph, "tls%d" % layer, [128, 8, 44])
            tlso = sbt(ph, "tlso%d" % layer, [128, 3, 128])
            extR = Rot(nc, ph, "ext%d" % layer, [128, 2, 4, 6], F32, 2)
            cvsR = Rot(nc, ph, "cvs%d" % layer, [128, 2, 4, 4], F32, 2)
            sasR = Rot(nc, ph, "sas%d" % layer, [128, 16], F32, 2)
            for q in range(6):
                xt, xk = xinR.next()
                ncol = min(1024, 2 * DFF - 1024 * q)
                S.dma("sp", [], [xk], xt[0:8, 0:ncol], cst_d[layer, :, 1024 * q:1024 * q + ncol])
                ntile = ncol // 128
                cps = PS[:, 6, 0:64].rearrange("p (t c) -> p t c", c=8)

                def ctr2(e, xt=xt, ntile=ntile, cps=cps):
                    for t in range(ntile):
                        last = e.transpose(cps[:, t, :], xt[0:8, 128 * t:128 * (t + 1)], identf[0:8, 0:8])
                    return last
                S.op("pe", [xk, "identf"], ["PS6"], ctr2)
                S.op("dve", ["PS6"], ["cstT"], lambda e, q=q, ntile=ntile, cps=cps: e.tensor_copy(
                    cstT[:, 8 * q:8 * q + ntile, :], cps[:, 0:ntile, :]))
            for i in range(NFF):
                wup, wupk = take("up")
                bank = i % 2
                ups = PS[:, bank, 0:32].rearrange("p (a t) -> p a t", a=2)

                def mms(e, wup=wup, ups=ups):
                    for ab in range(2):
                        for k in range(8):
                            last = e.matmul(ups[:, ab, :], wup[:, k, ab, :], h2Ts[:, k, :], start=(k == 0), stop=(k == 7))
                    return last
                S.op("pe", [wupk, wupk + "b", "h2Ts"], ["PS%d" % bank], mms)
                ext, extk = extR.next()
                cvs, cvsk = cvsR.next()

                def convs(e, ext=ext, cvs=cvs, ups=ups, i=i):
                    for ab in range(2):
                        ti = ab * NFF + i
                        e.tensor_copy(ext[:, ab, :, 0:2], cstT[:, ti, :].rearrange("p (s t) -> p s t", t=2))
                        e.tensor_copy(ext[:, ab, :, 2:6], ups[:, ab, :].rearrange("p (s t) -> p s t", t=4))
                        e.tensor_copy(tls[:, :, ti].rearrange("p (s t) -> p s t", t=2), ext[:, ab, :, 4:6])
                        c = cvs[:, ab]
                        e.tensor_scalar(out=c, in0=ext[:, ab, :, 2:6], scalar1=cwT[:, 2, ti:ti + 1],
                                        scalar2=cwT[:, 3, ti:ti + 1], op0=ALU.mult, op1=ALU.add)
                        e.scalar_tensor_tensor(out=c, in0=ext[:, ab, :, 1:5], scalar=cwT[:, 1, ti:ti + 1], in1=c,
                                               op0=ALU.mult, op1=ALU.add)
                        e.scalar_tensor_tensor(out=c, in0=ext[:, ab, :, 0:4], scalar=cwT[:, 0, ti:ti + 1], in1=c,
                                               op0=ALU.mult, op1=ALU.add)
                S.op("dve", ["PS%d" % bank, "cstT", "cwT"], [extk, cvsk, "tls"], convs, chain=True)
                sas, sask = sasR.next()
                S.op("act", [cvsk], [sask], lambda e, sas=sas, cvs=cvs: e.activation(
                    out=sas[:].rearrange("p (s t) -> p s t", t=4), in_=cvs[:, 0], func=AF.Silu))
                S.op("pool", [sask, cvsk], ["gTs%d" % i], lambda e, sas=sas, cvs=cvs, i=i: e.tensor_tensor(
                    out=gTs[:, i, :].rearrange("p (s t) -> p s t", t=4), in0=sas[:].rearrange("p (s t) -> p s t", t=4),
                    in1=cvs[:, 1], op=ALU.mult))
            for i in range(NFF):
                wdn, wdnk = take("dn")

                def dns(e, wdn=wdn, i=i):
                    for hf in range(2):
                        last = e.matmul(PS[0:16, 2 + hf, :], gTs[:, i, :], wdn[:, hf * 512:(hf + 1) * 512],
                                        start=(i == 0), stop=(i == NFF - 1))
                    return last
                S.op("pe", [wdnk, "gTs%d" % i], ["PS2", "PS3"], dns)
            S.op("dve", ["PS2", "PS3", "xsres"], ["xsres"], lambda e: e.tensor_tensor(
                out=xsres[0:16, :], in0=PS[0:16, 2:4, :].rearrange("p a b -> p (a b)"), in1=xsres[0:16, :], op=ALU.add))
            if final_out:
                S.dma("sp", ["xsres"], ["ys"], ys_o[:, :], xsres[0:16, :])
            tl2 = tls[:].rearrange("p a t -> p (a t)")
            for q in range(3):
                ncol = min(128, 352 - 128 * q)
                S.op("pe", ["tls", "identf"], ["PS7"], lambda e, q=q, ncol=ncol: e.transpose(
                    PS[0:ncol, 7, 0:128], tl2[:, 128 * q:128 * q + ncol], identf[:]))
                S.op("dve", ["PS7"], ["tlso%d" % q], lambda e, q=q, ncol=ncol: e.tensor_copy(
                    tlso[0:ncol, q, :], PS[0:ncol, 7, 0:128]))
                S.dma("sp", ["tlso%d" % q], ["sconv%d_%d" % (layer, q)], sconv_o[layer, 128 * q:128 * q + ncol, :],
                      tlso[0:ncol, q, :])
            tps = PS[:, 7, 0:128]

            def ttr(e):
                return e.transpose(tps[0:88, :], tails[:].rearrange("p t c -> p (t c)"), identf[:])
            S.op("pe", ["tails", "identf"], ["PS7"], ttr)
            S.op("dve", ["PS7"], ["tlo"], lambda e: e.tensor_copy(tlo[:], tps[0:88, :]))
            S.dma("sp", ["tlo"], ["convo%d" % layer], conv_o[layer, :, :], tlo[:])

        with contextlib.ExitStack() as ph:
            h2T = sbt(ph, "h2T0", [128, 8, 2 + T], BF16)
            with contextlib.ExitStack() as ph1:
                mk(ph1, ["wch", "qf", "sq", "qn", "qb", "vf"], {"wch": 2})
                memory_kv(0, ph1)
                barrier()
            phb = contextlib.ExitStack()
            phb.__enter__()
            DB = int(os.environ.get("DBG_B", "9"))
            if DB <= 1:
                return end()
            mk(phb, ["qu", "qT", "pT", "ou", "mg", "mgT", "numt", "rd"])
            load_wout(0, phb)
            gf, gfk = load_gbc(g_ffn[0:1, :])
            for n in range(NT):
                mg, mgk = W["mg"].next()
                nt_, ntk = W["numt"].next()
                for g in range(3):
                    S.dma("sp", ["NUM%d_%d" % (g, n)], [ntk + "_%d" % g], nt_[:, g, :], NUM[g][128 * n:128 * (n + 1), :])
                S.op("pool", [ntk + "_0", ntk + "_1"], [ntk + "_0"], lambda e, nt_=nt_: e.tensor_tensor(
                    out=nt_[:, 0, :], in0=nt_[:, 0, :], in1=nt_[:, 1, :], op=ALU.add))
                S.op("pool", [ntk + "_0", ntk + "_2"], [ntk + "_0"], lambda e, nt_=nt_: e.tensor_tensor(
                    out=nt_[:, 0, :], in0=nt_[:, 0, :], in1=nt_[:, 2, :], op=ALU.add))
                normalize_into(mg, mgk, 0, nt_, ntk + "_0", nt_[:, 0, :].rearrange("p (h c) -> p h c", c=129))
                cross_attention(n, mg, mgk)
                mgT, mgTk = W["mgT"].next()
                transposes([(mg[:, k * 128:(k + 1) * 128], mgk + "_%d" % (0 if k < 4 else 512)) for k in range(8)],
                           mgT[:], mgTk)
                xt, xk = xinR.next()
                S.dma("sp", [], [xk], xt[:], xkv[T + 128 * n:T + 128 * (n + 1), :])
                if DB <= 4:
                    continue
                out_proj_residual(n, [(mgT[:, k, :], mgTk) for k in range(8)], xt[:], xk)
                if DB <= 5:
                    continue
                rms_to_T(xres[:, n, :], "xres%d" % n, gf, gfk, h2T, "h2T%d" % n, 2 + 128 * n)
            if stage <= 2:
                for n in range(NT if DB >= 5 else 0):
                    S.dma("sp", ["xres%d" % n], ["dbg%d" % n], dbg[128 * n:128 * (n + 1), :], xres[:, n, :])
                return end()
            xt, xk = xinR.next()
            S.dma("sp", [], [xk], xt[0:16, :], xs_d[:, :])
            sample_tail(0, xt[0:16, :], xk, gf, gfk)
            barrier()
            phb.close()
            with contextlib.ExitStack() as ph2:
                ffn(0, ph2, h2T, final_out=False)
                barrier()
        if stage <= 3:
            for n in range(NT):
                S.dma("sp", ["xres%d" % n], ["dbg%d" % n], dbg[128 * n:128 * (n + 1), :], xres[:, n, :])
            return end()


        TWO_PI = 6.2831845

        def s5_setup(ph):
            P = {}
            prm = sbt(ph, "prm", [128, 48])
            sm = sbt(ph, "s5sm", [128, 16, 16])
            cosT = sbt(ph, "cosT", [128, 16, 128])
            sinT = sbt(ph, "sinT", [128, 16, 128])
            Bmat = sbt(ph, "Bmat", [128, 2, 16, 128], BF16)
            Cmat = sbt(ph, "Cmat", [128, 2, 16, 128], BF16)
            dbT = sbt(ph, "dbT", [128, 8])
            Dmat = sbt(ph, "Dmat", [128, 4, 128], BF16)
            wglu = sbt(ph, "wglu", [128, 4, 512], BF16)
            tmp = contextlib.ExitStack()
            L3 = sbt(tmp, "L3", [48, 128])
            S.dma("sp", [], ["L3a"], L3[0:16, :], lam_re_d[:, :])
            S.dma("sp", [], ["L3b"], L3[16:32, :], lam_im_d[:, :])
            Lt = sbt(tmp, "Lt", [48, 2])
            S.dma("sp", [], ["Lt"], Lt[32:48, :], log_dt_d[:, :])
            S.op("act", ["Lt"], ["L3c"], lambda e: e.activation(
                out=L3[32:48, :].rearrange("p (e q) -> p e q", e=2),
                in_=Lt[32:48, :].unsqueeze(2).to_broadcast([16, 2, 64]), func=AF.Copy))
            S.op("pe", ["L3a", "L3b", "L3c", "identf"], ["PS6"],
                 lambda e: e.transpose(PS[:, 6, 0:48], L3[0:48, :], identf[0:48, 0:48]))
            S.op("dve", ["PS6"], ["prm"], lambda e: e.tensor_copy(prm[:], PS[:, 6, 0:48]))
            are, aim, ldt = prm[:, 0:16], prm[:, 16:32], prm[:, 32:48]
            dt, ard, th, mag, yv, lbr, lbi, xr, den, fre, fim, ta, tb = [sm[:, i, :] for i in range(13)]
            S.op("act", ["prm"], ["dt"], lambda e: e.activation(out=dt, in_=ldt, func=AF.Exp))

            def c1(e):
                e.tensor_tensor(out=ard, in0=are, in1=dt, op=ALU.mult)
                e.tensor_tensor(out=th, in0=aim, in1=dt, op=ALU.mult)
                return e.tensor_scalar(out=yv, in0=th, scalar1=1.0 / (2 * math.pi), scalar2=None, op0=ALU.mult)
            S.op("dve", ["prm", "dt"], ["c1"], c1, chain=True)
            S.op("act", ["c1"], ["mag"], lambda e: e.activation(out=mag, in_=ard, func=AF.Exp))
            kki = sbt(tmp, "kki", [128, 128], mybir.dt.int32)
            kk = sbt(tmp, "kk", [128, 128])
            ang = sbt(tmp, "ang", [128, 16, 128])
            ki = sbt(tmp, "ki", [128, 16, 128], mybir.dt.int32)
            kf = sbt(tmp, "kf", [128, 16, 128])
            S.op("pool", [], ["kki"], lambda e: e.iota(kki[:], pattern=[[1, 128]], base=1, channel_multiplier=0))
            S.op("dve", ["kki"], ["kk"], lambda e: e.tensor_copy(kk[:], kki[:]))

            def c2(e):
                e.tensor_tensor(out=ang[:], in0=yv.unsqueeze(2).to_broadcast([128, 16, 128]),
                                in1=kk[:].unsqueeze(1).to_broadcast([128, 16, 128]), op=ALU.mult)
                e.tensor_copy(ki[:], ang[:])
                e.tensor_copy(kf[:], ki[:])
                return e.tensor_tensor(out=ang[:], in0=ang[:], in1=kf[:], op=ALU.subtract)
            S.op("dve", ["c1", "kk"], ["ang"], c2, chain=True)
            S.op("act", ["ang"], ["sinT"], lambda e: e.activation(out=sinT[:], in_=ang[:], func=AF.Sin, scale=TWO_PI))

            def c3(e):
                e.tensor_scalar(out=ang[:], in0=ang[:], scalar1=0.25, scalar2=None, op0=ALU.add)
                e.tensor_scalar(out=kf[:], in0=ang[:], scalar1=0.5, scalar2=None, op0=ALU.is_gt)
                return e.tensor_tensor(out=ang[:], in0=ang[:], in1=kf[:], op=ALU.subtract)
            S.op("dve", ["ang", "sinT"], ["ang2"], c3, chain=True)
            S.op("act", ["ang2"], ["cosT"], lambda e: e.activation(out=cosT[:], in_=ang[:], func=AF.Sin, scale=TWO_PI))
            barrier()
            tmp.close()
            tmp = contextlib.ExitStack()
            braw = sbt(tmp, "braw", [128, 2, 16, 16])
            bb = sbt(tmp, "bb", [128, 2, 16, 16])
            tbb = sbt(tmp, "tbb", [128, 2, 16, 16])
            S.dma("sp", [], ["braw0"], braw[:, 0], b_re_d.rearrange("(j q) c -> q j c", q=128))
            S.dma("sp", [], ["braw1"], braw[:, 1], b_im_d.rearrange("(j q) c -> q j c", q=128))

            def c4(e):
                e.tensor_tensor(out=lbr, in0=mag, in1=cosT[:, :, 0], op=ALU.mult)
                e.tensor_tensor(out=lbi, in0=mag, in1=sinT[:, :, 0], op=ALU.mult)
                e.tensor_scalar(out=xr, in0=lbr, scalar1=-1.0, scalar2=None, op0=ALU.add)
                e.tensor_tensor(out=den, in0=are, in1=are, op=ALU.mult)
                e.tensor_tensor(out=ta, in0=aim, in1=aim, op=ALU.mult)
                e.tensor_tensor(out=den, in0=den, in1=ta, op=ALU.add)
                e.reciprocal(den, den)
                e.tensor_tensor(out=fre, in0=xr, in1=are, op=ALU.mult)
                e.tensor_tensor(out=ta, in0=lbi, in1=aim, op=ALU.mult)
                e.tensor_tensor(out=fre, in0=fre, in1=ta, op=ALU.add)
                e.tensor_tensor(out=fre, in0=fre, in1=den, op=ALU.mult)
                e.tensor_tensor(out=fim, in0=lbi, in1=are, op=ALU.mult)
                e.tensor_tensor(out=ta, in0=xr, in1=aim, op=ALU.mult)
                e.tensor_tensor(out=fim, in0=fim, in1=ta, op=ALU.subtract)
                e.tensor_tensor(out=fim, in0=fim, in1=den, op=ALU.mult)
                frb = fre.unsqueeze(2).to_broadcast([128, 16, 16])
                fib = fim.unsqueeze(2).to_broadcast([128, 16, 16])
                e.tensor_tensor(out=bb[:, 0], in0=braw[:, 0], in1=frb, op=ALU.mult)
                e.tensor_tensor(out=tbb[:, 0], in0=braw[:, 1], in1=fib, op=ALU.mult)
                e.tensor_tensor(out=bb[:, 0], in0=bb[:, 0], in1=tbb[:, 0], op=ALU.subtract)
                e.tensor_tensor(out=bb[:, 1], in0=braw[:, 1], in1=frb, op=ALU.mult)
                e.tensor_tensor(out=tbb[:, 1], in0=braw[:, 0], in1=fib, op=ALU.mult)
                return e.tensor_tensor(out=bb[:, 1], in0=bb[:, 1], in1=tbb[:, 1], op=ALU.add)
            S.op("dve", ["mag", "cosT", "sinT", "prm", "braw0", "braw1"], ["bb"], c4, chain=True)
            E = sbt(tmp, "Eexp", [128, 2, 16, 128])
            S.op("pool", [], ["E0"], lambda e: e.memset(E[:], 0.0))

            def c5(e):
                for arr in range(2):
                    for ee in range(2):
                        dst = _ap(E[64 * ee:64 * ee + 64], arr * 2048 + ee * 16, [[512, 4], [160, 4], [1, 16]])
                        src = _ap(bb[64 * ee:64 * ee + 64], arr * 256, [[64, 4], [16, 4], [1, 16]])
                        last = e.tensor_copy(dst, src)
                return last
            S.op("dve", ["bb", "E0"], ["E"], c5)
            for arr in range(2):
                for jq in range(4):
                    bank = 6 + (arr * 4 + jq) % 2
                    pb = PS[:, bank, :].rearrange("p (x q) -> p x q", q=128)

                    def trE(e, arr=arr, jq=jq, pb=pb):
                        for x in range(4):
                            last = e.transpose(pb[:, x, :], E[:, arr, 4 * jq + x, :], identf[:])
                        return last
                    S.op("pe", ["E", "identf"], ["PS%d" % bank], trE)
                    S.op("act", ["PS%d" % bank], ["Bmat"], lambda e, arr=arr, jq=jq, pb=pb: e.activation(
                        out=Bmat[:, arr, 4 * jq:4 * jq + 4, :], in_=pb, func=AF.Copy))
            S.op("pool", [], ["C0"], lambda e: e.memset(Cmat[:], 0.0))
            XR = Rot(nc, tmp, "Xc", [128, 128], F32, 2)
            for arr, cd in enumerate((c_re_d, c_im_d)):
                for a in range(4):
                    X, Xk = XR.next()
                    S.dma("sp", [], [Xk + "a"], X[:, 0:64], cd[128 * a:128 * (a + 1), :])
                    S.dma("sp", [], [Xk + "b"], X[:, 64:128], cd[128 * a:128 * (a + 1), :])
                    bank = 6 + (arr * 4 + a) % 2
                    S.op("pe", [Xk + "a", Xk + "b", "identf"], ["PS%d" % bank],
                         lambda e, X=X, bank=bank: e.transpose(PS[:, bank, 0:128], X[:], identf[:]))
                    for ee in range(2):
                        dst = _ap(Cmat[64 * ee:64 * ee + 64], arr * 2048 + 4 * a * 128 + ee * 16, [[160, 4], [1, 16]])
                        src = _ap(PS[64 * ee:64 * ee + 64, bank, :], ee * 16, [[32, 4], [1, 16]])
                        S.op("act", ["PS%d" % bank, "C0"], ["Cmat"], lambda e, dst=dst, src=src, arr=arr: e.activation(
                            out=dst, in_=src, func=AF.Identity, scale=(1.0 if arr == 0 else -1.0)))
            DB8 = sbt(tmp, "DB8", [8, 128])
            S.dma("sp", [], ["DB8a"], DB8[0:4, :], d_skip_d[:, :])
            S.dma("sp", [], ["DB8b"], DB8[4:8, :], b_glu_d[:, :])
            S.op("pe", ["DB8a", "DB8b", "identf"], ["PS6"],
                 lambda e: e.transpose(PS[:, 6, 0:8], DB8[0:8, :], identf[0:8, 0:8]))
            S.op("dve", ["PS6"], ["dbT"], lambda e: e.tensor_copy(dbT[:], PS[:, 6, 0:8]))

            def c6(e):
                for c in range(4):
                    last = e.tensor_scalar(out=Dmat[:, c, :], in0=identf[:], scalar1=dbT[:, c:c + 1], scalar2=None,
                                           op0=ALU.mult)
                return last
            S.op("dve", ["dbT", "identf"], ["Dmat"], c6)
            S.dma("pool", [], ["wglu"], wglu[:], w_glu.rearrange("(k p) n -> p k n", p=128))
            barrier()
            tmp.close()
            P.update(mag=mag, cosT=cosT, sinT=sinT, Bmat=Bmat, Cmat=Cmat, Dmat=Dmat, dbT=dbT, wglu=wglu, sm=sm)
            return P

        def s5_pass(P, uT, carry, final, wk):
            mag, cosT, sinT = P["mag"], P["cosT"], P["sinT"]
            tt, zre, zim, Sre, Sim, yg, ygb, sg, cl, bimS = wk
            wre, wim = tt[:, 0], tt[:, 2]
            for n in range(NT):
                for hh in range(2):
                    j0 = 8 * hh
                    Bre = PS[:, 0:2, :].rearrange("p b (x q) -> p (b x) q", q=128)
                    Bim = PS[:, 2:4, :].rearrange("p b (x q) -> p (b x) q", q=128)

                    def bu(e):
                        for arr, dstp in ((0, Bre), (1, Bim)):
                            for jj in range(8):
                                j = j0 + jj
                                last = e.matmul(dstp[:, jj, :], P["Bmat"][:, arr, j, :], uT[:, j // 4, 128 * n:128 * (n + 1)],
                                                start=True, stop=True)
                        return last
                    S.op("pe", ["uT%d" % n, "Bmat"], ["PS0", "PS1", "PS2", "PS3"], bu)
                    cs_ = cosT[:, j0:j0 + 8, :]
                    sn_ = sinT[:, j0:j0 + 8, :]

                    S.op("act", ["PS2", "PS3"], ["bimS"], lambda e: e.activation(out=bimS[:], in_=Bim, func=AF.Copy))

                    def rot_in(e):
                        e.tensor_tensor(out=tt[:, 0], in0=Bre, in1=cs_, op=ALU.mult)
                        return e.tensor_tensor(out=tt[:, 3], in0=Bre, in1=sn_, op=ALU.mult)
                    S.op("dve", ["PS0", "PS1", "cosT", "sinT"], ["tt0", "tt3"], rot_in)

                    def rot_in_p(e):
                        e.tensor_tensor(out=tt[:, 1], in0=bimS[:], in1=sn_, op=ALU.mult)
                        return e.tensor_tensor(out=tt[:, 2], in0=bimS[:], in1=cs_, op=ALU.mult)
                    S.op("pool", ["bimS", "cosT", "sinT"], ["tt1", "tt2"], rot_in_p)

                    def rot_in2(e):
                        e.tensor_tensor(out=wre, in0=tt[:, 0], in1=tt[:, 1], op=ALU.add)
                        return e.tensor_tensor(out=wim, in0=tt[:, 2], in1=tt[:, 3], op=ALU.subtract)
                    S.op("pool", ["tt0", "tt1", "tt2", "tt3"], ["tt0", "tt2"], rot_in2)

                    def scans(e):
                        for jj in range(8):
                            j = j0 + jj
                            rho = mag[:, j:j + 1].to_broadcast([128, 128])
                            e.tensor_tensor_scan(out=zre[:, jj, :], data0=rho, data1=wre[:, jj, :],
                                                 initial=carry[:, 0, j:j + 1], op0=ALU.mult, op1=ALU.add)
                            last = e.tensor_tensor_scan(out=zim[:, jj, :], data0=rho, data1=wim[:, jj, :],
                                                        initial=carry[:, 1, j:j + 1], op0=ALU.mult, op1=ALU.add)
                        return last
                    S.op("dve", ["tt0", "tt2", "carry", "mag"], ["z"], scans)

                    cL = cosT[:, j0:j0 + 8, 127]
                    sL = sinT[:, j0:j0 + 8, 127]
                    zr = zre[:, :, 127]
                    zi = zim[:, :, 127]

                    def carry_a(e):
                        e.tensor_tensor(out=cl[:, 0, :], in0=cL, in1=zr, op=ALU.mult)
                        e.tensor_tensor(out=cl[:, 1, :], in0=sL, in1=zi, op=ALU.mult)
                        e.tensor_tensor(out=cl[:, 2, :], in0=cL, in1=zi, op=ALU.mult)
                        return e.tensor_tensor(out=cl[:, 3, :], in0=sL, in1=zr, op=ALU.mult)
                    S.op("dve", ["z", "cosT", "sinT"], ["cl"], carry_a)

                    def carry_b(e):
                        e.tensor_tensor(out=carry[:, 0, j0:j0 + 8], in0=cl[:, 0, :], in1=cl[:, 1, :], op=ALU.subtract)
                        return e.tensor_tensor(out=carry[:, 1, j0:j0 + 8], in0=cl[:, 2, :], in1=cl[:, 3, :], op=ALU.add)
                    S.op("dve", ["cl"], ["carry"], carry_b)
                    if final:
                        def rot_out(e):
                            e.tensor_tensor(out=tt[:, 0], in0=zre[:], in1=cs_, op=ALU.mult)
                            return e.tensor_tensor(out=tt[:, 1], in0=zim[:], in1=sn_, op=ALU.mult)
                        S.op("dve", ["z", "cosT", "sinT"], ["tt0", "tt1"], rot_out)

                        def rot_out_b(e):
                            e.tensor_tensor(out=tt[:, 2], in0=zim[:], in1=cs_, op=ALU.mult)
                            return e.tensor_tensor(out=tt[:, 3], in0=zre[:], in1=sn_, op=ALU.mult)
                        S.op("pool", ["z", "cosT", "sinT"], ["tt2", "tt3"], rot_out_b)

                        def rot_out_c(e):
                            e.tensor_tensor(out=Sim[:, j0:j0 + 8, :], in0=tt[:, 2], in1=tt[:, 3], op=ALU.add)
                            return e.tensor_tensor(out=Sre[:, j0:j0 + 8, :], in0=tt[:, 0], in1=tt[:, 1], op=ALU.subtract)
                        S.op("pool", ["tt0", "tt1", "tt2", "tt3"], ["S%d" % hh], rot_out_c)
                if not final:
                    continue
                Y = PS[:, 4, :].rearrange("p (c q) -> p c q", q=128)

                def ymm(e):
                    for c in range(4):
                        for x in range(4):
                            j = 4 * c + x
                            e.matmul(Y[:, c, :], P["Cmat"][:, 0, j, :], Sre[:, j, :], start=(x == 0), stop=False)
                            e.matmul(Y[:, c, :], P["Cmat"][:, 1, j, :], Sim[:, j, :], start=False, stop=False)
                        last = e.matmul(Y[:, c, :], P["Dmat"][:, c, :], uT[:, c, 128 * n:128 * (n + 1)], start=False, stop=True)
                    return last
                S.op("pe", ["S0", "S1", "Cmat", "Dmat", "uT%d" % n], ["PS4"], ymm)
                S.op("act", ["PS4"], ["yg"], lambda e: e.activation(out=yg[:], in_=Y, func=AF.Gelu_apprx_tanh))
                S.op("pool", ["yg"], ["ygb"], lambda e: e.tensor_copy(ygb[:], yg[:]))
                Z = PS[:, 5, :].rearrange("p (c q) -> p c q", q=128)

                def zmm(e):
                    for c2 in range(4):
                        for c in range(4):
                            last = e.matmul(Z[:, c2, :], P["wglu"][:, c, 128 * c2:128 * (c2 + 1)], ygb[:, c, :],
                                            start=(c == 0), stop=(c == 3))
                    return last
                S.op("pe", ["ygb", "wglu"], ["PS5"], zmm)

                def sig(e):
                    for c2 in range(4):
                        last = e.activation(out=sg[:, c2, :], in_=Z[:, c2, :], func=AF.Sigmoid,
                                            bias=P["dbT"][:, 4 + c2:5 + c2])
                    return last
                S.op("act", ["PS5", "dbT"], ["sg"], sig)
                S.op("pool", ["yg", "sg"], ["uT%d" % n], lambda e, n=n: e.tensor_tensor(
                    out=uT[:, :, 128 * n:128 * (n + 1)], in0=yg[:], in1=sg[:], op=ALU.mult))


        def s5_sample(P, ph):
            sm = P["sm"]
            lbr, lbi = sm[:, 5, :], sm[:, 6, :]
            stin = sbt(ph, "stin", [128, 128])
            st0 = sbt(ph, "st0", [128, 4, 2, 16])
            bus = sbt(ph, "bus", [128, 2, 16, 16])
            ssb = sbt(ph, "ssb", [128, 2, 16, 16], BF16)
            cur = sbt(ph, "s5cur", [128, 2, 4, 16])
            tq = sbt(ph, "s5tq", [128, 4, 4, 16])
            sto = sbt(ph, "s5sto", [128, 128])
            ygs = sbt(ph, "ygs", [128, 4, 16])
            ygsb = sbt(ph, "ygsb", [128, 4, 16], BF16)
            sgs = sbt(ph, "sgs", [128, 4, 16])
            S.dma("sp", [], ["stin"], stin[:], st5_d[:, :])
            S.op("pe", ["stin", "identf"], ["PS6"], lambda e: e.transpose(PS[:, 6, 0:128], stin[:], identf[:]))
            S.op("dve", ["PS6"], ["st0"], lambda e: e.tensor_copy(st0[:].rearrange("p s a j -> p (s a j)"), PS[:, 6, 0:128]))
            bup = PS[:, 0, :].rearrange("p (a j t) -> p a j t", a=2, j=16)

            def bu(e):
                for arr in range(2):
                    for j in range(16):
                        last = e.matmul(bup[:, arr, j, :], P["Bmat"][:, arr, j, :], uTs[:, j // 4, :], start=True, stop=True)
                return last
            S.op("pe", ["uTs", "Bmat"], ["PS0"], bu)
            S.op("dve", ["PS0"], ["bus"], lambda e: e.tensor_copy(bus[:], bup))
            S.op("dve", ["st0"], ["s5cur"], lambda e: e.tensor_copy(cur[:], st0[:].rearrange("p s a j -> p a s j")))
            lrb = lbr.unsqueeze(1).to_broadcast([128, 4, 16])
            lib = lbi.unsqueeze(1).to_broadcast([128, 4, 16])

            def rec(e):
                for t in range(4):
                    bre = bus[:, 0].rearrange("p j (s t) -> p s j t", t=4)[:, :, :, t]
                    bim = bus[:, 1].rearrange("p j (s t) -> p s j t", t=4)[:, :, :, t]
                    e.tensor_tensor(out=tq[:, 0], in0=cur[:, 0], in1=lrb, op=ALU.mult)
                    e.tensor_tensor(out=tq[:, 1], in0=cur[:, 1], in1=lib, op=ALU.mult)
                    e.tensor_tensor(out=tq[:, 2], in0=cur[:, 1], in1=lrb, op=ALU.mult)
                    e.tensor_tensor(out=tq[:, 3], in0=cur[:, 0], in1=lib, op=ALU.mult)
                    e.tensor_tensor(out=tq[:, 0], in0=tq[:, 0], in1=tq[:, 1], op=ALU.subtract)
                    e.tensor_tensor(out=tq[:, 2], in0=tq[:, 2], in1=tq[:, 3], op=ALU.add)
                    e.tensor_tensor(out=cur[:, 0], in0=tq[:, 0], in1=bre, op=ALU.add)
                    e.tensor_tensor(out=cur[:, 1], in0=tq[:, 2], in1=bim, op=ALU.add)
                    e.tensor_copy(ssb[:, 0].rearrange("p j (s t) -> p s j t", t=4)[:, :, :, t], cur[:, 0])
                    e.tensor_copy(ssb[:, 1].rearrange("p j (s t) -> p s j t", t=4)[:, :, :, t], cur[:, 1])
            S.op("dve", ["bus", "s5cur", "s5sm"], ["s5cur", "ssb"], rec, chain=True)
            S.op("dve", ["s5cur"], ["s5sto"], lambda e: e.tensor_copy(
                sto[:].rearrange("p (s a j) -> p s a j", s=4, a=2), cur[:].rearrange("p a s j -> p s a j")))
            S.op("pe", ["s5sto", "identf"], ["PS6"], lambda e: e.transpose(PS[:, 6, 0:128], sto[:], identf[:]))
            S.op("dve", ["PS6"], ["stin"], lambda e: e.tensor_copy(stin[:], PS[:, 6, 0:128]))
            S.dma("sp", ["stin"], ["ss5o"], ss5_o[:, :], stin[:])
            Y = PS[:, 4, 0:64].rearrange("p (c t) -> p c t", c=4)

            def ymm(e):
                for c in range(4):
                    for x in range(4):
                        j = 4 * c + x
                        e.matmul(Y[:, c, :], P["Cmat"][:, 0, j, :], ssb[:, 0, j, :], start=(x == 0), stop=False)
                        e.matmul(Y[:, c, :], P["Cmat"][:, 1, j, :], ssb[:, 1, j, :], start=False, stop=False)
                    last = e.matmul(Y[:, c, :], P["Dmat"][:, c, :], uTs[:, c, :], start=False, stop=True)
                return last
            S.op("pe", ["ssb", "Cmat", "Dmat", "uTs"], ["PS4"], ymm)
            S.op("act", ["PS4"], ["ygs"], lambda e: e.activation(out=ygs[:], in_=Y, func=AF.Gelu_apprx_tanh))
            S.op("pool", ["ygs"], ["ygsb"], lambda e: e.tensor_copy(ygsb[:], ygs[:]))
            Z = PS[:, 5, 0:64].rearrange("p (c t) -> p c t", c=4)

            def zmm(e):
                for c2 in range(4):
                    for c in range(4):
                        last = e.matmul(Z[:, c2, :], P["wglu"][:, c, 128 * c2:128 * (c2 + 1)], ygsb[:, c, :],
                                        start=(c == 0), stop=(c == 3))
                return last
            S.op("pe", ["ygsb", "wglu"], ["PS5"], zmm)

            def sig(e):
                for c2 in range(4):
                    last = e.activation(out=sgs[:, c2, :], in_=Z[:, c2, :], func=AF.Sigmoid, bias=P["dbT"][:, 4 + c2:5 + c2])
                return last
            S.op("act", ["PS5", "dbT"], ["sgs"], sig)
            S.op("pool", ["ygs", "sgs"], ["uTs"], lambda e: e.tensor_tensor(out=uTs[:], in0=ygs[:], in1=sgs[:], op=ALU.mult))

        with contextlib.ExitStack() as ph:
            uT = sbt(ph, "uT", [128, 4, T], BF16)
            with contextlib.ExitStack() as ph1:
                h2T = sbt(ph1, "hT1", [128, 8, 2 + T], BF16)
                mk(ph1, ["wch", "qf", "sq", "qn", "qb"], {"wch": 2})
                gb1, gb1k = load_gbc(g_mix[1:2, :])
                for n in range(NT):
                    rms_to_T(xres[:, n, :], "xres%d" % n, gb1, gb1k, h2T, "h1T%d" % n, 2 + 128 * n)
                wu, wuk = load_w(w_in_b[:, 0:512])
                wqc, wqck = load_w(w_in_b[:, 512:1024])
                ub = 0
                for c in range(4):
                    for blk in range(4):
                        bank = ub % 2
                        ub += 1

                        def umm(e, c=c, blk=blk, bank=bank):
                            for k in range(8):
                                last = e.matmul(PS[:, bank, :], wu[:, k, 128 * c:128 * (c + 1)],
                                                h2T[:, k, 2 + 512 * blk:2 + 512 * (blk + 1)], start=(k == 0), stop=(k == 7))
                            return last
                        S.op("pe", [wuk] + ["h1T%d" % (4 * blk + x) for x in range(4)], ["PS%d" % bank], umm)
                        S.op("act", ["PS%d" % bank], ["uT%d" % (4 * blk + x) for x in range(4)],
                             lambda e, c=c, blk=blk, bank=bank: e.activation(
                                 out=uT[:, c, 512 * blk:512 * (blk + 1)], in_=PS[:, bank, :], func=AF.Copy))
                for n in range(NT):
                    ps, psk = proj_tile(wqc, wqck, h2T, "h1T%d" % n, 2 + 128 * n)
                    qn, qnk = qk_post(ps, psk, gqc_bc, "gqc_bc", 1, 0, rope=False)
                    qb, qbk = to_bf16(qn, qnk)
                    S.dma("sp", [qbk], ["QC_%d" % n], QC[128 * n:128 * (n + 1), :], qb[:])
                rms_to_T(xsres[0:16, :], "xsres", gb1, gb1k, hTs, "hTs", 0, m=16)
                usp = PS[:, 0, 0:64].rearrange("p (c t) -> p c t", c=4)

                def umms(e):
                    for c in range(4):
                        for k in range(8):
                            last = e.matmul(usp[:, c, :], wu[:, k, 128 * c:128 * (c + 1)], hTs[:, k, :],
                                            start=(k == 0), stop=(k == 7))
                    return last
                S.op("pe", [wuk, "hTs"], ["PS0"], umms)
                S.op("act", ["PS0"], ["uTs"], lambda e: e.activation(out=uTs[:], in_=usp, func=AF.Copy))
                ps, psk = proj_tile(wqc, wqck, hTs, "hTs", 0, m=16)
                qn, qnk = qk_post(ps, psk, gqc_bc, "gqc_bc", 1, 0, rope=False, m=16)
                S.op("pool", [qnk], ["sqc1"], lambda e, qn=qn: e.tensor_copy(sqc[0:16, 1, :], qn[0:16].rearrange("p h d -> p (h d)")))
                barrier()
            with contextlib.ExitStack() as ph2:
                P5 = s5_setup(ph2)
                with contextlib.ExitStack() as phss:
                    s5_sample(P5, phss)
                    barrier()
                carry = sbt(ph2, "carry", [128, 2, 16])
                cin = sbt(ph2, "cin", [128, 32])
                wk = (sbt(ph2, "s5tt", [128, 4, 8, 128]),
                      sbt(ph2, "zre", [128, 8, 128]), sbt(ph2, "zim", [128, 8, 128]),
                      sbt(ph2, "Sre", [128, 16, 128], BF16), sbt(ph2, "Sim", [128, 16, 128], BF16),
                      sbt(ph2, "yg", [128, 4, 128]), sbt(ph2, "ygb", [128, 4, 128], BF16),
                      sbt(ph2, "sg", [128, 4, 128]), sbt(ph2, "cl", [128, 4, 8]),
                      sbt(ph2, "bimS", [128, 8, 128]))
                S.op("dve", [], ["carry"], lambda e: e.memset(carry[:], 0.0))
                if os.environ.get("DBG_S5PASS1", "1") == "1":
                    s5_pass(P5, uT, carry, False, wk)
                    S.dma("pool", ["carry"], ["ccs2"], cc_src[2][:, 0:32], carry[:].rearrange("p a j -> p (a j)"))
                    S.allgather_pairs(["ccs2"], ["ccd2"], cc_src[2][:, :], cc_dst[2][:, :])
                    S.dma("pool", ["ccd2"], ["cin"], cin[:], cc_dst[2][0:128, 0:32])
                    S.op("dve", ["cin", "flag"], ["carry"], lambda e: e.tensor_scalar(
                        out=carry[:].rearrange("p a j -> p (a j)"), in0=cin[:], scalar1=flag[:, 0:1], scalar2=None,
                        op0=ALU.mult))
                s5_pass(P5, uT, carry, True, wk)
                S.op("pe", ["carry", "identf"], ["PS6"], lambda e: e.transpose(
                    PS[0:32, 6, 0:128], carry[:].rearrange("p a j -> p (a j)"), identf[:]))
                so = sbt(ph2, "s5out", [32, 128])
                S.op("dve", ["PS6"], ["s5out"], lambda e: e.tensor_copy(so[:], PS[0:32, 6, 0:128]))
                S.dma("sp", ["s5out"], ["s5o"], s5_o.rearrange("a j q -> (a j) q"), so[:])
                barrier()
            if stage <= 4:
                return end()
            S.dma("sp", ["uT%d" % n for n in range(NT)], ["MIXall"], MIX[:, :], uT[:].rearrange("p c t -> p (c t)"))
            barrier()
            ph.close()
            mxR = Rot(nc, ph, "mx", [128, 4, 128], BF16, 2)
            h2T = sbt(ph, "h2T1", [128, 8, 2 + T], BF16)
            with contextlib.ExitStack() as ph3:
                mk(ph3, ["wch", "qf", "sq", "qn", "qb", "vf"], {"wch": 2})
                memory_kv(1, ph3)
                barrier()
            sample_attention(1, False)
            with contextlib.ExitStack() as phb:
                mk(phb, ["qu", "qT", "pT", "ou", "mg", "mgT", "rd"])
                load_wout(1, phb)
                gf1, gf1k = load_gbc(g_ffn[1:2, :])
                for n in range(NT):
                    mg, mgk = W["mg"].next()
                    cross_attention(n, mg, mgk)
                    mgT, mgTk = W["mgT"].next()
                    transposes([(mg[:, 512 + k * 128:512 + (k + 1) * 128], mgk + "_512") for k in range(4)],
                               mgT[:, 0:4, :], mgTk)
                    mx, mxk = mxR.next()
                    S.dma("sp", ["MIXall"], [mxk], mx[:], MIX.rearrange("p (c t) -> p c t", c=4)[:, :, 128 * n:128 * (n + 1)])
                    lhs = [(mx[:, k, :], mxk) for k in range(4)] + [(mgT[:, k, :], mgTk) for k in range(4)]
                    out_proj_residual(n, lhs, xres[:, n, :], "xres%d" % n)
                    rms_to_T(xres[:, n, :], "xres%d" % n, gf1, gf1k, h2T, "h2T%d" % n, 2 + 128 * n)
                sample_tail(1, xsres[0:16, :], "xsres", gf1, gf1k)
                barrier()
            with contextlib.ExitStack() as ph4:
                ffn(1, ph4, h2T, final_out=True)
                barrier()

        S.finish("sp")
    return nc


def _consts():
    ident = np.eye(128, dtype=np.float32)
    k = np.arange(128)[:, None]
    q = np.arange(128)[None, :]
    maskb = np.zeros((128, 2, 128), np.float32)
    maskb[:, 0, :] = np.where(k >= q, 0.0, NEGB)
    maskb[:, 1, :] = np.where(k <= q, 0.0, NEGB)
    return ident, maskb


def _rope_table(pos):
    half = 16
    inv = np.exp(-math.log(500000.0) * np.arange(half, dtype=np.float32) / half).astype(np.float32)
    ang = pos.astype(np.float32)[:, None] * inv[None, :]
    return np.concatenate([np.cos(ang), np.sin(ang)], axis=1).astype(np.float32)


def prep(inputs, stage=99):
    ident, maskb = _consts()
    f = lambda k: np.ascontiguousarray(np.asarray(inputs[k], np.float32))
    xp = f("x_prompt")
    shared = {
        "ident": ident, "maskb": maskb,
        "g_mix": f("g_mix"), "g_ffn": f("g_ffn"), "g_mem": f("g_mem"),
        "w_in_a": f("w_in_a")[0], "w_in_b": f("w_in_b")[0],
        "g_q_dil": f("g_q_dil")[0], "g_k_dil": f("g_k_dil")[0],
        "g_q_cross": f("g_q_cross"), "g_k_cross": f("g_k_cross"),
        "w_mem_kv": f("w_mem_kv"), "w_out": f("w_out"), "w_up": f("w_up"),
        "conv_w": f("conv_w"), "conv_b": f("conv_b"), "w_down": f("w_down"),
        "s5_lam_re": f("s5_lam_re").reshape(16, 128), "s5_lam_im": f("s5_lam_im").reshape(16, 128),
        "s5_log_dt": f("s5_log_dt").reshape(16, 2),
        "s5_b_re": f("s5_b_re").reshape(2048, 16), "s5_b_im": f("s5_b_im").reshape(2048, 16),
        "s5_c_re": f("s5_c_re").reshape(512, 64), "s5_c_im": f("s5_c_im").reshape(512, 64),
        "s5_d": f("s5_d").reshape(4, 128), "w_glu": f("w_glu")[0], "b_glu": f("b_glu").reshape(4, 128),
    }
    mem = f("mem_prompt")
    xs = f("x_sample")
    cw = [f("cache_win0_kv")[0], f("cache_win1_kv")[0], f("cache_win2_kv")[0]]
    cmem = f("cache_mem_kv")
    st5 = f("state_s5")[0]
    cst = f("state_ffn_conv")
    g0bias = np.zeros((128, 4), np.float32)
    for t in range(4):
        g0bias[:t, t] = NEGB * SCALE
    biasnew = np.full((16, 2, 16), NEGB * SCALE, np.float32)
    for kr in range(16):
        for qr in range(16):
            if kr // 4 == qr // 4:
                if kr % 4 <= qr % 4:
                    biasnew[kr, 0, qr] = 0.0
                if kr == qr:
                    biasnew[kr, 1, qr] = 0.0
    in_maps = []
    for c in range(8):
        b, hf = c // 2, c % 2
        xkv = np.zeros((TH, D), np.float32)
        if hf == 0:
            xkv[T:] = xp[b, 0:T]
        else:
            xkv[:] = xp[b]
        pos = np.concatenate([np.arange(TH) + (hf * T - T), PAST + (np.arange(128) % 4)])
        m = dict(shared)
        sl = slice(4 * c, 4 * c + 4)
        m.update({
            "xkv": xkv,
            "flag": np.full((128, 1), float(hf), np.float32),
            "ropecs": _rope_table(pos),
            "mem": mem[b],
            "xs": np.ascontiguousarray(xs[sl].reshape(16, D)),
            "cwin0": np.ascontiguousarray(cw[0][sl]), "cwin1": np.ascontiguousarray(cw[1][sl]),
            "cwin2": np.ascontiguousarray(cw[2][sl]),
            "cmem": np.ascontiguousarray(cmem[:, sl]),
            "st5": np.ascontiguousarray(st5[sl].reshape(128, 128)),
            "cst": np.ascontiguousarray(cst[:, sl].reshape(2, 8, 2 * DFF)),
            "g0bias": g0bias, "biasnew": biasnew,
        })
        in_maps.append(m)
    return in_maps


_NC_CACHE = {}


def kernel(**inputs):
    if "nc" not in _NC_CACHE:
        _NC_CACHE["nc"] = build(99)
    nc = _NC_CACHE["nc"]
    in_maps = prep(inputs)
    res = run_bass_kernel_spmd(nc, in_maps, core_ids=list(range(8)))
    R = res.results
    f32 = np.float32
    y_prompt = np.stack([np.concatenate([R[2 * b]["y_p"], R[2 * b + 1]["y_p"]], 0) for b in range(4)]).astype(f32)
    y_sample = np.concatenate([R[c]["y_s"].reshape(4, 4, D) for c in range(8)], 0).astype(f32)
    p_win = [np.stack([R[2 * b + 1]["win%d" % g] for b in range(4)])[None].astype(f32) for g in range(3)]
    p_mem = np.stack([R[2 * b]["memkv"] for b in range(4)], 1).astype(f32)
    p_s5 = np.stack([R[2 * b + 1]["s5o"].reshape(2, 32, 64) for b in range(4)])[None].astype(f32)
    p_conv = np.stack([R[2 * b + 1]["convo"].reshape(2, 2, 2 * DFF) for b in range(4)], 1).astype(f32)
    s_win = [np.concatenate([R[c]["swin%d" % g] for c in range(8)], 0)[None].astype(f32) for g in range(3)]
    s_s5 = np.concatenate([R[c]["ss5"].reshape(4, 2, 32, 64) for c in range(8)], 0)[None].astype(f32)
    s_conv = np.concatenate([R[c]["sconv"].reshape(2, 4, 2, 2 * DFF) for c in range(8)], 1).astype(f32)
    return (y_prompt, y_sample, p_win[0], p_win[1], p_win[2], p_mem, p_s5, p_conv,
            s_win[0], s_win[1], s_win[2], s_s5, s_conv)
```

```python
import contextlib
import math
import os

import numpy as np
import concourse.bass as bass
import concourse.mybir as mybir
from concourse.bass_utils import run_bass_kernel_spmd

F32 = mybir.dt.float32
BF16 = mybir.dt.bfloat16
AF = mybir.ActivationFunctionType
ALU = mybir.AluOpType
AX = mybir.AxisListType

D = 1024
T = 2048
NT = 16
TH = 4096
NTH = 32
DFF = 2816
NFF = 22
SCALE = 128 ** -0.5
EPS = 1e-6
GROUPS = ((128, 1), (512, 4), (2048, 16))
PAST = 16384
NEGB = -30000.0


class Sched:
    def __init__(self, nc, stack, n_dma_sems=32):
        self.nc = nc
        self.engs = {"pe": nc.tensor, "act": nc.scalar, "dve": nc.vector,
                     "pool": nc.gpsimd, "sp": nc.sync}
        self.sem, self.cnt = {}, {}
        for k in self.engs:
            self.sem[k] = stack.enter_context(nc.semaphore("s_" + k))
            self.cnt[k] = 0
        self.dsem = [stack.enter_context(nc.semaphore("d_%d" % i)) for i in range(n_dma_sems)]
        self.dcnt = [0] * n_dma_sems
        self.dnext = 0
        self.dnext_sw = 0
        self.ccsem = stack.enter_context(nc.semaphore("s_cc"))
        self.cccnt = 0
        self.waited = {k: {} for k in self.engs}
        self.last_w = {}
        self.readers = {}

    def _semobj(self, key):
        if key == "cc":
            return self.ccsem
        return self.sem[key] if isinstance(key, str) else self.dsem[key]

    def _wait(self, engname, key, val):
        w = self.waited[engname]
        if w.get(key, 0) >= val:
            return
        self.engs[engname].wait_ge(self._semobj(key), val)
        w[key] = val

    def _deps(self, reads, writes):
        deps = {}

        def add(d, raw):
            if d is None:
                return
            k, v = d
            o = deps.get(k)
            if o is None or o[0] < v:
                deps[k] = (v, raw or (o[1] if o else False))
            elif raw:
                deps[k] = (o[0], True)

        for b in reads:
            add(self.last_w.get(b), True)
        for b in writes:
            add(self.last_w.get(b), False)
            for k, v in self.readers.get(b, {}).items():
                add((k, v), False)
        return deps

    def _commit(self, reads, writes, key, val):
        for b in reads:
            self.readers.setdefault(b, {})[key] = val
        for b in writes:
            self.last_w[b] = (key, val)
            self.readers[b] = {}

    def op(self, engname, reads, writes, fn, chain=False, lag=1):
        deps = self._deps(reads, writes)
        for k, (v, raw) in deps.items():
            if k == engname and (engname == "pe" or (not raw and engname != "pool")):
                continue
            self._wait(engname, k, v)
        if chain:
            fn(_Chain(self, engname, lag))
        else:
            last = fn(self.engs[engname])
            self.cnt[engname] += 1
            last.then_inc(self.sem[engname], 1)
        self._commit(reads, writes, engname, self.cnt[engname])

    def dma(self, issuer, reads, writes, out, in_, **kw):
        deps = self._deps(reads, writes)
        nh = len(self.dsem) - 8
        if issuer == "pool":
            i = nh + self.dnext_sw
            self.dnext_sw = (self.dnext_sw + 1) % 8
        else:
            i = self.dnext
            self.dnext = (self.dnext + 1) % nh
        if self.dcnt[i] > 0:
            o = deps.get(i)
            if o is None or o[0] < self.dcnt[i]:
                deps[i] = (self.dcnt[i], True)
        for k, (v, raw) in deps.items():
            self._wait(issuer, k, v)
        self.dcnt[i] += 16
        self.engs[issuer].dma_start(out=out, in_=in_, **kw).then_inc(self.dsem[i], 16)
        self._commit(reads, writes, i, self.dcnt[i])

    def allgather_pairs(self, reads, writes, src, dst):
        deps = self._deps(reads, writes)
        for k, (v, raw) in deps.items():
            self._wait("pool", k, v)
        self.nc.gpsimd.collective_compute(
            "AllGather", ALU.bypass, replica_groups=[[2 * i, 2 * i + 1] for i in range(int(os.environ.get("DBG_NCORES", "8")) // 2)],
            ins=[src], outs=[dst]).then_inc(self.ccsem)
        self.cccnt += 1
        self._commit(reads, writes, "cc", self.cccnt)

    def finish(self, engname="sp"):
        for i in range(len(self.dsem)):
            if self.dcnt[i] > 0:
                self._wait(engname, i, self.dcnt[i])
        for k in self.engs:
            if k != engname and self.cnt[k] > 0:
                self._wait(engname, k, self.cnt[k])


class _Chain:
    def __init__(self, sched, engname, lag=1):
        self.s, self.n, self.lag, self.k, self.base = sched, engname, lag, 0, None

    def __getattr__(self, name):
        real = getattr(self.s.engs[self.n], name)
        s, n = self.s, self.n

        def call(*a, **kw):
            if self.base is None:
                self.base = s.cnt[n]
            if self.k >= self.lag:
                s._wait(n, n, self.base + self.k - self.lag + 1)
            self.k += 1
            ins = real(*a, **kw)
            s.cnt[n] += 1
            ins.then_inc(s.sem[n], 1)
            return ins
        return call


class Rot:
    def __init__(self, nc, stack, name, shape, dt, n):
        self.t = [stack.enter_context(nc.sbuf_tensor("%s_%d" % (name, i), list(shape), dt)) for i in range(n)]
        self.k = ["%s_%d" % (name, i) for i in range(n)]
        self.i = 0

    def next(self):
        j = self.i
        self.i = (self.i + 1) % len(self.t)
        return self.t[j], self.k[j]


def _ap(base, extra_off, dims):
    return bass.AP(base.tensor, base.offset + extra_off, [list(base.ap[0])] + [list(d) for d in dims])


def build(stage=99):
    nc = bass.Bass("TRN2", target_bir_lowering=False)

    def din(name, shape, dt=F32):
        return nc.dram_tensor(name, list(shape), dt, kind="ExternalInput").ap()

    def dout(name, shape, dt=F32):
        return nc.dram_tensor(name, list(shape), dt, kind="ExternalOutput").ap()

    def dscr(name, shape, dt):
        return nc.dram_tensor(name, list(shape), dt, kind="Internal").ap()

    xkv = din("xkv", [TH, D])
    flag_d = din("flag", [128, 1])
    ident_d = din("ident", [128, 128])
    maskb_d = din("maskb", [128, 2, 128])
    ropecs_d = din("ropecs", [TH + 128, 32])
    mem_d = din("mem", [256, D])
    g_mix = din("g_mix", [2, D])
    g_ffn = din("g_ffn", [2, D])
    g_mem = din("g_mem", [2, D])
    w_in_a = din("w_in_a", [D, 5120])
    w_in_b = din("w_in_b", [D, 1024])
    g_q_dil = din("g_q_dil", [3, 128])
    g_k_dil = din("g_k_dil", [3, 128])
    g_q_cross = din("g_q_cross", [2, 128])
    g_k_cross = din("g_k_cross", [2, 128])
    w_mem_kv = din("w_mem_kv", [2, D, 1024])
    w_out = din("w_out", [2, D, D])
    w_up = din("w_up", [2, D, 2 * DFF])
    conv_w = din("conv_w", [2, 3, 2 * DFF])
    conv_b = din("conv_b", [2, 2 * DFF])
    w_down = din("w_down", [2, DFF, D])
    lam_re_d = din("s5_lam_re", [16, 128])
    lam_im_d = din("s5_lam_im", [16, 128])
    log_dt_d = din("s5_log_dt", [16, 2])
    b_re_d = din("s5_b_re", [2048, 16])
    b_im_d = din("s5_b_im", [2048, 16])
    c_re_d = din("s5_c_re", [512, 64])
    c_im_d = din("s5_c_im", [512, 64])
    d_skip_d = din("s5_d", [4, 128])
    w_glu = din("w_glu", [512, 512])
    b_glu_d = din("b_glu", [4, 128])

    xs_d = din("xs", [16, D])
    cwin_d = [din("cwin%d" % g, [4, GROUPS[g][0], 2, 4, 128]) for g in range(3)]
    cmem_d = din("cmem", [2, 4, 256, 2, 4, 128])
    st5_d = din("st5", [128, 128])
    cst_d = din("cst", [2, 8, 2 * DFF])
    g0bias_d = din("g0bias", [128, 4])
    biasnew_d = din("biasnew", [16, 2, 16])

    ys_o = dout("y_s", [16, D])
    swin_o = [dout("swin%d" % g, [4, GROUPS[g][0], 2, 4, 128]) for g in range(3)]
    ss5_o = dout("ss5", [128, 128])
    sconv_o = dout("sconv", [2, 352, 128])
    y_o = dout("y_p", [T, D])
    win_o = [dout("win%d" % g, [GROUPS[g][0], 2, 4, 128]) for g in range(3)]
    memkv_o = dout("memkv", [2, 256, 2, 4, 128])
    s5_o = dout("s5o", [2, 16, 128])
    conv_o = dout("convo", [2, 88, 128])
    dbg = dout("dbg", [T, D]) if stage < 50 else None

    Qs = [dscr("Qs%d" % g, [T, 512], BF16) for g in range(3)]
    Ks = [dscr("Ks%d" % g, [TH, 512], BF16) for g in range(3)]
    Vs = [dscr("Vs%d" % g, [TH, 520], BF16) for g in range(3)]
    NUM = [dscr("NUM%d" % g, [T, 516], F32) for g in range(3)]
    QC = dscr("QC", [T, 512], BF16)
    MIX = dscr("MIX", [128, 4 * T], BF16)
    cc_w = [16, 16, 32]
    cc_src = [dscr("cc_src%d" % i, [128, cc_w[i]], F32) for i in range(3)]
    cc_dst = [dscr("cc_dst%d" % i, [256, cc_w[i]], F32) for i in range(3)]

    with contextlib.ExitStack() as st:
        S = Sched(nc, st)

        def sbt(stack, name, shape, dt=F32):
            return stack.enter_context(nc.sbuf_tensor(name, list(shape), dt))

        def sb(name, shape, dt=F32):
            return sbt(st, name, shape, dt)

        PS = st.enter_context(nc.psum_tensor("PS", [128, 8, 512], F32))

        def barrier():
            for e in S.engs:
                for i in range(len(S.dsem)):
                    if S.dcnt[i] > 0:
                        S._wait(e, i, S.dcnt[i])
                if S.cccnt > 0:
                    S._wait(e, "cc", S.cccnt)
                for k in S.engs:
                    if k != e and S.cnt[k] > 0:
                        S._wait(e, k, S.cnt[k])

        def end(dump=None):
            S.finish("sp")
            return nc

        identf = sb("identf", [128, 128])
        identb = sb("identb", [128, 128], BF16)
        maskf = sb("maskf", [128, 2, 128])
        maskb = sb("maskb_sb", [128, 2, 128], BF16)
        flag = sb("flag_sb", [128, 1])
        ones1 = sb("ones1", [128, 1])
        epsT = sb("epsT", [128, 1])
        ropecs = sb("ropecs_sb", [128, NTH + 1, 32])
        gq_bc = sb("gq_bc", [128, 3, 128])
        gk_bc = sb("gk_bc", [128, 3, 128])
        gqc_bc = sb("gqc_bc", [128, 2, 128])
        gkc_bc = sb("gkc_bc", [128, 2, 128])
        S.dma("sp", [], ["identf"], identf[:], ident_d[:, :])
        S.dma("sp", [], ["maskf"], maskf[:], maskb_d[:, :, :])
        S.dma("sp", [], ["flag"], flag[:], flag_d[:, :])
        S.dma("sp", [], ["ropecs"], ropecs[:], ropecs_d.rearrange("(n p) c -> p n c", p=128))

        def bc_load(dst, key, src2d):
            S.dma("sp", [], [key], dst[:], src2d.rearrange("g d -> (g d)").unsqueeze(0).partition_broadcast(128))
        bc_load(gq_bc, "gq_bc", g_q_dil)
        bc_load(gk_bc, "gk_bc", g_k_dil)
        bc_load(gqc_bc, "gqc_bc", g_q_cross)
        bc_load(gkc_bc, "gkc_bc", g_k_cross)
        S.op("dve", ["identf"], ["identb"], lambda e: e.tensor_copy(identb[:], identf[:]))
        S.op("dve", ["maskf"], ["maskb"], lambda e: e.tensor_copy(maskb[:], maskf[:]))
        S.op("dve", [], ["ones1"], lambda e: e.memset(ones1[:], 1.0))
        S.op("dve", [], ["epsT"], lambda e: e.memset(epsT[:], EPS))

        gbcR = Rot(nc, st, "gbc", [128, D], F32, 2)

        def load_gbc(src_row):
            t, k = gbcR.next()
            S.dma("sp", [], [k], t[:], src_row.partition_broadcast(128))
            return t, k

        ssR = Rot(nc, st, "ss", [128, 4], F32, 4)
        sdR = Rot(nc, st, "sd", [128, 4], F32, 4)
        rsR = Rot(nc, st, "rs", [128, 4], F32, 4)
        hbR = Rot(nc, st, "hb", [128, D], BF16, 2)
        xinR = Rot(nc, st, "xin", [128, D], F32, 2)
        SPECS = {
            "wch": ([128, 8, 512], BF16, 4), "qf": ([128, 4, 128], F32, 2), "sq": ([128, 4, 128], F32, 2),
            "qn": ([128, 4, 128], F32, 2), "rt": ([128, 4, 4, 16], F32, 2), "qb": ([128, 512], BF16, 2),
            "vf": ([128, 512], F32, 2), "v1": ([128, 4, 130], BF16, 3), "qu": ([128, 512], BF16, 2),
            "qT": ([128, 4, 128], BF16, 2), "pT": ([128, 4, 2, 128], BF16, 2), "ou": ([128, 4, 129], F32, 2),
            "mg": ([128, D], BF16, 2), "mgT": ([128, 8, 128], BF16, 2), "numt": ([128, 3, 516], F32, 2),
            "rd": ([128, 8], F32, 2),
        }
        W = {}
        uid = [0]

        def mk(stack, names, counts=None):
            uid[0] += 1
            for nm in names:
                shp, dt, n = SPECS[nm]
                if counts and nm in counts:
                    n = counts[nm]
                W[nm] = Rot(nc, stack, "%s%d" % (nm, uid[0]), shp, dt, n)

        def rstd_of(ss, ssk, ncol, inv_n):
            sd, sdk = sdR.next()
            rs, rsk = rsR.next()
            S.op("act", [ssk, "epsT"], [sdk], lambda e: e.activation(
                out=sd[:, 0:ncol], in_=ss[:, 0:ncol], func=AF.Sqrt, bias=epsT[:, 0:1], scale=inv_n))
            S.op("dve", [sdk], [rsk], lambda e: e.reciprocal(rs[:, 0:ncol], sd[:, 0:ncol]))
            return rs, rsk

        tp_i = [0]

        def next_tp(banks=(2, 3)):
            b = banks[tp_i[0] % 2]
            tp_i[0] += 1
            return PS[:, b, :].bitcast(BF16).rearrange("p (k n) -> p k n", k=8), "PS%d" % b

        def transposes(srcs, dst_ap, dstk, m=128):
            tp3, tpk = next_tp()

            def tr(e):
                for i, (a, _) in enumerate(srcs):
                    last = e.transpose(tp3[:, i, 0:m], a, identb[0:m, 0:m])
                return last
            S.op("pe", [k for _, k in srcs] + ["identb"], [tpk], tr)
            S.op("act", [tpk], [dstk], lambda e: e.activation(out=dst_ap, in_=tp3[:, 0:len(srcs), 0:m], func=AF.Copy))

        def rms_to_T(xt, xk, gbc, gk, dstT, dstk, col0, m=128):
            ss, ssk = ssR.next()
            hb, hbk = hbR.next()
            S.op("act", [xk], [hbk, ssk], lambda e: e.activation(
                out=hb[0:m], in_=xt, func=AF.Square, accum_out=ss[0:m, 0:1]))
            rs, rsk = rstd_of(ss, ssk, 1, 1.0 / D)
            S.op("dve", [xk, rsk, gk], [hbk], lambda e: e.scalar_tensor_tensor(
                out=hb[0:m], in0=xt, scalar=rs[0:m, 0:1], in1=gbc[0:m], op0=ALU.mult, op1=ALU.mult))
            transposes([(hb[0:m, k * 128:(k + 1) * 128], hbk) for k in range(8)],
                       dstT[:, :, col0:col0 + m], dstk, m)

        pj_i = [0]

        def next_pj(banks=(0, 1)):
            b = banks[pj_i[0] % len(banks)]
            pj_i[0] += 1
            return PS[:, b, :], "PS%d" % b

        def load_w(src_cols):
            w, wk = W["wch"].next()
            S.dma("pool", [], [wk], w[:], src_cols.rearrange("(k p) n -> p k n", p=128))
            return w, wk

        def proj_tile(w, wk, srcT, srck, col0, m=128):
            ps, psk = next_pj()

            def mm(e):
                for k in range(8):
                    last = e.matmul(ps[0:m, :], srcT[:, k, col0:col0 + m], w[:, k, :],
                                    start=(k == 0), stop=(k == 7))
                return last
            S.op("pe", [wk, srck], [psk], mm)
            return ps, psk

        def qk_post(ps, psk, gbc3, gbk, gidx, n_comb, rope=True, m=128):
            qf, qfk = W["qf"].next()
            S.op("act", [psk], [qfk], lambda e: e.activation(
                out=qf[0:m].rearrange("p h d -> p (h d)"), in_=ps[0:m, :], func=AF.Copy))
            sq, sqk = W["sq"].next()
            S.op("pool", [qfk], [sqk], lambda e: e.tensor_tensor(out=sq[0:m], in0=qf[0:m], in1=qf[0:m], op=ALU.mult))
            ss, ssk = ssR.next()
            S.op("dve", [sqk], [ssk], lambda e: e.tensor_reduce(out=ss[0:m, :], in_=sq[0:m], axis=AX.X, op=ALU.add))
            rs, rsk = rstd_of(ss, ssk, 4, 1.0 / 128)
            qn, qnk = W["qn"].next()
            S.op("dve", [qfk, rsk], [qnk], lambda e: e.tensor_tensor(
                out=qn[0:m], in0=qf[0:m], in1=rs[0:m, :].unsqueeze(2).to_broadcast([m, 4, 128]), op=ALU.mult))
            S.op("pool", [qnk, gbk], [qnk], lambda e: e.tensor_tensor(
                out=qn[0:m], in0=qn[0:m], in1=gbc3[0:m, gidx:gidx + 1, :].to_broadcast([m, 4, 128]), op=ALU.mult))
            if rope:
                rt, rtk = W["rt"].next()
                cosb = ropecs[0:m, n_comb, 0:16].unsqueeze(1).to_broadcast([m, 4, 16])
                sinb = ropecs[0:m, n_comb, 16:32].unsqueeze(1).to_broadcast([m, 4, 16])
                x1 = qn[0:m, :, 0:16]
                x2 = qn[0:m, :, 16:32]

                def r1(e):
                    e.tensor_tensor(out=rt[0:m, 0], in0=x1, in1=cosb, op=ALU.mult)
                    e.tensor_tensor(out=rt[0:m, 1], in0=x2, in1=sinb, op=ALU.mult)
                    e.tensor_tensor(out=rt[0:m, 2], in0=x2, in1=cosb, op=ALU.mult)
                    return e.tensor_tensor(out=rt[0:m, 3], in0=x1, in1=sinb, op=ALU.mult)
                S.op("dve", [qnk, "ropecs"], [rtk], r1)

                def r2(e):
                    e.tensor_tensor(out=x1, in0=rt[0:m, 0], in1=rt[0:m, 1], op=ALU.subtract)
                    return e.tensor_tensor(out=x2, in0=rt[0:m, 2], in1=rt[0:m, 3], op=ALU.add)
                S.op("dve", [rtk], [qnk], r2)
            return qn, qnk

        def to_bf16(qn, qnk, m=128):
            qb, qbk = W["qb"].next()
            S.op("act", [qnk], [qbk], lambda e: e.activation(
                out=qb[0:m, :], in_=qn[0:m].rearrange("p h d -> p (h d)"), func=AF.Copy))
            return qb, qbk

        sc_banks = [4, 0]
        sc_i = [0]

        def attn_unit(qT, qTk, kT, kTks, vu, vuks, use_mask):
            sb0 = sc_banks[sc_i[0] % len(sc_banks)]
            sc_i[0] += 1
            sck_ = ["PS%d" % sb0, "PS%d" % (sb0 + 1)]
            sc = PS[:, sb0:sb0 + 2, :].rearrange("p b (x q) -> p (b x) q", q=128)

            def mm(e):
                for h in range(4):
                    for kb in range(2):
                        o = sc[:, h * 2 + kb, :]
                        last = e.matmul(o, kT[:, kb, h, :], qT[:, h, :], start=True, stop=not use_mask)
                        if use_mask:
                            last = e.matmul(o, identb[:], maskb[:, kb, :], start=False, stop=True)
                return last
            S.op("pe", [qTk] + kTks + ["identb", "maskb"], sck_, mm)
            pT, pTk = W["pT"].next()
            S.op("act", sck_, [pTk], lambda e: e.activation(
                out=pT[:].rearrange("p h k q -> p (h k) q"), in_=sc, func=AF.Exp, scale=SCALE))
            ov = PS[:, 6:8, 0:258].rearrange("p b (x c) -> p b x c", c=129)

            def pv(e):
                for h in range(4):
                    for kb in range(2):
                        last = e.matmul(ov[:, h // 2, h % 2, :], pT[:, h, kb, :], vu[:, kb, h, 0:129],
                                        start=(kb == 0), stop=(kb == 1))
                return last
            S.op("pe", [pTk] + vuks, ["PS6", "PS7"], pv)
            ou, ouk = W["ou"].next()
            S.op("dve", ["PS6", "PS7"], [ouk], lambda e: e.tensor_copy(
                ou[:].rearrange("p (b x) c -> p b (x c)", b=2), PS[:, 6:8, 0:258]))
            return ou, ouk

        g0bias = sb("g0bias_sb", [128, 4])
        biasnew = sb("biasnew_sb", [16, 2, 16])
        S.dma("sp", [], ["g0bias"], g0bias[:], g0bias_d[:, :])
        S.dma("sp", [], ["biasnew"], biasnew[:], biasnew_d[:, :, :])
        sqc = sb("sqc", [16, 2, 512])
        smg = sb("smg", [16, D], BF16)
        xsres = sb("xsres", [16, D])
        hTs = sb("hTs", [128, 8, 16], BF16)
        h2Ts = sb("h2Ts", [128, 8, 16], BF16)
        uTs = sb("uTs", [128, 4, 16], BF16)
        phs0 = contextlib.ExitStack()
        sqkv = sbt(phs0, "sqkv", [16, 3, 3, 512])
        pending = []
        for g, (Lg, r) in enumerate(GROUPS):
            for sq_ in range(4):
                r0 = 0
                while r0 < Lg - 4:
                    nr = min(256, Lg - 4 - r0)
                    pending.append((g, sq_, r0, nr))
                    r0 += nr

        def drip(k=1):
            for _ in range(k):
                if pending:
                    g_, s_, r0, nr = pending.pop(0)
                    S.dma("sp", [], ["swc%d_%d_%d" % (g_, s_, r0)], swin_o[g_][s_, r0:r0 + nr], cwin_d[g_][s_, r0 + 4:r0 + 4 + nr])

        with contextlib.ExitStack() as ph:
            hT = sbt(ph, "hT", [128, 8, TH], BF16)
            mk(ph, ["wch", "qf", "sq", "qn", "rt", "qb", "vf", "v1", "qu", "qT", "pT", "ou"])
            kuR = Rot(nc, ph, "ku", [128, 2, 512], BF16, 2)
            vuR = Rot(nc, ph, "vu", [128, 2, 4, 130], BF16, 2)
            kTR = Rot(nc, ph, "kT", [128, 2, 4, 128], BF16, 2)
            gb0, gb0k = load_gbc(g_mix[0:1, :])
            for n in range(NTH):
                xt, xk = xinR.next()
                S.dma("sp", [], [xk], xt[:], xkv[128 * n:128 * (n + 1), :])
                rms_to_T(xt[:], xk, gb0, gb0k, hT, "hT%d" % n, 128 * n)
            xt, xk = xinR.next()
            S.dma("sp", [], [xk], xt[0:16, :], xs_d[:, :])
            rms_to_T(xt[0:16, :], xk, gb0, gb0k, hTs, "hTs", 0, m=16)

            for g, (Lg, r) in enumerate(GROUPS):
                wq, wqk = load_w(w_in_a[:, g * 512:(g + 1) * 512])
                wk_, wkk = load_w(w_in_a[:, 1536 + g * 512:1536 + (g + 1) * 512])
                wv, wvk = load_w(w_in_a[:, 3072 + g * 512:3072 + (g + 1) * 512])
                n_lo = 16 - Lg // 128
                for n in range(n_lo, NTH):
                    own = n >= 16
                    srck = "hT%d" % n
                    in_win = own and (128 * (n - 16) >= T - Lg)
                    wrow = 128 * (n - 16) - (T - Lg)
                    ps, psk = proj_tile(wk_, wkk, hT, srck, 128 * n)
                    qn, qnk = qk_post(ps, psk, gk_bc, "gk_bc", g, n)
                    if in_win:
                        S.dma("sp", [qnk], ["wink%d_%d" % (g, n)], win_o[g][wrow:wrow + 128, 0, :, :], qn[:])
                    qb, qbk = to_bf16(qn, qnk)
                    S.dma("sp", [qbk], ["Ks%d_%d" % (g, n)], Ks[g][128 * n:128 * (n + 1), :], qb[:])
                    ps, psk = proj_tile(wv, wvk, hT, srck, 128 * n)
                    if in_win:
                        vf, vfk = W["vf"].next()
                        S.op("act", [psk], [vfk], lambda e, vf=vf, ps=ps: e.activation(out=vf[:], in_=ps, func=AF.Copy))
                        S.dma("sp", [vfk], ["winv%d_%d" % (g, n)], win_o[g][wrow:wrow + 128, 1, :, :],
                              vf[:].rearrange("p (h d) -> p h d", h=4))
                    v1, v1k = W["v1"].next()
                    fsrc = ones1 if own else flag

                    def vcp(e, v1=v1, ps=ps, fsrc=fsrc):
                        e.tensor_copy(v1[:, :, 0:128], ps.rearrange("p (h d) -> p h d", h=4))
                        return e.tensor_copy(v1[:, :, 128:130], fsrc[:, 0:1].unsqueeze(1).to_broadcast([128, 4, 2]))
                    S.op("dve", [psk, "flag", "ones1"], [v1k], vcp)
                    S.dma("sp", [v1k], ["Vs%d_%d" % (g, n)], Vs[g][128 * n:128 * (n + 1), :],
                          v1[:].rearrange("p h c -> p (h c)"))
                    if own:
                        ps, psk = proj_tile(wq, wqk, hT, srck, 128 * n)
                        qn, qnk = qk_post(ps, psk, gq_bc, "gq_bc", g, n)
                        qb, qbk = to_bf16(qn, qnk)
                        S.dma("sp", [qbk], ["Qs%d_%d" % (g, n - 16)], Qs[g][128 * (n - 16):128 * (n - 15), :], qb[:])

                for qi, (w_, wk2, gb3, gbk3) in enumerate(((wq, wqk, gq_bc, "gq_bc"), (wk_, wkk, gk_bc, "gk_bc"))):
                    ps, psk = proj_tile(w_, wk2, hTs, "hTs", 0, m=16)
                    qn, qnk = qk_post(ps, psk, gb3, gbk3, g, NTH, m=16)
                    S.op("pool", [qnk], ["sqkv%d_%d" % (qi, g)], lambda e, qn=qn, qi=qi, g=g: e.tensor_copy(
                        sqkv[0:16, qi, g, :], qn[0:16].rearrange("p h d -> p (h d)")))
                ps, psk = proj_tile(wv, wvk, hTs, "hTs", 0, m=16)
                S.op("act", [psk], ["sqkv2_%d" % g], lambda e, ps=ps, g=g: e.activation(
                    out=sqkv[0:16, 2, g, :], in_=ps[0:16, :], func=AF.Copy))
                for sq_ in range(4):
                    for kv_ in range(2):
                        S.dma("sp", ["sqkv%d_%d" % (kv_ + 1, g)], ["swn%d_%d_%d" % (g, sq_, kv_)],
                              swin_o[g][sq_, Lg - 4:Lg, kv_, :, :],
                              sqkv[4 * sq_:4 * sq_ + 4, kv_ + 1, g, :].rearrange("p (h d) -> p h d", h=4))
                nblk = 16 // r
                for rho in range(r):
                    for b in range(nblk):
                        q0 = rho + r * 128 * b
                        span = r * 127 + 1
                        qtiles = sorted(set((q0 + r * i) // 128 for i in range(128)))
                        c0 = 2048 + q0
                        p0 = c0 - 128 * r
                        ctiles = sorted(set((c0 + r * i) // 128 for i in range(128)))
                        ptiles = sorted(set((p0 + r * i) // 128 for i in range(128)))
                        drip(1)
                        qu, quk = W["qu"].next()
                        ku, kuk = kuR.next()
                        vu, vuk = vuR.next()
                        S.dma("sp", ["Qs%d_%d" % (g, t) for t in qtiles], [quk], qu[:], Qs[g][q0:q0 + span:r, :])
                        S.dma("sp", ["Ks%d_%d" % (g, t) for t in ptiles], [kuk + "a"], ku[:, 0, :], Ks[g][p0:p0 + span:r, :])
                        S.dma("sp", ["Ks%d_%d" % (g, t) for t in ctiles], [kuk + "b"], ku[:, 1, :], Ks[g][c0:c0 + span:r, :])
                        S.dma("sp", ["Vs%d_%d" % (g, t) for t in ptiles], [vuk + "a"],
                              vu[:, 0].rearrange("p h c -> p (h c)"), Vs[g][p0:p0 + span:r, :])
                        S.dma("sp", ["Vs%d_%d" % (g, t) for t in ctiles], [vuk + "b"],
                              vu[:, 1].rearrange("p h c -> p (h c)"), Vs[g][c0:c0 + span:r, :])
                        qT, qTk = W["qT"].next()
                        kT, kTk = kTR.next()
                        transposes([(qu[:, h * 128:(h + 1) * 128], quk) for h in range(4)], qT[:], qTk)
                        transposes([(ku[:, 0, h * 128:(h + 1) * 128], kuk + "a") for h in range(4)], kT[:, 0], kTk + "a")
                        transposes([(ku[:, 1, h * 128:(h + 1) * 128], kuk + "b") for h in range(4)], kT[:, 1], kTk + "b")
                        ou, ouk = attn_unit(qT, qTk, kT, [kTk + "a", kTk + "b"], vu, [vuk + "a", vuk + "b"], True)
                        S.dma("sp", [ouk], ["NUM%d_%d" % (g, t) for t in qtiles], NUM[g][q0:q0 + span:r, :],
                              ou[:].rearrange("p h c -> p (h c)"))

            wqc, wqck = load_w(w_in_a[:, 4608:5120])
            for n in range(16, NTH):
                ps, psk = proj_tile(wqc, wqck, hT, "hT%d" % n, 128 * n)
                qn, qnk = qk_post(ps, psk, gqc_bc, "gqc_bc", 0, n, rope=False)
                qb, qbk = to_bf16(qn, qnk)
                S.dma("sp", [qbk], ["QC_%d" % (n - 16)], QC[128 * (n - 16):128 * (n - 15), :], qb[:])
            ps, psk = proj_tile(wqc, wqck, hTs, "hTs", 0, m=16)
            qn, qnk = qk_post(ps, psk, gqc_bc, "gqc_bc", 0, NTH, rope=False, m=16)
            S.op("pool", [qnk], ["sqc0"], lambda e, qn=qn: e.tensor_copy(sqc[0:16, 0, :], qn[0:16].rearrange("p h d -> p (h d)")))
            drip(100)
            barrier()


        def sample_attention(layer, with_windows):
            with contextlib.ExitStack() as pa:
                kvR = Rot(nc, pa, "skv%d" % layer, [128, 2, 4, 128], F32, 3)
                prR = Rot(nc, pa, "spr%d" % layer, [128, 4, 128], F32, 2)
                scR = Rot(nc, pa, "ssc%d" % layer, [128, 4], F32, 4)
                ncc = (48 if with_windows else 0) + 32
                lt = sbt(pa, "slt%d" % layer, [128, ncc, 4, 16])
                lt2 = sbt(pa, "slt2%d" % layer, [16, 48 if with_windows else 1, 4, 16])
                selT = sbt(pa, "selT%d" % layer, [16, 16, 128])
                S.op("pool", [], ["lt0"], lambda e: e.memset(lt[:], 0.0))
                S.op("pool", [], ["lt20"], lambda e: e.memset(lt2[:], 0.0))
                S.op("dve", ["identf"], ["selT"], lambda e: e.tensor_copy(
                    selT[:], identf[0:16, 0:16].unsqueeze(2).to_broadcast([16, 16, 128])))
                accm = sbt(pa, "saccm%d" % layer, [16, 516])
                accc = sbt(pa, "saccc%d" % layer, [16, 516])
                S.op("pool", [], ["saccm"], lambda e: e.memset(accm[:], 0.0))
                S.op("pool", [], ["saccc"], lambda e: e.memset(accc[:], 0.0))
                qb_i = [0]
                cb_i = [0]
                cidx = [0, 0]

                def q_bcast(src_ap, srck, row):
                    bank = qb_i[0] % 2
                    qb_i[0] += 1
                    S.op("pe", [srck, "selT"], ["PS%d" % bank], lambda e: e.matmul(
                        PS[:, bank, :], selT[0:16, row, :], src_ap, start=True, stop=True))
                    return PS[:, bank, :].rearrange("p (h d) -> p h d", h=4), "PS%d" % bank

                def combo(qb, qbk, row, K_ap, V_ap, kvk, bias_ap, biask, acc, npart, ltile, which):
                    pr, prk = prR.next()
                    sc_, sck = scR.next()
                    S.op("dve", [qbk] + kvk, [prk], lambda e: e.tensor_tensor(
                        out=pr[0:npart], in0=K_ap, in1=qb[0:npart], op=ALU.mult))
                    S.op("dve", [prk], [sck], lambda e: e.tensor_reduce(
                        out=sc_[0:npart, :], in_=pr[0:npart], axis=AX.X, op=ALU.add))
                    c = cidx[which]
                    cidx[which] += 1
                    dst = ltile[0:npart, c, :, row]
                    ltk = "lt%d_%d" % (which, c)
                    if bias_ap is None:
                        S.op("act", [sck, "lt0", "lt20"], [ltk], lambda e: e.activation(
                            out=dst, in_=sc_[0:npart, :], func=AF.Exp, scale=SCALE))
                    else:
                        S.op("act", [sck, biask, "lt0", "lt20"], [ltk], lambda e: e.activation(
                            out=dst, in_=sc_[0:npart, :], func=AF.Exp, scale=SCALE, bias=bias_ap))
                    bn = 2 + 2 * (cb_i[0] % 2)
                    cb_i[0] += 1
                    num_ps, den_ps = PS[0:16, bn, :], PS[0:16, bn + 1, 0:4]

                    def pe(e):
                        for h in range(4):
                            e.matmul(num_ps[:, h * 128:(h + 1) * 128], ltile[0:npart, c, h, :], V_ap[:, h, :],
                                     start=True, stop=True)
                            last = e.matmul(den_ps[:, h:h + 1], ltile[0:npart, c, h, :], ones1[0:npart, 0:1],
                                            start=True, stop=True)
                        return last
                    S.op("pe", [ltk] + kvk + ["ones1"], ["PS%d" % bn, "PS%d" % (bn + 1)], pe)

                    def addacc(e):
                        e.tensor_tensor(out=acc[0][0:16, 0:512], in0=num_ps, in1=acc[0][0:16, 0:512], op=ALU.add)
                        return e.tensor_tensor(out=acc[0][0:16, 512:516], in0=den_ps, in1=acc[0][0:16, 512:516], op=ALU.add)
                    S.op("dve", ["PS%d" % bn, "PS%d" % (bn + 1), acc[1]], [acc[1]], addacc)

                mix_acc = (accm, "saccm")
                crs_acc = (accc, "saccc")
                for sq_ in range(4):
                    if with_windows:
                        for g, (Lg, r) in enumerate(GROUPS):
                            kv0 = None
                            for t in range(4):
                                row = 4 * sq_ + t
                                qb, qbk = q_bcast(sqkv[0:16, 0, g, :], "sqkv0_%d" % g, row)
                                if g == 0:
                                    if kv0 is None:
                                        kv0 = kvR.next()
                                        S.dma("sp", [], [kv0[1]], kv0[0][:], cwin_d[0][sq_, 0:128])
                                    kv, kvk = kv0
                                    bias_ap, biask = g0bias[:, t:t + 1], "g0bias"
                                else:
                                    kv, kvk = kvR.next()
                                    S.dma("sp", [], [kvk], kv[:], cwin_d[g][sq_, t:t + 127 * r + 1:r])
                                    bias_ap, biask = None, None
                                combo(qb, qbk, row, kv[:, 0], kv[:, 1], [kvk], bias_ap, biask, mix_acc, 128, lt, 0)
                                combo(qb, qbk, row, sqkv[0:16, 1, g, :].rearrange("p (h d) -> p h d", h=4),
                                      sqkv[0:16, 2, g, :].rearrange("p (h d) -> p h d", h=4),
                                      ["sqkv1_%d" % g, "sqkv2_%d" % g], biasnew[0:16, 0 if g == 0 else 1, row:row + 1],
                                      "biasnew", mix_acc, 16, lt2, 1)
                    kvm = [kvR.next() for _ in range(2)]
                    for mt in range(2):
                        S.dma("sp", [], [kvm[mt][1]], kvm[mt][0][:], cmem_d[layer, sq_, 128 * mt:128 * (mt + 1)])
                    for t in range(4):
                        row = 4 * sq_ + t
                        qb, qbk = q_bcast(sqc[0:16, layer, :], "sqc%d" % layer, row)
                        for mt in range(2):
                            combo(qb, qbk, row, kvm[mt][0][:, 0], kvm[mt][0][:, 1], [kvm[mt][1]], None, None,
                                  crs_acc, 128, lt, 0)
                for (acc_t, acck), col0 in ([(mix_acc, 0)] if with_windows else []) + [(crs_acc, 512)]:
                    def nrm(e, acc_t=acc_t, col0=col0):
                        e.reciprocal(acc_t[0:16, 512:516], acc_t[0:16, 512:516])
                        e.tensor_tensor(
                            out=smg[0:16, col0:col0 + 512].rearrange("p (h d) -> p h d", h=4),
                            in0=acc_t[0:16, 0:512].rearrange("p (h d) -> p h d", h=4),
                            in1=acc_t[0:16, 512:516].unsqueeze(2).to_broadcast([16, 4, 128]), op=ALU.mult)
                    S.op("dve", [acck], ["smg_%d" % col0], nrm, chain=True)
                barrier()

        def sample_tail(layer, x_src_ap, x_srck, gf, gfk):
            mgT, mgTk = W["mgT"].next()
            if layer == 0:
                transposes([(smg[0:16, k * 128:(k + 1) * 128], "smg_%d" % (0 if k < 4 else 512)) for k in range(8)],
                           mgT[:, :, 0:16], mgTk, m=16)
                lhs = [(mgT[:, k, 0:16], mgTk) for k in range(8)]
            else:
                transposes([(smg[0:16, 512 + k * 128:512 + (k + 1) * 128], "smg_512") for k in range(4)],
                           mgT[:, 4:8, 0:16], mgTk, m=16)
                lhs = [(uTs[:, k, :], "uTs") for k in range(4)] + [(mgT[:, 4 + k, 0:16], mgTk) for k in range(4)]
            out_proj_residual(0, lhs, x_src_ap, x_srck, dst_ap=xsres[0:16, :], dstk="xsres", m=16)
            rms_to_T(xsres[0:16, :], "xsres", gf, gfk, h2Ts, "h2Ts", 0, m=16)

        sample_attention(0, True)
        phs0.close()
        if stage <= 1:
            return end()

        xres = sb("xres", [128, NT, D])
        KmT = sb("KmT", [128, 2, 4, 128], BF16)
        V1m = sb("V1m", [128, 2, 4, 130], BF16)
        S.op("dve", [], ["V1mones"], lambda e: e.memset(V1m[:], 1.0))

        def memory_kv(layer, ph):
            memT = sbt(ph, "memT%d" % layer, [128, 8, 256], BF16)
            gm, gmk = load_gbc(g_mem[layer:layer + 1, :])
            for mt in range(2):
                xt, xk = xinR.next()
                S.dma("sp", [], [xk], xt[:], mem_d[128 * mt:128 * (mt + 1), :])
                rms_to_T(xt[:], xk, gm, gmk, memT, "memT%d" % mt, 128 * mt)
            DM = int(os.environ.get("DBG_M", "9"))
            if DM <= 1:
                return
            wk_, wkk = load_w(w_mem_kv[layer, :, 0:512])
            wv, wvk = load_w(w_mem_kv[layer, :, 512:1024])
            for mt in range(2):
                ps, psk = proj_tile(wk_, wkk, memT, "memT%d" % mt, 128 * mt)
                qn, qnk = qk_post(ps, psk, gkc_bc, "gkc_bc", layer, 0, rope=False)
                S.dma("sp", [qnk], ["memk%d_%d" % (layer, mt)], memkv_o[layer, 128 * mt:128 * (mt + 1), 0, :, :], qn[:])
                qb, qbk = to_bf16(qn, qnk)
                transposes([(qb[:, h * 128:(h + 1) * 128], qbk) for h in range(4)], KmT[:, mt], "KmT%d" % mt)
                if DM <= 2:
                    continue
                ps, psk = proj_tile(wv, wvk, memT, "memT%d" % mt, 128 * mt)
                vf, vfk = W["vf"].next()
                S.op("act", [psk], [vfk], lambda e, vf=vf, ps=ps: e.activation(out=vf[:], in_=ps, func=AF.Copy))
                S.dma("sp", [vfk], ["memv%d_%d" % (layer, mt)], memkv_o[layer, 128 * mt:128 * (mt + 1), 1, :, :],
                      vf[:].rearrange("p (h d) -> p h d", h=4))

                S.op("act", [psk, "V1mones"], ["V1m%d" % mt], lambda e, mt=mt, ps=ps: e.activation(
                    out=V1m[:, mt, :, 0:128], in_=ps.rearrange("p (h d) -> p h d", h=4), func=AF.Copy))

        def load_wout(layer, stack):
            W["wout"] = sbt(stack, "wout%d" % layer, [128, 8, D], BF16)
            S.dma("pool", [], ["wout"], W["wout"][:], w_out[layer].rearrange("(k p) n -> p k n", p=128))

        def normalize_into(mg, mgk, col0, src, srck, srcap):
            rd, rdk = W["rd"].next()
            S.op("dve", [srck], [rdk], lambda e: e.reciprocal(rd[:, 0:4], srcap[:, :, 128]))
            S.op("dve", [srck, rdk], [mgk + "_%d" % col0], lambda e: e.tensor_tensor(
                out=mg[:, col0:col0 + 512].rearrange("p (h d) -> p h d", h=4), in0=srcap[:, :, 0:128],
                in1=rd[:, 0:4].unsqueeze(2).to_broadcast([128, 4, 128]), op=ALU.mult))

        def cross_attention(n, mg, mgk):
            qu, quk = W["qu"].next()
            S.dma("sp", ["QC_%d" % n], [quk], qu[:], QC[128 * n:128 * (n + 1), :])
            qT, qTk = W["qT"].next()
            transposes([(qu[:, h * 128:(h + 1) * 128], quk) for h in range(4)], qT[:], qTk)
            ou, ouk = attn_unit(qT, qTk, KmT, ["KmT0", "KmT1"], V1m, ["V1m0", "V1m1"], False)
            normalize_into(mg, mgk, 512, ou, ouk, ou[:])

        def out_proj_residual(n, lhs, x_src_ap, x_srck, dst_ap=None, dstk=None, m=128):
            if dst_ap is None:
                dst_ap, dstk = xres[:, n, :], "xres%d" % n

            def mm(e):
                for hf in range(2):
                    for k in range(8):
                        last = e.matmul(PS[0:m, hf, :], lhs[k][0], W["wout"][:, k, hf * 512:(hf + 1) * 512],
                                        start=(k == 0), stop=(k == 7))
                return last
            S.op("pe", sorted(set(k for _, k in lhs)) + ["wout"], ["PS0", "PS1"], mm)
            S.op("dve", ["PS0", "PS1", x_srck], [dstk], lambda e: e.tensor_tensor(
                out=dst_ap, in0=PS[0:m, 0:2, :].rearrange("p a b -> p (a b)"), in1=x_src_ap, op=ALU.add))

        def ffn(layer, ph, h2T, final_out):
            gT = sbt(ph, "gT%d" % layer, [128, NFF, 512], BF16)
            wupR = Rot(nc, ph, "wup%d" % layer, [128, 8, 2, 128], BF16, 3)
            wdnR = Rot(nc, ph, "wdn%d" % layer, [128, D], BF16, 3)
            cR = Rot(nc, ph, "cv%d" % layer, [128, 2, 512], F32, 2)
            saR = Rot(nc, ph, "sa%d" % layer, [128, 512], F32, 2)
            cwl = sbt(ph, "cwl%d" % layer, [44, 4, 128])
            cwT = sbt(ph, "cwT%d" % layer, [128, 4, 44])
            tails = sbt(ph, "tails%d" % layer, [128, 2, 44])
            tlo = sbt(ph, "tlo%d" % layer, [88, 128])
            hx = sbt(ph, "hx%d" % layer, [128, 16], BF16)
            hxf = sbt(ph, "hxf%d" % layer, [128, 32])
            hxr = sbt(ph, "hxr%d" % layer, [128, 32])
            for j in range(3):
                S.dma("sp", [], ["cwl%d" % j], cwl[:, j, :], conv_w[layer, j, :].rearrange("(t p) -> t p", p=128))
            S.dma("sp", [], ["cwl3"], cwl[:, 3, :], conv_b[layer, :].rearrange("(t p) -> t p", p=128))
            cps = PS[:, 7, 0:176].rearrange("p (j t) -> p j t", j=4)

            def ctr(e):
                for j in range(4):
                    last = e.transpose(cps[:, j, :], cwl[:, j, :], identf[0:44, 0:44])
                return last
            S.op("pe", ["cwl0", "cwl1", "cwl2", "cwl3", "identf"], ["PS7"], ctr)
            S.op("dve", ["PS7"], ["cwT"], lambda e: e.tensor_copy(cwT[:], cps))
            ci = layer
            S.op("dve", ["h2T%d" % 15], ["hxf"], lambda e: e.tensor_copy(
                hxf[:, 0:16].rearrange("p (k t) -> p k t", t=2), h2T[:, :, 2 + T - 2:2 + T]))
            S.dma("pool", ["hxf"], ["ccs%d" % ci], cc_src[ci][:, 0:16], hxf[:, 0:16])
            S.allgather_pairs(["ccs%d" % ci], ["ccd%d" % ci], cc_src[ci][:, :], cc_dst[ci][:, :])
            S.dma("pool", ["ccd%d" % ci], ["hxr"], hxr[:, 0:16], cc_dst[ci][0:128, 0:16])
            S.op("dve", ["hxr", "flag"], ["h2Th"], lambda e: e.tensor_scalar(
                out=h2T[:, :, 0:2], in0=hxr[:, 0:16].rearrange("p (k t) -> p k t", t=2),
                scalar1=flag[:, 0:1], scalar2=None, op0=ALU.mult))
            by_kind = {"up": [i for _b in range(5) for i in range(NFF)], "dn": [i for _b in range(5) for i in range(NFF)]}
            issued = {"up": 0, "dn": 0}
            consumed = {"up": 0, "dn": 0}
            handles = {}
            PFD = {"up": 2, "dn": 2}

            def issue_upto():
                for kind in ("up", "dn"):
                    while issued[kind] < len(by_kind[kind]) and issued[kind] <= consumed[kind] + PFD[kind]:
                        n_ = issued[kind]
                        i_ = by_kind[kind][n_]
                        if kind == "up":
                            t_, k_ = wupR.next()
                            S.dma("pool", [], [k_], t_[:, :, 0, :],
                                  w_up[layer, :, 128 * i_:128 * (i_ + 1)].rearrange("(k p) n -> p k n", p=128))
                            S.dma("pool", [], [k_ + "b"], t_[:, :, 1, :],
                                  w_up[layer, :, DFF + 128 * i_:DFF + 128 * (i_ + 1)].rearrange("(k p) n -> p k n", p=128))
                        else:
                            t_, k_ = wdnR.next()
                            S.dma("pool", [], [k_], t_[:], w_down[layer, 128 * i_:128 * (i_ + 1), :])
                        handles[(kind, n_)] = (t_, k_)
                        issued[kind] += 1

            def take(kind):
                issue_upto()
                n_ = consumed[kind]
                consumed[kind] += 1
                return handles.pop((kind, n_))

            up_banks = [(0, 1), (2, 3)]
            upi = 0
            for blk in range(4):
                t0 = 2 + 512 * blk
                hkeys = ["h2T%d" % (4 * blk + j) for j in range(4)] + (["h2Th"] if blk == 0 else ["h2T%d" % (4 * blk - 1)])
                for i in range(NFF):
                    wup, wupk = take("up")
                    ba, bb = up_banks[upi % 2]
                    upi += 1
                    hb_ = PS[:, 4 + i % 2, 0:4].rearrange("p (a t) -> p a t", a=2)

                    def mm(e, wup=wup, ba=ba, bb=bb, hb_=hb_):
                        for ab, bank in ((0, ba), (1, bb)):
                            for k in range(8):
                                e.matmul(PS[:, bank, :], wup[:, k, ab, :], h2T[:, k, t0:t0 + 512],
                                         start=(k == 0), stop=(k == 7))
                            for k in range(8):
                                last = e.matmul(hb_[:, ab, :], wup[:, k, ab, :], h2T[:, k, t0 - 2:t0],
                                                start=(k == 0), stop=(k == 7))
                        return last
                    S.op("pe", [wupk, wupk + "b"] + hkeys, ["PS%d" % ba, "PS%d" % bb, "PS%d" % (4 + i % 2)], mm)
                    cv, cvk = cR.next()

                    def conv(e, cv=cv, ba=ba, bb=bb, hb_=hb_, i=i):
                        hv = ((0, ba), (1, bb))
                        ti_ = lambda ab: ab * NFF + i
                        for ab, bank in hv:
                            e.tensor_scalar(out=cv[:, ab, :], in0=PS[:, bank, :], scalar1=cwT[:, 2, ti_(ab):ti_(ab) + 1],
                                            scalar2=cwT[:, 3, ti_(ab):ti_(ab) + 1], op0=ALU.mult, op1=ALU.add)
                        for ab, bank in hv:
                            e.scalar_tensor_tensor(out=cv[:, ab, 1:512], in0=PS[:, bank, 0:511],
                                                   scalar=cwT[:, 1, ti_(ab):ti_(ab) + 1], in1=cv[:, ab, 1:512],
                                                   op0=ALU.mult, op1=ALU.add)
                        for ab, bank in hv:
                            e.scalar_tensor_tensor(out=cv[:, ab, 2:512], in0=PS[:, bank, 0:510],
                                                   scalar=cwT[:, 0, ti_(ab):ti_(ab) + 1], in1=cv[:, ab, 2:512],
                                                   op0=ALU.mult, op1=ALU.add)
                        for ab, bank in hv:
                            e.scalar_tensor_tensor(out=cv[:, ab, 0:2], in0=hb_[:, ab, :], scalar=cwT[:, 0, ti_(ab):ti_(ab) + 1],
                                                   in1=cv[:, ab, 0:2], op0=ALU.mult, op1=ALU.add)
                        for ab, bank in hv:
                            e.scalar_tensor_tensor(out=cv[:, ab, 0:1], in0=hb_[:, ab, 1:2], scalar=cwT[:, 1, ti_(ab):ti_(ab) + 1],
                                                   in1=cv[:, ab, 0:1], op0=ALU.mult, op1=ALU.add)
                        if blk == 3:
                            for ab, bank in hv:
                                e.tensor_copy(tails[:, :, ti_(ab)], PS[:, bank, 510:512])
                    S.op("dve", ["PS%d" % ba, "PS%d" % bb, "PS%d" % (4 + i % 2), "cwT"],
                         [cvk] + (["tails"] if blk == 3 else []), conv, chain=True, lag=2)
                    sa, sak = saR.next()
                    S.op("act", [cvk], [sak], lambda e, sa=sa, cv=cv: e.activation(out=sa[:], in_=cv[:, 0, :], func=AF.Silu))
                    S.op("pool", [sak, cvk], ["gT%d" % i], lambda e, sa=sa, cv=cv, i=i: e.tensor_tensor(
                        out=gT[:, i, :], in0=sa[:], in1=cv[:, 1, :], op=ALU.mult))
                for i in range(NFF):
                    wdn, wdnk = take("dn")

                    def dn(e, wdn=wdn, i=i):
                        for tt in range(4):
                            for hf in range(2):
                                last = e.matmul(PS[:, 2 * tt + hf, :], gT[:, i, 128 * tt:128 * (tt + 1)],
                                                wdn[:, hf * 512:(hf + 1) * 512], start=(i == 0), stop=(i == NFF - 1))
                        return last
                    S.op("pe", [wdnk, "gT%d" % i], ["PS%d" % b for b in range(8)], dn)
                for tt in range(4):
                    n = 4 * blk + tt
                    S.op("dve", ["PS%d" % (2 * tt), "PS%d" % (2 * tt + 1), "xres%d" % n], ["xres%d" % n],
                         lambda e, tt=tt, n=n: e.tensor_tensor(
                             out=xres[:, n, :], in0=PS[:, 2 * tt:2 * tt + 2, :].rearrange("p a b -> p (a b)"),
                             in1=xres[:, n, :], op=ALU.add))
                    if final_out:
                        S.dma("sp", ["xres%d" % n], ["y%d" % n], y_o[128 * n:128 * (n + 1), :], xres[:, n, :])

            gTs = sbt(ph, "gTs%d" % layer, [128, NFF, 16], BF16)
            cstT = sbt(ph, "cstT%d" % layer, [128, 44, 8])
            tls = sbt(ph, "tls%d" % layer, [128, 8, 44])
            tlso = sbt(ph, "tlso%d" % layer, [128, 3, 128])
            extR = Rot(nc, ph, "ext%d" % layer, [128, 2, 4, 6], F32, 2)
            cvsR = Rot(nc, ph, "cvs%d" % layer, [128, 2, 4, 4], F32, 2)
            sasR = Rot(nc, ph, "sas%d" % layer, [128, 16], F32, 2)
            for q in range(6):
                xt, xk = xinR.next()
                ncol = min(1024, 2 * DFF - 1024 * q)
                S.dma("sp", [], [xk], xt[0:8, 0:ncol], cst_d[layer, :, 1024 * q:1024 * q + ncol])
                ntile = ncol // 128
                cps = PS[:, 6, 0:64].rearrange("p (t c) -> p t c", c=8)

                def ctr2(e, xt=xt, ntile=ntile, cps=cps):
                    for t in range(ntile):
                        last = e.transpose(cps[:, t, :], xt[0:8, 128 * t:128 * (t + 1)], identf[0:8, 0:8])
                    return last
                S.op("pe", [xk, "identf"], ["PS6"], ctr2)
                S.op("dve", ["PS6"], ["cstT"], lambda e, q=q, ntile=ntile, cps=cps: e.tensor_copy(
                    cstT[:, 8 * q:8 * q + ntile, :], cps[:, 0:ntile, :]))
            for i in range(NFF):
                wup, wupk = take("up")
                bank = i % 2
                ups = PS[:, bank, 0:32].rearrange("p (a t) -> p a t", a=2)

                def mms(e, wup=wup, ups=ups):
                    for ab in range(2):
                        for k in range(8):
                            last = e.matmul(ups[:, ab, :], wup[:, k, ab, :], h2Ts[:, k, :], start=(k == 0), stop=(k == 7))
                    return last
                S.op("pe", [wupk, wupk + "b", "h2Ts"], ["PS%d" % bank], mms)
                ext, extk = extR.next()
                cvs, cvsk = cvsR.next()

                def convs(e, ext=ext, cvs=cvs, ups=ups, i=i):
                    for ab in range(2):
                        ti = ab * NFF + i
                        e.tensor_copy(ext[:, ab, :, 0:2], cstT[:, ti, :].rearrange("p (s t) -> p s t", t=2))
                        e.tensor_copy(ext[:, ab, :, 2:6], ups[:, ab, :].rearrange("p (s t) -> p s t", t=4))
                        e.tensor_copy(tls[:, :, ti].rearrange("p (s t) -> p s t", t=2), ext[:, ab, :, 4:6])
                        c = cvs[:, ab]
                        e.tensor_scalar(out=c, in0=ext[:, ab, :, 2:6], scalar1=cwT[:, 2, ti:ti + 1],
                                        scalar2=cwT[:, 3, ti:ti + 1], op0=ALU.mult, op1=ALU.add)
                        e.scalar_tensor_tensor(out=c, in0=ext[:, ab, :, 1:5], scalar=cwT[:, 1, ti:ti + 1], in1=c,
                                               op0=ALU.mult, op1=ALU.add)
                        e.scalar_tensor_tensor(out=c, in0=ext[:, ab, :, 0:4], scalar=cwT[:, 0, ti:ti + 1], in1=c,
                                               op0=ALU.mult, op1=ALU.add)
                S.op("dve", ["PS%d" % bank, "cstT", "cwT"], [extk, cvsk, "tls"], convs, chain=True)
                sas, sask = sasR.next()
                S.op("act", [cvsk], [sask], lambda e, sas=sas, cvs=cvs: e.activation(
                    out=sas[:].rearrange("p (s t) -> p s t", t=4), in_=cvs[:, 0], func=AF.Silu))
                S.op("pool", [sask, cvsk], ["gTs%d" % i], lambda e, sas=sas, cvs=cvs, i=i: e.tensor_tensor(
                    out=gTs[:, i, :].rearrange("p (s t) -> p s t", t=4), in0=sas[:].rearrange("p (s t) -> p s t", t=4),
                    in1=cvs[:, 1], op=ALU.mult))
            for i in range(NFF):
                wdn, wdnk = take("dn")

                def dns(e, wdn=wdn, i=i):
                    for hf in range(2):
                        last = e.matmul(PS[0:16, 2 + hf, :], gTs[:, i, :], wdn[:, hf * 512:(hf + 1) * 512],
                                        start=(i == 0), stop=(i == NFF - 1))
                    return last
                S.op("pe", [wdnk, "gTs%d" % i], ["PS2", "PS3"], dns)
            S.op("dve", ["PS2", "PS3", "xsres"], ["xsres"], lambda e: e.tensor_tensor(
                out=xsres[0:16, :], in0=PS[0:16, 2:4, :].rearrange("p a b -> p (a b)"), in1=xsres[0:16, :], op=ALU.add))
            if final_out:
                S.dma("sp", ["xsres"], ["ys"], ys_o[:, :], xsres[0:16, :])
            tl2 = tls[:].rearrange("p a t -> p (a t)")
            for q in range(3):
                ncol = min(128, 352 - 128 * q)
                S.op("pe", ["tls", "identf"], ["PS7"], lambda e, q=q, ncol=ncol: e.transpose(
                    PS[0:ncol, 7, 0:128], tl2[:, 128 * q:128 * q + ncol], identf[:]))
                S.op("dve", ["PS7"], ["tlso%d" % q], lambda e, q=q, ncol=ncol: e.tensor_copy(
                    tlso[0:ncol, q, :], PS[0:ncol, 7, 0:128]))
                S.dma("sp", ["tlso%d" % q], ["sconv%d_%d" % (layer, q)], sconv_o[layer, 128 * q:128 * q + ncol, :],
                      tlso[0:ncol, q, :])
            tps = PS[:, 7, 0:128]

            def ttr(e):
                return e.transpose(tps[0:88, :], tails[:].rearrange("p t c -> p (t c)"), identf[:])
            S.op("pe", ["tails", "identf"], ["PS7"], ttr)
            S.op("dve", ["PS7"], ["tlo"], lambda e: e.tensor_copy(tlo[:], tps[0:88, :]))
            S.dma("sp", ["tlo"], ["convo%d" % layer], conv_o[layer, :, :], tlo[:])

        with contextlib.ExitStack() as ph:
            h2T = sbt(ph, "h2T0", [128, 8, 2 + T], BF16)
            with contextlib.ExitStack() as ph1:
                mk(ph1, ["wch", "qf", "sq", "qn", "qb", "vf"], {"wch": 2})
                memory_kv(0, ph1)
                barrier()
            phb = contextlib.ExitStack()
            phb.__enter__()
            DB = int(os.environ.get("DBG_B", "9"))
            if DB <= 1:
                return end()
            mk(phb, ["qu", "qT", "pT", "ou", "mg", "mgT", "numt", "rd"])
            load_wout(0, phb)
            gf, gfk = load_gbc(g_ffn[0:1, :])
            for n in range(NT):
                mg, mgk = W["mg"].next()
                nt_, ntk = W["numt"].next()
                for g in range(3):
                    S.dma("sp", ["NUM%d_%d" % (g, n)], [ntk + "_%d" % g], nt_[:, g, :], NUM[g][128 * n:128 * (n + 1), :])
                S.op("pool", [ntk + "_0", ntk + "_1"], [ntk + "_0"], lambda e, nt_=nt_: e.tensor_tensor(
                    out=nt_[:, 0, :], in0=nt_[:, 0, :], in1=nt_[:, 1, :], op=ALU.add))
                S.op("pool", [ntk + "_0", ntk + "_2"], [ntk + "_0"], lambda e, nt_=nt_: e.tensor_tensor(
                    out=nt_[:, 0, :], in0=nt_[:, 0, :], in1=nt_[:, 2, :], op=ALU.add))
                normalize_into(mg, mgk, 0, nt_, ntk + "_0", nt_[:, 0, :].rearrange("p (h c) -> p h c", c=129))
                cross_attention(n, mg, mgk)
                mgT, mgTk = W["mgT"].next()
                transposes([(mg[:, k * 128:(k + 1) * 128], mgk + "_%d" % (0 if k < 4 else 512)) for k in range(8)],
                           mgT[:], mgTk)
                xt, xk = xinR.next()
                S.dma("sp", [], [xk], xt[:], xkv[T + 128 * n:T + 128 * (n + 1), :])
                if DB <= 4:
                    continue
                out_proj_residual(n, [(mgT[:, k, :], mgTk) for k in range(8)], xt[:], xk)
                if DB <= 5:
                    continue
                rms_to_T(xres[:, n, :], "xres%d" % n, gf, gfk, h2T, "h2T%d" % n, 2 + 128 * n)
            if stage <= 2:
                for n in range(NT if DB >= 5 else 0):
                    S.dma("sp", ["xres%d" % n], ["dbg%d" % n], dbg[128 * n:128 * (n + 1), :], xres[:, n, :])
                return end()
            xt, xk = xinR.next()
            S.dma("sp", [], [xk], xt[0:16, :], xs_d[:, :])
            sample_tail(0, xt[0:16, :], xk, gf, gfk)
            barrier()
            phb.close()
            with contextlib.ExitStack() as ph2:
                ffn(0, ph2, h2T, final_out=False)
                barrier()
        if stage <= 3:
            for n in range(NT):
                S.dma("sp", ["xres%d" % n], ["dbg%d" % n], dbg[128 * n:128 * (n + 1), :], xres[:, n, :])
            return end()


        TWO_PI = 6.2831845

        def s5_setup(ph):
            P = {}
            prm = sbt(ph, "prm", [128, 48])
            sm = sbt(ph, "s5sm", [128, 16, 16])
            cosT = sbt(ph, "cosT", [128, 16, 128])
            sinT = sbt(ph, "sinT", [128, 16, 128])
            Bmat = sbt(ph, "Bmat", [128, 2, 16, 128], BF16)
            Cmat = sbt(ph, "Cmat", [128, 2, 16, 128], BF16)
            dbT = sbt(ph, "dbT", [128, 8])
            Dmat = sbt(ph, "Dmat", [128, 4, 128], BF16)
            wglu = sbt(ph, "wglu", [128, 4, 512], BF16)
            tmp = contextlib.ExitStack()
            L3 = sbt(tmp, "L3", [48, 128])
            S.dma("sp", [], ["L3a"], L3[0:16, :], lam_re_d[:, :])
            S.dma("sp", [], ["L3b"], L3[16:32, :], lam_im_d[:, :])
            Lt = sbt(tmp, "Lt", [48, 2])
            S.dma("sp", [], ["Lt"], Lt[32:48, :], log_dt_d[:, :])
            S.op("act", ["Lt"], ["L3c"], lambda e: e.activation(
                out=L3[32:48, :].rearrange("p (e q) -> p e q", e=2),
                in_=Lt[32:48, :].unsqueeze(2).to_broadcast([16, 2, 64]), func=AF.Copy))
            S.op("pe", ["L3a", "L3b", "L3c", "identf"], ["PS6"],
                 lambda e: e.transpose(PS[:, 6, 0:48], L3[0:48, :], identf[0:48, 0:48]))
            S.op("dve", ["PS6"], ["prm"], lambda e: e.tensor_copy(prm[:], PS[:, 6, 0:48]))
            are, aim, ldt = prm[:, 0:16], prm[:, 16:32], prm[:, 32:48]
            dt, ard, th, mag, yv, lbr, lbi, xr, den, fre, fim, ta, tb = [sm[:, i, :] for i in range(13)]
            S.op("act", ["prm"], ["dt"], lambda e: e.activation(out=dt, in_=ldt, func=AF.Exp))

            def c1(e):
                e.tensor_tensor(out=ard, in0=are, in1=dt, op=ALU.mult)
                e.tensor_tensor(out=th, in0=aim, in1=dt, op=ALU.mult)
                return e.tensor_scalar(out=yv, in0=th, scalar1=1.0 / (2 * math.pi), scalar2=None, op0=ALU.mult)
            S.op("dve", ["prm", "dt"], ["c1"], c1, chain=True)
            S.op("act", ["c1"], ["mag"], lambda e: e.activation(out=mag, in_=ard, func=AF.Exp))
            kki = sbt(tmp, "kki", [128, 128], mybir.dt.int32)
            kk = sbt(tmp, "kk", [128, 128])
            ang = sbt(tmp, "ang", [128, 16, 128])
            ki = sbt(tmp, "ki", [128, 16, 128], mybir.dt.int32)
            kf = sbt(tmp, "kf", [128, 16, 128])
            S.op("pool", [], ["kki"], lambda e: e.iota(kki[:], pattern=[[1, 128]], base=1, channel_multiplier=0))
            S.op("dve", ["kki"], ["kk"], lambda e: e.tensor_copy(kk[:], kki[:]))

            def c2(e):
                e.tensor_tensor(out=ang[:], in0=yv.unsqueeze(2).to_broadcast([128, 16, 128]),
                                in1=kk[:].unsqueeze(1).to_broadcast([128, 16, 128]), op=ALU.mult)
                e.tensor_copy(ki[:], ang[:])
                e.tensor_copy(kf[:], ki[:])
                return e.tensor_tensor(out=ang[:], in0=ang[:], in1=kf[:], op=ALU.subtract)
            S.op("dve", ["c1", "kk"], ["ang"], c2, chain=True)
            S.op("act", ["ang"], ["sinT"], lambda e: e.activation(out=sinT[:], in_=ang[:], func=AF.Sin, scale=TWO_PI))

            def c3(e):
                e.tensor_scalar(out=ang[:], in0=ang[:], scalar1=0.25, scalar2=None, op0=ALU.add)
                e.tensor_scalar(out=kf[:], in0=ang[:], scalar1=0.5, scalar2=None, op0=ALU.is_gt)
                return e.tensor_tensor(out=ang[:], in0=ang[:], in1=kf[:], op=ALU.subtract)
            S.op("dve", ["ang", "sinT"], ["ang2"], c3, chain=True)
            S.op("act", ["ang2"], ["cosT"], lambda e: e.activation(out=cosT[:], in_=ang[:], func=AF.Sin, scale=TWO_PI))
            barrier()
            tmp.close()
            tmp = contextlib.ExitStack()
            braw = sbt(tmp, "braw", [128, 2, 16, 16])
            bb = sbt(tmp, "bb", [128, 2, 16, 16])
            tbb = sbt(tmp, "tbb", [128, 2, 16, 16])
            S.dma("sp", [], ["braw0"], braw[:, 0], b_re_d.rearrange("(j q) c -> q j c", q=128))
            S.dma("sp", [], ["braw1"], braw[:, 1], b_im_d.rearrange("(j q) c -> q j c", q=128))

            def c4(e):
                e.tensor_tensor(out=lbr, in0=mag, in1=cosT[:, :, 0], op=ALU.mult)
                e.tensor_tensor(out=lbi, in0=mag, in1=sinT[:, :, 0], op=ALU.mult)
                e.tensor_scalar(out=xr, in0=lbr, scalar1=-1.0, scalar2=None, op0=ALU.add)
                e.tensor_tensor(out=den, in0=are, in1=are, op=ALU.mult)
                e.tensor_tensor(out=ta, in0=aim, in1=aim, op=ALU.mult)
                e.tensor_tensor(out=den, in0=den, in1=ta, op=ALU.add)
                e.reciprocal(den, den)
                e.tensor_tensor(out=fre, in0=xr, in1=are, op=ALU.mult)
                e.tensor_tensor(out=ta, in0=lbi, in1=aim, op=ALU.mult)
                e.tensor_tensor(out=fre, in0=fre, in1=ta, op=ALU.add)
                e.tensor_tensor(out=fre, in0=fre, in1=den, op=ALU.mult)
                e.tensor_tensor(out=fim, in0=lbi, in1=are, op=ALU.mult)
                e.tensor_tensor(out=ta, in0=xr, in1=aim, op=ALU.mult)
                e.tensor_tensor(out=fim, in0=fim, in1=ta, op=ALU.subtract)
                e.tensor_tensor(out=fim, in0=fim, in1=den, op=ALU.mult)
                frb = fre.unsqueeze(2).to_broadcast([128, 16, 16])
                fib = fim.unsqueeze(2).to_broadcast([128, 16, 16])
                e.tensor_tensor(out=bb[:, 0], in0=braw[:, 0], in1=frb, op=ALU.mult)
                e.tensor_tensor(out=tbb[:, 0], in0=braw[:, 1], in1=fib, op=ALU.mult)
                e.tensor_tensor(out=bb[:, 0], in0=bb[:, 0], in1=tbb[:, 0], op=ALU.subtract)
                e.tensor_tensor(out=bb[:, 1], in0=braw[:, 1], in1=frb, op=ALU.mult)
                e.tensor_tensor(out=tbb[:, 1], in0=braw[:, 0], in1=fib, op=ALU.mult)
                return e.tensor_tensor(out=bb[:, 1], in0=bb[:, 1], in1=tbb[:, 1], op=ALU.add)
            S.op("dve", ["mag", "cosT", "sinT", "prm", "braw0", "braw1"], ["bb"], c4, chain=True)
            E = sbt(tmp, "Eexp", [128, 2, 16, 128])
            S.op("pool", [], ["E0"], lambda e: e.memset(E[:], 0.0))

            def c5(e):
                for arr in range(2):
                    for ee in range(2):
                        dst = _ap(E[64 * ee:64 * ee + 64], arr * 2048 + ee * 16, [[512, 4], [160, 4], [1, 16]])
                        src = _ap(bb[64 * ee:64 * ee + 64], arr * 256, [[64, 4], [16, 4], [1, 16]])
                        last = e.tensor_copy(dst, src)
                return last
            S.op("dve", ["bb", "E0"], ["E"], c5)
            for arr in range(2):
                for jq in range(4):
                    bank = 6 + (arr * 4 + jq) % 2
                    pb = PS[:, bank, :].rearrange("p (x q) -> p x q", q=128)

                    def trE(e, arr=arr, jq=jq, pb=pb):
                        for x in range(4):
                            last = e.transpose(pb[:, x, :], E[:, arr, 4 * jq + x, :], identf[:])
                        return last
                    S.op("pe", ["E", "identf"], ["PS%d" % bank], trE)
                    S.op("act", ["PS%d" % bank], ["Bmat"], lambda e, arr=arr, jq=jq, pb=pb: e.activation(
                        out=Bmat[:, arr, 4 * jq:4 * jq + 4, :], in_=pb, func=AF.Copy))
            S.op("pool", [], ["C0"], lambda e: e.memset(Cmat[:], 0.0))
            XR = Rot(nc, tmp, "Xc", [128, 128], F32, 2)
            for arr, cd in enumerate((c_re_d, c_im_d)):
                for a in range(4):
                    X, Xk = XR.next()
                    S.dma("sp", [], [Xk + "a"], X[:, 0:64], cd[128 * a:128 * (a + 1), :])
                    S.dma("sp", [], [Xk + "b"], X[:, 64:128], cd[128 * a:128 * (a + 1), :])
                    bank = 6 + (arr * 4 + a) % 2
                    S.op("pe", [Xk + "a", Xk + "b", "identf"], ["PS%d" % bank],
                         lambda e, X=X, bank=bank: e.transpose(PS[:, bank, 0:128], X[:], identf[:]))
                    for ee in range(2):
                        dst = _ap(Cmat[64 * ee:64 * ee + 64], arr * 2048 + 4 * a * 128 + ee * 16, [[160, 4], [1, 16]])
                        src = _ap(PS[64 * ee:64 * ee + 64, bank, :], ee * 16, [[32, 4], [1, 16]])
                        S.op("act", ["PS%d" % bank, "C0"], ["Cmat"], lambda e, dst=dst, src=src, arr=arr: e.activation(
                            out=dst, in_=src, func=AF.Identity, scale=(1.0 if arr == 0 else -1.0)))
            DB8 = sbt(tmp, "DB8", [8, 128])
            S.dma("sp", [], ["DB8a"], DB8[0:4, :], d_skip_d[:, :])
            S.dma("sp", [], ["DB8b"], DB8[4:8, :], b_glu_d[:, :])
            S.op("pe", ["DB8a", "DB8b", "identf"], ["PS6"],
                 lambda e: e.transpose(PS[:, 6, 0:8], DB8[0:8, :], identf[0:8, 0:8]))
            S.op("dve", ["PS6"], ["dbT"], lambda e: e.tensor_copy(dbT[:], PS[:, 6, 0:8]))

            def c6(e):
                for c in range(4):
                    last = e.tensor_scalar(out=Dmat[:, c, :], in0=identf[:], scalar1=dbT[:, c:c + 1], scalar2=None,
                                           op0=ALU.mult)
                return last
            S.op("dve", ["dbT", "identf"], ["Dmat"], c6)
            S.dma("pool", [], ["wglu"], wglu[:], w_glu.rearrange("(k p) n -> p k n", p=128))
            barrier()
            tmp.close()
            P.update(mag=mag, cosT=cosT, sinT=sinT, Bmat=Bmat, Cmat=Cmat, Dmat=Dmat, dbT=dbT, wglu=wglu, sm=sm)
            return P

        def s5_pass(P, uT, carry, final, wk):
            mag, cosT, sinT = P["mag"], P["cosT"], P["sinT"]
            tt, zre, zim, Sre, Sim, yg, ygb, sg, cl, bimS = wk
            wre, wim = tt[:, 0], tt[:, 2]
            for n in range(NT):
                for hh in range(2):
                    j0 = 8 * hh
                    Bre = PS[:, 0:2, :].rearrange("p b (x q) -> p (b x) q", q=128)
                    Bim = PS[:, 2:4, :].rearrange("p b (x q) -> p (b x) q", q=128)

                    def bu(e):
                        for arr, dstp in ((0, Bre), (1, Bim)):
                            for jj in range(8):
                                j = j0 + jj
                                last = e.matmul(dstp[:, jj, :], P["Bmat"][:, arr, j, :], uT[:, j // 4, 128 * n:128 * (n + 1)],
                                                start=True, stop=True)
                        return last
                    S.op("pe", ["uT%d" % n, "Bmat"], ["PS0", "PS1", "PS2", "PS3"], bu)
                    cs_ = cosT[:, j0:j0 + 8, :]
                    sn_ = sinT[:, j0:j0 + 8, :]

                    def rot_in(e):
                        e.tensor_tensor(out=tt[:, 0], in0=Bre, in1=cs_, op=ALU.mult)
                        e.tensor_tensor(out=tt[:, 1], in0=Bim, in1=sn_, op=ALU.mult)
                        e.tensor_tensor(out=tt[:, 2], in0=Bim, in1=cs_, op=ALU.mult)
                        return e.tensor_tensor(out=tt[:, 3], in0=Bre, in1=sn_, op=ALU.mult)
                    S.op("dve", ["PS0", "PS1", "PS2", "PS3", "cosT", "sinT"], ["tt0", "tt1", "tt2", "tt3"], rot_in)

                    def rot_in2(e):
                        e.tensor_tensor(out=wre, in0=tt[:, 0], in1=tt[:, 1], op=ALU.add)
                        return e.tensor_tensor(out=wim, in0=tt[:, 2], in1=tt[:, 3], op=ALU.subtract)
                    S.op("dve", ["tt0", "tt1", "tt2", "tt3"], ["tt0", "tt2"], rot_in2)

                    def scans(e):
                        for jj in range(8):
                            j = j0 + jj
                            rho = mag[:, j:j + 1].to_broadcast([128, 128])
                            e.tensor_tensor_scan(out=zre[:, jj, :], data0=rho, data1=wre[:, jj, :],
                                                 initial=carry[:, 0, j:j + 1], op0=ALU.mult, op1=ALU.add)
                            last = e.tensor_tensor_scan(out=zim[:, jj, :], data0=rho, data1=wim[:, jj, :],
                                                        initial=carry[:, 1, j:j + 1], op0=ALU.mult, op1=ALU.add)
                        return last
                    S.op("dve", ["tt0", "tt2", "carry", "mag"], ["z"], scans)

                    cL = cosT[:, j0:j0 + 8, 127]
                    sL = sinT[:, j0:j0 + 8, 127]
                    zr = zre[:, :, 127]
                    zi = zim[:, :, 127]

                    def carry_a(e):
                        e.tensor_tensor(out=cl[:, 0, :], in0=cL, in1=zr, op=ALU.mult)
                        e.tensor_tensor(out=cl[:, 1, :], in0=sL, in1=zi, op=ALU.mult)
                        e.tensor_tensor(out=cl[:, 2, :], in0=cL, in1=zi, op=ALU.mult)
                        return e.tensor_tensor(out=cl[:, 3, :], in0=sL, in1=zr, op=ALU.mult)
                    S.op("dve", ["z", "cosT", "sinT"], ["cl"], carry_a)

                    def carry_b(e):
                        e.tensor_tensor(out=carry[:, 0, j0:j0 + 8], in0=cl[:, 0, :], in1=cl[:, 1, :], op=ALU.subtract)
                        return e.tensor_tensor(out=carry[:, 1, j0:j0 + 8], in0=cl[:, 2, :], in1=cl[:, 3, :], op=ALU.add)
                    S.op("dve", ["cl"], ["carry"], carry_b)
                    if final:
                        def rot_out(e):
                            e.tensor_tensor(out=tt[:, 0], in0=zre[:], in1=cs_, op=ALU.mult)
                            e.tensor_tensor(out=tt[:, 1], in0=zim[:], in1=sn_, op=ALU.mult)
                            e.tensor_tensor(out=tt[:, 2], in0=zim[:], in1=cs_, op=ALU.mult)
                            return e.tensor_tensor(out=tt[:, 3], in0=zre[:], in1=sn_, op=ALU.mult)
                        S.op("dve", ["z", "cosT", "sinT"], ["tt0", "tt1", "tt2", "tt3"], rot_out)

                        def rot_out_c(e):
                            e.tensor_tensor(out=Sim[:, j0:j0 + 8, :], in0=tt[:, 2], in1=tt[:, 3], op=ALU.add)
                            return e.tensor_tensor(out=Sre[:, j0:j0 + 8, :], in0=tt[:, 0], in1=tt[:, 1], op=ALU.subtract)
                        S.op("dve", ["tt0", "tt1", "tt2", "tt3"], ["S%d" % hh], rot_out_c)
                if not final:
                    continue
                Y = PS[:, 4, :].rearrange("p (c q) -> p c q", q=128)

                def ymm(e):
                    for c in range(4):
                        for x in range(4):
                            j = 4 * c + x
                            e.matmul(Y[:, c, :], P["Cmat"][:, 0, j, :], Sre[:, j, :], start=(x == 0), stop=False)
                            e.matmul(Y[:, c, :], P["Cmat"][:, 1, j, :], Sim[:, j, :], start=False, stop=False)
                        last = e.matmul(Y[:, c, :], P["Dmat"][:, c, :], uT[:, c, 128 * n:128 * (n + 1)], start=False, stop=True)
                    return last
                S.op("pe", ["S0", "S1", "Cmat", "Dmat", "uT%d" % n], ["PS4"], ymm)
                S.op("act", ["PS4"], ["yg"], lambda e: e.activation(out=yg[:], in_=Y, func=AF.Gelu_apprx_tanh))
                S.op("pool", ["yg"], ["ygb"], lambda e: e.tensor_copy(ygb[:], yg[:]))
                Z = PS[:, 5, :].rearrange("p (c q) -> p c q", q=128)

                def zmm(e):
                    for c2 in range(4):
                        for c in range(4):
                            last = e.matmul(Z[:, c2, :], P["wglu"][:, c, 128 * c2:128 * (c2 + 1)], ygb[:, c, :],
                                            start=(c == 0), stop=(c == 3))
                    return last
                S.op("pe", ["ygb", "wglu"], ["PS5"], zmm)

                def sig(e):
                    for c2 in range(4):
                        last = e.activation(out=sg[:, c2, :], in_=Z[:, c2, :], func=AF.Sigmoid,
                                            bias=P["dbT"][:, 4 + c2:5 + c2])
                    return last
                S.op("act", ["PS5", "dbT"], ["sg"], sig)
                S.op("pool", ["yg", "sg"], ["uT%d" % n], lambda e, n=n: e.tensor_tensor(
                    out=uT[:, :, 128 * n:128 * (n + 1)], in0=yg[:], in1=sg[:], op=ALU.mult))


        def s5_sample(P, ph):
            sm = P["sm"]
            lbr, lbi = sm[:, 5, :], sm[:, 6, :]
            stin = sbt(ph, "stin", [128, 128])
            st0 = sbt(ph, "st0", [128, 4, 2, 16])
            bus = sbt(ph, "bus", [128, 2, 16, 16])
            ssb = sbt(ph, "ssb", [128, 2, 16, 16], BF16)
            cur = sbt(ph, "s5cur", [128, 2, 4, 16])
            tq = sbt(ph, "s5tq", [128, 4, 4, 16])
            sto = sbt(ph, "s5sto", [128, 128])
            ygs = sbt(ph, "ygs", [128, 4, 16])
            ygsb = sbt(ph, "ygsb", [128, 4, 16], BF16)
            sgs = sbt(ph, "sgs", [128, 4, 16])
            S.dma("sp", [], ["stin"], stin[:], st5_d[:, :])
            S.op("pe", ["stin", "identf"], ["PS6"], lambda e: e.transpose(PS[:, 6, 0:128], stin[:], identf[:]))
            S.op("dve", ["PS6"], ["st0"], lambda e: e.tensor_copy(st0[:].rearrange("p s a j -> p (s a j)"), PS[:, 6, 0:128]))
            bup = PS[:, 0, :].rearrange("p (a j t) -> p a j t", a=2, j=16)

            def bu(e):
                for arr in range(2):
                    for j in range(16):
                        last = e.matmul(bup[:, arr, j, :], P["Bmat"][:, arr, j, :], uTs[:, j // 4, :], start=True, stop=True)
                return last
            S.op("pe", ["uTs", "Bmat"], ["PS0"], bu)
            S.op("dve", ["PS0"], ["bus"], lambda e: e.tensor_copy(bus[:], bup))
            S.op("dve", ["st0"], ["s5cur"], lambda e: e.tensor_copy(cur[:], st0[:].rearrange("p s a j -> p a s j")))
            lrb = lbr.unsqueeze(1).to_broadcast([128, 4, 16])
            lib = lbi.unsqueeze(1).to_broadcast([128, 4, 16])

            def rec(e):
                for t in range(4):
                    bre = bus[:, 0].rearrange("p j (s t) -> p s j t", t=4)[:, :, :, t]
                    bim = bus[:, 1].rearrange("p j (s t) -> p s j t", t=4)[:, :, :, t]
                    e.tensor_tensor(out=tq[:, 0], in0=cur[:, 0], in1=lrb, op=ALU.mult)
                    e.tensor_tensor(out=tq[:, 1], in0=cur[:, 1], in1=lib, op=ALU.mult)
                    e.tensor_tensor(out=tq[:, 2], in0=cur[:, 1], in1=lrb, op=ALU.mult)
                    e.tensor_tensor(out=tq[:, 3], in0=cur[:, 0], in1=lib, op=ALU.mult)
                    e.tensor_tensor(out=tq[:, 0], in0=tq[:, 0], in1=tq[:, 1], op=ALU.subtract)
                    e.tensor_tensor(out=tq[:, 2], in0=tq[:, 2], in1=tq[:, 3], op=ALU.add)
                    e.tensor_tensor(out=cur[:, 0], in0=tq[:, 0], in1=bre, op=ALU.add)
                    e.tensor_tensor(out=cur[:, 1], in0=tq[:, 2], in1=bim, op=ALU.add)
                    e.tensor_copy(ssb[:, 0].rearrange("p j (s t) -> p s j t", t=4)[:, :, :, t], cur[:, 0])
                    e.tensor_copy(ssb[:, 1].rearrange("p j (s t) -> p s j t", t=4)[:, :, :, t], cur[:, 1])
            S.op("dve", ["bus", "s5cur", "s5sm"], ["s5cur", "ssb"], rec, chain=True)
            S.op("dve", ["s5cur"], ["s5sto"], lambda e: e.tensor_copy(
                sto[:].rearrange("p (s a j) -> p s a j", s=4, a=2), cur[:].rearrange("p a s j -> p s a j")))
            S.op("pe", ["s5sto", "identf"], ["PS6"], lambda e: e.transpose(PS[:, 6, 0:128], sto[:], identf[:]))
            S.op("dve", ["PS6"], ["stin"], lambda e: e.tensor_copy(stin[:], PS[:, 6, 0:128]))
            S.dma("sp", ["stin"], ["ss5o"], ss5_o[:, :], stin[:])
            Y = PS[:, 4, 0:64].rearrange("p (c t) -> p c t", c=4)

            def ymm(e):
                for c in range(4):
                    for x in range(4):
                        j = 4 * c + x
                        e.matmul(Y[:, c, :], P["Cmat"][:, 0, j, :], ssb[:, 0, j, :], start=(x == 0), stop=False)
                        e.matmul(Y[:, c, :], P["Cmat"][:, 1, j, :], ssb[:, 1, j, :], start=False, stop=False)
                    last = e.matmul(Y[:, c, :], P["Dmat"][:, c, :], uTs[:, c, :], start=False, stop=True)
                return last
            S.op("pe", ["ssb", "Cmat", "Dmat", "uTs"], ["PS4"], ymm)
            S.op("act", ["PS4"], ["ygs"], lambda e: e.activation(out=ygs[:], in_=Y, func=AF.Gelu_apprx_tanh))
            S.op("pool", ["ygs"], ["ygsb"], lambda e: e.tensor_copy(ygsb[:], ygs[:]))
            Z = PS[:, 5, 0:64].rearrange("p (c t) -> p c t", c=4)

            def zmm(e):
                for c2 in range(4):
                    for c in range(4):
                        last = e.matmul(Z[:, c2, :], P["wglu"][:, c, 128 * c2:128 * (c2 + 1)], ygsb[:, c, :],
                                        start=(c == 0), stop=(c == 3))
                return last
            S.op("pe", ["ygsb", "wglu"], ["PS5"], zmm)

            def sig(e):
                for c2 in range(4):
                    last = e.activation(out=sgs[:, c2, :], in_=Z[:, c2, :], func=AF.Sigmoid, bias=P["dbT"][:, 4 + c2:5 + c2])
                return last
            S.op("act", ["PS5", "dbT"], ["sgs"], sig)
            S.op("pool", ["ygs", "sgs"], ["uTs"], lambda e: e.tensor_tensor(out=uTs[:], in0=ygs[:], in1=sgs[:], op=ALU.mult))

        with contextlib.ExitStack() as ph:
            uT = sbt(ph, "uT", [128, 4, T], BF16)
            with contextlib.ExitStack() as ph1:
                h2T = sbt(ph1, "hT1", [128, 8, 2 + T], BF16)
                mk(ph1, ["wch", "qf", "sq", "qn", "qb"], {"wch": 2})
                gb1, gb1k = load_gbc(g_mix[1:2, :])
                for n in range(NT):
                    rms_to_T(xres[:, n, :], "xres%d" % n, gb1, gb1k, h2T, "h1T%d" % n, 2 + 128 * n)
                wu, wuk = load_w(w_in_b[:, 0:512])
                wqc, wqck = load_w(w_in_b[:, 512:1024])
                ub = 0
                for c in range(4):
                    for blk in range(4):
                        bank = ub % 2
                        ub += 1

                        def umm(e, c=c, blk=blk, bank=bank):
                            for k in range(8):
                                last = e.matmul(PS[:, bank, :], wu[:, k, 128 * c:128 * (c + 1)],
                                                h2T[:, k, 2 + 512 * blk:2 + 512 * (blk + 1)], start=(k == 0), stop=(k == 7))
                            return last
                        S.op("pe", [wuk] + ["h1T%d" % (4 * blk + x) for x in range(4)], ["PS%d" % bank], umm)
                        S.op("act", ["PS%d" % bank], ["uT%d" % (4 * blk + x) for x in range(4)],
                             lambda e, c=c, blk=blk, bank=bank: e.activation(
                                 out=uT[:, c, 512 * blk:512 * (blk + 1)], in_=PS[:, bank, :], func=AF.Copy))
                for n in range(NT):
                    ps, psk = proj_tile(wqc, wqck, h2T, "h1T%d" % n, 2 + 128 * n)
                    qn, qnk = qk_post(ps, psk, gqc_bc, "gqc_bc", 1, 0, rope=False)
                    qb, qbk = to_bf16(qn, qnk)
                    S.dma("sp", [qbk], ["QC_%d" % n], QC[128 * n:128 * (n + 1), :], qb[:])
                rms_to_T(xsres[0:16, :], "xsres", gb1, gb1k, hTs, "hTs", 0, m=16)
                usp = PS[:, 0, 0:64].rearrange("p (c t) -> p c t", c=4)

                def umms(e):
                    for c in range(4):
                        for k in range(8):
                            last = e.matmul(usp[:, c, :], wu[:, k, 128 * c:128 * (c + 1)], hTs[:, k, :],
                                            start=(k == 0), stop=(k == 7))
                    return last
                S.op("pe", [wuk, "hTs"], ["PS0"], umms)
                S.op("act", ["PS0"], ["uTs"], lambda e: e.activation(out=uTs[:], in_=usp, func=AF.Copy))
                ps, psk = proj_tile(wqc, wqck, hTs, "hTs", 0, m=16)
                qn, qnk = qk_post(ps, psk, gqc_bc, "gqc_bc", 1, 0, rope=False, m=16)
                S.op("pool", [qnk], ["sqc1"], lambda e, qn=qn: e.tensor_copy(sqc[0:16, 1, :], qn[0:16].rearrange("p h d -> p (h d)")))
                barrier()
            with contextlib.ExitStack() as ph2:
                P5 = s5_setup(ph2)
                with contextlib.ExitStack() as phss:
                    s5_sample(P5, phss)
                    barrier()
                carry = sbt(ph2, "carry", [128, 2, 16])
                cin = sbt(ph2, "cin", [128, 32])
                wk = (sbt(ph2, "s5tt", [128, 4, 8, 128]),
                      sbt(ph2, "zre", [128, 8, 128]), sbt(ph2, "zim", [128, 8, 128]),
                      sbt(ph2, "Sre", [128, 16, 128], BF16), sbt(ph2, "Sim", [128, 16, 128], BF16),
                      sbt(ph2, "yg", [128, 4, 128]), sbt(ph2, "ygb", [128, 4, 128], BF16),
                      sbt(ph2, "sg", [128, 4, 128]), sbt(ph2, "cl", [128, 4, 8]),
                      sbt(ph2, "bimS", [128, 8, 128]))
                S.op("dve", [], ["carry"], lambda e: e.memset(carry[:], 0.0))
                if os.environ.get("DBG_S5PASS1", "1") == "1":
                    s5_pass(P5, uT, carry, False, wk)
                    S.dma("pool", ["carry"], ["ccs2"], cc_src[2][:, 0:32], carry[:].rearrange("p a j -> p (a j)"))
                    S.allgather_pairs(["ccs2"], ["ccd2"], cc_src[2][:, :], cc_dst[2][:, :])
                    S.dma("pool", ["ccd2"], ["cin"], cin[:], cc_dst[2][0:128, 0:32])
                    S.op("dve", ["cin", "flag"], ["carry"], lambda e: e.tensor_scalar(
                        out=carry[:].rearrange("p a j -> p (a j)"), in0=cin[:], scalar1=flag[:, 0:1], scalar2=None,
                        op0=ALU.mult))
                s5_pass(P5, uT, carry, True, wk)
                S.op("pe", ["carry", "identf"], ["PS6"], lambda e: e.transpose(
                    PS[0:32, 6, 0:128], carry[:].rearrange("p a j -> p (a j)"), identf[:]))
                so = sbt(ph2, "s5out", [32, 128])
                S.op("dve", ["PS6"], ["s5out"], lambda e: e.tensor_copy(so[:], PS[0:32, 6, 0:128]))
                S.dma("sp", ["s5out"], ["s5o"], s5_o.rearrange("a j q -> (a j) q"), so[:])
                barrier()
            if stage <= 4:
                return end()
            S.dma("sp", ["uT%d" % n for n in range(NT)], ["MIXall"], MIX[:, :], uT[:].rearrange("p c t -> p (c t)"))
            barrier()
            ph.close()
            mxR = Rot(nc, ph, "mx", [128, 4, 128], BF16, 2)
            h2T = sbt(ph, "h2T1", [128, 8, 2 + T], BF16)
            with contextlib.ExitStack() as ph3:
                mk(ph3, ["wch", "qf", "sq", "qn", "qb", "vf"], {"wch": 2})
                memory_kv(1, ph3)
                barrier()
            sample_attention(1, False)
            with contextlib.ExitStack() as phb:
                mk(phb, ["qu", "qT", "pT", "ou", "mg", "mgT", "rd"])
                load_wout(1, phb)
                gf1, gf1k = load_gbc(g_ffn[1:2, :])
                for n in range(NT):
                    mg, mgk = W["mg"].next()
                    cross_attention(n, mg, mgk)
                    mgT, mgTk = W["mgT"].next()
                    transposes([(mg[:, 512 + k * 128:512 + (k + 1) * 128], mgk + "_512") for k in range(4)],
                               mgT[:, 0:4, :], mgTk)
                    mx, mxk = mxR.next()
                    S.dma("sp", ["MIXall"], [mxk], mx[:], MIX.rearrange("p (c t) -> p c t", c=4)[:, :, 128 * n:128 * (n + 1)])
                    lhs = [(mx[:, k, :], mxk) for k in range(4)] + [(mgT[:, k, :], mgTk) for k in range(4)]
                    out_proj_residual(n, lhs, xres[:, n, :], "xres%d" % n)
                    rms_to_T(xres[:, n, :], "xres%d" % n, gf1, gf1k, h2T, "h2T%d" % n, 2 + 128 * n)
                sample_tail(1, xsres[0:16, :], "xsres", gf1, gf1k)
                barrier()
            with contextlib.ExitStack() as ph4:
                ffn(1, ph4, h2T, final_out=True)
                barrier()

        S.finish("sp")
    return nc


def _consts():
    ident = np.eye(128, dtype=np.float32)
    k = np.arange(128)[:, None]
    q = np.arange(128)[None, :]
    maskb = np.zeros((128, 2, 128), np.float32)
    maskb[:, 0, :] = np.where(k >= q, 0.0, NEGB)
    maskb[:, 1, :] = np.where(k <= q, 0.0, NEGB)
    return ident, maskb


def _rope_table(pos):
    half = 16
    inv = np.exp(-math.log(500000.0) * np.arange(half, dtype=np.float32) / half).astype(np.float32)
    ang = pos.astype(np.float32)[:, None] * inv[None, :]
    return np.concatenate([np.cos(ang), np.sin(ang)], axis=1).astype(np.float32)


def prep(inputs, stage=99):
    ident, maskb = _consts()
    f = lambda k: np.ascontiguousarray(np.asarray(inputs[k], np.float32))
    xp = f("x_prompt")
    shared = {
        "ident": ident, "maskb": maskb,
        "g_mix": f("g_mix"), "g_ffn": f("g_ffn"), "g_mem": f("g_mem"),
        "w_in_a": f("w_in_a")[0], "w_in_b": f("w_in_b")[0],
        "g_q_dil": f("g_q_dil")[0], "g_k_dil": f("g_k_dil")[0],
        "g_q_cross": f("g_q_cross"), "g_k_cross": f("g_k_cross"),
        "w_mem_kv": f("w_mem_kv"), "w_out": f("w_out"), "w_up": f("w_up"),
        "conv_w": f("conv_w"), "conv_b": f("conv_b"), "w_down": f("w_down"),
        "s5_lam_re": f("s5_lam_re").reshape(16, 128), "s5_lam_im": f("s5_lam_im").reshape(16, 128),
        "s5_log_dt": f("s5_log_dt").reshape(16, 2),
        "s5_b_re": f("s5_b_re").reshape(2048, 16), "s5_b_im": f("s5_b_im").reshape(2048, 16),
        "s5_c_re": f("s5_c_re").reshape(512, 64), "s5_c_im": f("s5_c_im").reshape(512, 64),
        "s5_d": f("s5_d").reshape(4, 128), "w_glu": f("w_glu")[0], "b_glu": f("b_glu").reshape(4, 128),
    }
    mem = f("mem_prompt")
    xs = f("x_sample")
    cw = [f("cache_win0_kv")[0], f("cache_win1_kv")[0], f("cache_win2_kv")[0]]
    cmem = f("cache_mem_kv")
    st5 = f("state_s5")[0]
    cst = f("state_ffn_conv")
    g0bias = np.zeros((128, 4), np.float32)
    for t in range(4):
        g0bias[:t, t] = NEGB * SCALE
    biasnew = np.full((16, 2, 16), NEGB * SCALE, np.float32)
    for kr in range(16):
        for qr in range(16):
            if kr // 4 == qr // 4:
                if kr % 4 <= qr % 4:
                    biasnew[kr, 0, qr] = 0.0
                if kr == qr:
                    biasnew[kr, 1, qr] = 0.0
    in_maps = []
    for c in range(8):
        b, hf = c // 2, c % 2
        xkv = np.zeros((TH, D), np.float32)
        if hf == 0:
            xkv[T:] = xp[b, 0:T]
        else:
            xkv[:] = xp[b]
        pos = np.concatenate([np.arange(TH) + (hf * T - T), PAST + (np.arange(128) % 4)])
        m = dict(shared)
        sl = slice(4 * c, 4 * c + 4)
        m.update({
            "xkv": xkv,
            "flag": np.full((128, 1), float(hf), np.float32),
            "ropecs": _rope_table(pos),
            "mem": mem[b],
            "xs": np.ascontiguousarray(xs[sl].reshape(16, D)),
            "cwin0": np.ascontiguousarray(cw[0][sl]), "cwin1": np.ascontiguousarray(cw[1][sl]),
            "cwin2": np.ascontiguousarray(cw[2][sl]),
            "cmem": np.ascontiguousarray(cmem[:, sl]),
            "st5": np.ascontiguousarray(st5[sl].reshape(128, 128)),
            "cst": np.ascontiguousarray(cst[:, sl].reshape(2, 8, 2 * DFF)),
            "g0bias": g0bias, "biasnew": biasnew,
        })
        in_maps.append(m)
    return in_maps


_NC_CACHE = {}


def kernel(**inputs):
    if "nc" not in _NC_CACHE:
        _NC_CACHE["nc"] = build(99)
    nc = _NC_CACHE["nc"]
    in_maps = prep(inputs)
    res = run_bass_kernel_spmd(nc, in_maps, core_ids=list(range(8)))
    R = res.results
    f32 = np.float32
    y_prompt = np.stack([np.concatenate([R[2 * b]["y_p"], R[2 * b + 1]["y_p"]], 0) for b in range(4)]).astype(f32)
    y_sample = np.concatenate([R[c]["y_s"].reshape(4, 4, D) for c in range(8)], 0).astype(f32)
    p_win = [np.stack([R[2 * b + 1]["win%d" % g] for b in range(4)])[None].astype(f32) for g in range(3)]
    p_mem = np.stack([R[2 * b]["memkv"] for b in range(4)], 1).astype(f32)
    p_s5 = np.stack([R[2 * b + 1]["s5o"].reshape(2, 32, 64) for b in range(4)])[None].astype(f32)
    p_conv = np.stack([R[2 * b + 1]["convo"].reshape(2, 2, 2 * DFF) for b in range(4)], 1).astype(f32)
    s_win = [np.concatenate([R[c]["swin%d" % g] for c in range(8)], 0)[None].astype(f32) for g in range(3)]
    s_s5 = np.concatenate([R[c]["ss5"].reshape(4, 2, 32, 64) for c in range(8)], 0)[None].astype(f32)
    s_conv = np.concatenate([R[c]["sconv"].reshape(2, 4, 2, 2 * DFF) for c in range(8)], 1).astype(f32)
    return (y_prompt, y_sample, p_win[0], p_win[1], p_win[2], p_mem, p_s5, p_conv,
            s_win[0], s_win[1], s_win[2], s_s5, s_conv)
```
